# Optimizing a Trainium2 kernel written in Bass

```python
import math
import jax
import jax.numpy as jnp
from jax import lax
import numpy as np

D_MODEL = 1024
BATCH = 8
SEQ = 4096
DEPTH = 2
DEC_BATCH = 8
DEC_SEQ = 16
PAST_LEN = 1024

CHUNK = 64
N_A_LAYERS = DEPTH // 2
N_B_LAYERS = DEPTH - N_A_LAYERS
D_RNN = ((4 * D_MODEL // 3 + 127) // 128) * 128
N_RNN_BLOCKS = 16
RNN_BLOCK = D_RNN // N_RNN_BLOCKS
CONV_W = 4
LRU_C = 8.0
HEAD_DIM = 64
N_HEADS = D_MODEL // (2 * HEAD_DIM)
V_DIM = 2 * HEAD_DIM
ATTN_W = N_HEADS * V_DIM
Q_BLOCK = 128
ROPE_THETA = 10000.0
EPS = 1e-6

kernel_name = 'hawk_yoco_diff_attn_stream'


def _rmsnorm(x, g):
    xf = x.astype(jnp.float32)
    y = xf * lax.rsqrt(jnp.mean(xf * xf, axis=-1, keepdims=True) + EPS)
    return (y * g.astype(jnp.float32)).astype(x.dtype)


def _rope(x, pos):
    half = HEAD_DIM // 2
    inv = ROPE_THETA ** (-jnp.arange(half, dtype=jnp.float32) / half)
    ang = pos.astype(jnp.float32)[:, None] * inv[None, :]
    shape = (1, pos.shape[0]) + (1,) * (x.ndim - 3) + (half,)
    cos = jnp.cos(ang).reshape(shape)
    sin = jnp.sin(ang).reshape(shape)
    xf = x.astype(jnp.float32)
    x1, x2 = xf[..., :half], xf[..., half:]
    return jnp.concatenate([x1 * cos - x2 * sin, x2 * cos + x1 * sin], axis=-1).astype(x.dtype)


def _causal_conv(u, buf, w, b):
    t = u.shape[1]
    up = jnp.concatenate([buf.astype(u.dtype), u], axis=1)
    out = b + up[:, 0:t] * w[0]
    for j in range(1, CONV_W):
        out = out + up[:, j:j + t] * w[j]
    return out, up[:, t:]


def _block_diag(u, w, b):
    bsz, t, _ = u.shape
    ub = u.reshape(bsz, t, N_RNN_BLOCKS, RNN_BLOCK)
    return jnp.einsum('btni,nij->btnj', ub, w).reshape(bsz, t, D_RNN) + b


def _lin_comb(left, right):
    a1, b1 = left
    a2, b2 = right
    return a1 * a2, a2 * b1 + b2


def _rg_lru(u, h0, wr, br, wi, bi, lam):
    uf = u.astype(jnp.float32)
    r = jax.nn.sigmoid(_block_diag(u, wr, br).astype(jnp.float32))
    i = jax.nn.sigmoid(_block_diag(u, wi, bi).astype(jnp.float32))
    log_a = -LRU_C * r * jax.nn.softplus(-lam.astype(jnp.float32))
    a = jnp.exp(log_a)
    mult = jnp.sqrt(-jnp.expm1(2.0 * log_a))
    if h0 is None:
        mult = mult.at[:, 0].set(1.0)
        b = mult * (i * uf)
    else:
        b = mult * (i * uf)
        b = b.at[:, 0].add(a[:, 0] * h0.astype(jnp.float32))
    _, h = lax.associative_scan(_lin_comb, (a, b), axis=1)
    return h.astype(u.dtype), h[:, -1].astype(u.dtype)


def _a_layer(x, conv_buf, h0, norm_g, w_in, conv_w, conv_b, wr, br, wi, bi, lam, w_out):
    h = _rmsnorm(x, norm_g) @ w_in
    u, gate = h[..., :D_RNN], h[..., D_RNN:]
    u, new_buf = _causal_conv(u, conv_buf, conv_w, conv_b)
    y, h_last = _rg_lru(u, h0, wr, br, wi, bi, lam)
    return x + (y * jax.nn.silu(gate)) @ w_out, new_buf, h_last


def _shared_kv(x, pos, kv_norm, kv_w, k_norm):
    bsz, t, _ = x.shape
    kv = _rmsnorm(x, kv_norm) @ kv_w
    k = kv[..., :ATTN_W].reshape(bsz, t, N_HEADS, 2, HEAD_DIM)
    v = kv[..., ATTN_W:].reshape(bsz, t, N_HEADS, V_DIM)
    k = _rope(_rmsnorm(k, k_norm), pos).reshape(bsz, t, N_HEADS, V_DIM)
    return k, v


def _diff_attn(q, k, v, lam, mask):
    bsz, nk = k.shape[:2]
    k = k.reshape(bsz, nk, N_HEADS, 2, HEAD_DIM)
    s = jnp.einsum('bqhcd,bkhcd->bchqk', q, k).astype(jnp.float32) * (HEAD_DIM ** -0.5)
    if mask is not None:
        s = jnp.where(mask, s, -jnp.inf)
    p = jax.nn.softmax(s, axis=-1)
    pd = p[:, 0] - lam * p[:, 1]
    return jnp.einsum('bhqk,bkhe->bqhe', pd.astype(v.dtype), v)


def _prompt_attn(q, k, v, lam):
    bsz, t = q.shape[:2]
    nb = t // Q_BLOCK
    qb = q.reshape(bsz, nb, Q_BLOCK, N_HEADS, 2, HEAD_DIM).swapaxes(0, 1)
    kchunk = jnp.arange(t) // CHUNK

    def one(args):
        qblk, start = args
        qchunk = (start + jnp.arange(Q_BLOCK)) // CHUNK
        mask = kchunk[None, :] <= qchunk[:, None]
        return _diff_attn(qblk, k, v, lam, mask)

    o = lax.map(one, (qb, jnp.arange(nb) * Q_BLOCK))
    return o.swapaxes(0, 1).reshape(bsz, t, N_HEADS, V_DIM)


def _b_layer(x, pos, k, v, norm_g, w_in, q_norm, lq1, lk1, lq2, lk2, head_g, w_out, lam_init, prompt):
    bsz, t, _ = x.shape
    h = _rmsnorm(x, norm_g) @ w_in
    q, gate = h[..., :ATTN_W], h[..., ATTN_W:]
    q = _rope(_rmsnorm(q.reshape(bsz, t, N_HEADS, 2, HEAD_DIM), q_norm), pos)
    lam = (jnp.exp(jnp.sum(lq1.astype(jnp.float32) * lk1.astype(jnp.float32)))
           - jnp.exp(jnp.sum(lq2.astype(jnp.float32) * lk2.astype(jnp.float32))) + lam_init)
    if prompt:
        o = _prompt_attn(q, k, v, lam)
    else:
        o = _diff_attn(q, k, v, lam, None)
    o = _rmsnorm(o, head_g) * (1.0 - lam_init)
    o = o.reshape(bsz, t, ATTN_W) * jax.nn.silu(gate)
    return x + o @ w_out


def _run_group(x, pos, conv_bufs, h0s, past_k, past_v, p):
    prompt = past_k is None
    bsz = x.shape[0]
    new_bufs, new_hs = [], []
    k_new, v_new, k_all, v_all = None, None, None, None
    for i in range(DEPTH):
        if i < N_A_LAYERS:
            if prompt:
                buf = jnp.zeros((bsz, CONV_W - 1, D_RNN), x.dtype)
                h0 = None
            else:
                buf = conv_bufs[i]
                h0 = h0s[i]
            x, nb, nh = _a_layer(x, buf, h0, p['a_norm'][i], p['a_w_in'][i], p['a_conv_w'][i],
                                 p['a_conv_b'][i], p['a_gate_r_w'][i], p['a_gate_r_b'][i],
                                 p['a_gate_i_w'][i], p['a_gate_i_b'][i], p['a_lambda'][i],
                                 p['a_w_out'][i])
            new_bufs.append(nb)
            new_hs.append(nh)
        else:
            if i == N_A_LAYERS:
                k_new, v_new = _shared_kv(x, pos, p['kv_norm'], p['kv_w'], p['k_norm'])
                if prompt:
                    k_all, v_all = k_new, v_new
                else:
                    k_all = jnp.concatenate([past_k.astype(k_new.dtype), k_new], axis=1)
                    v_all = jnp.concatenate([past_v.astype(v_new.dtype), v_new], axis=1)
            j = i - N_A_LAYERS
            lam_init = 0.8 - 0.6 * math.exp(-0.3 * i)
            x = _b_layer(x, pos, k_all, v_all, p['b_norm'][j], p['b_w_in'][j], p['b_q_norm'][j],
                         p['b_lambda_q1'][j], p['b_lambda_k1'][j], p['b_lambda_q2'][j],
                         p['b_lambda_k2'][j], p['b_head_norm'][j], p['b_w_out'][j], lam_init, prompt)
    return x, jnp.stack(new_bufs), jnp.stack(new_hs), k_new, v_new


def _normal(k, shape, scale):
    return jax.random.normal(k, shape, jnp.float32) * scale


def setup_inputs(seed: int = 0) -> dict:
    key = jax.random.key(seed)
    ks = jax.random.split(key, 32)
    na, nbl = N_A_LAYERS, N_B_LAYERS
    u = jax.random.uniform(ks[14], (na, D_RNN), jnp.float32, minval=0.9, maxval=0.999)
    a0 = u ** (1.0 / LRU_C)
    a_lambda = jnp.log(a0) - jnp.log1p(-a0)
    return {
        'x_prompt': _normal(ks[0], (BATCH, SEQ, D_MODEL), 1.0),
        'x_sample': _normal(ks[1], (DEC_BATCH, DEC_SEQ, D_MODEL), 1.0),
        'state_conv': _normal(ks[2], (na, DEC_BATCH, CONV_W - 1, D_RNN), 1.0),
        'state_h': _normal(ks[3], (na, DEC_BATCH, D_RNN), 0.5),
        'cache_k': _normal(ks[4], (DEC_BATCH, PAST_LEN, N_HEADS, V_DIM), 1.0),
        'cache_v': _normal(ks[5], (DEC_BATCH, PAST_LEN, N_HEADS, V_DIM), 1.0),
        'a_norm': 1.0 + _normal(ks[6], (na, D_MODEL), 0.05),
        'a_w_in': _normal(ks[7], (na, D_MODEL, 2 * D_RNN), D_MODEL ** -0.5),
        'a_conv_w': _normal(ks[8], (na, CONV_W, D_RNN), 0.5),
        'a_conv_b': _normal(ks[9], (na, D_RNN), 0.02),
        'a_gate_r_w': _normal(ks[10], (na, N_RNN_BLOCKS, RNN_BLOCK, RNN_BLOCK), RNN_BLOCK ** -0.5),
        'a_gate_r_b': _normal(ks[11], (na, D_RNN), 0.02),
        'a_gate_i_w': _normal(ks[12], (na, N_RNN_BLOCKS, RNN_BLOCK, RNN_BLOCK), RNN_BLOCK ** -0.5),
        'a_gate_i_b': _normal(ks[13], (na, D_RNN), 0.02),
        'a_lambda': a_lambda,
        'a_w_out': _normal(ks[15], (na, D_RNN, D_MODEL), D_RNN ** -0.5),
        'kv_norm': 1.0 + _normal(ks[16], (D_MODEL,), 0.05),
        'kv_w': _normal(ks[17], (D_MODEL, 2 * ATTN_W), D_MODEL ** -0.5),
        'k_norm': 1.0 + _normal(ks[18], (HEAD_DIM,), 0.05),
        'b_norm': 1.0 + _normal(ks[19], (nbl, D_MODEL), 0.05),
        'b_w_in': _normal(ks[20], (nbl, D_MODEL, 2 * ATTN_W), D_MODEL ** -0.5),
        'b_q_norm': 1.0 + _normal(ks[21], (nbl, HEAD_DIM), 0.05),
        'b_lambda_q1': _normal(ks[22], (nbl, HEAD_DIM), 0.1),
        'b_lambda_k1': _normal(ks[23], (nbl, HEAD_DIM), 0.1),
        'b_lambda_q2': _normal(ks[24], (nbl, HEAD_DIM), 0.1),
        'b_lambda_k2': _normal(ks[25], (nbl, HEAD_DIM), 0.1),
        'b_head_norm': 1.0 + _normal(ks[26], (nbl, V_DIM), 0.05),
        'b_w_out': _normal(ks[27], (nbl, ATTN_W, D_MODEL), ATTN_W ** -0.5),
    }


def reference(x_prompt, x_sample, state_conv, state_h, cache_k, cache_v,
              a_norm, a_w_in, a_conv_w, a_conv_b, a_gate_r_w, a_gate_r_b, a_gate_i_w, a_gate_i_b,
              a_lambda, a_w_out, kv_norm, kv_w, k_norm, b_norm, b_w_in, b_q_norm,
              b_lambda_q1, b_lambda_k1, b_lambda_q2, b_lambda_k2, b_head_norm, b_w_out):
    p = dict(a_norm=a_norm, a_w_in=a_w_in, a_conv_w=a_conv_w, a_conv_b=a_conv_b,
             a_gate_r_w=a_gate_r_w, a_gate_r_b=a_gate_r_b, a_gate_i_w=a_gate_i_w,
             a_gate_i_b=a_gate_i_b, a_lambda=a_lambda, a_w_out=a_w_out, kv_norm=kv_norm,
             kv_w=kv_w, k_norm=k_norm, b_norm=b_norm, b_w_in=b_w_in, b_q_norm=b_q_norm,
             b_lambda_q1=b_lambda_q1, b_lambda_k1=b_lambda_k1, b_lambda_q2=b_lambda_q2,
             b_lambda_k2=b_lambda_k2, b_head_norm=b_head_norm, b_w_out=b_w_out)
    pos_p = jnp.arange(x_prompt.shape[1], dtype=jnp.int32)
    pos_s = cache_k.shape[1] + jnp.arange(x_sample.shape[1], dtype=jnp.int32)
    y_p, conv_p, h_p, k_p, v_p = _run_group(x_prompt, pos_p, None, None, None, None, p)
    y_s, conv_s, h_s, k_s, v_s = _run_group(x_sample, pos_s, state_conv, state_h, cache_k, cache_v, p)
    return (y_p, y_s, conv_p, h_p, k_p, v_p, conv_s, h_s, k_s, v_s)
```

```python
import math
from contextlib import ExitStack

import numpy as np
import concourse.bass as bass
import concourse.mybir as mybir
from concourse.bass_utils import run_bass_kernel_spmd

F32 = mybir.dt.float32
BF16 = mybir.dt.bfloat16
AF = mybir.ActivationFunctionType
ALU = mybir.AluOpType
AX = mybir.AxisListType

D = 1024
SEQ = 4096
NS = 16
PAST = 1024
DR = 1408
NCH = 11
NH = 8
EPS = 1e-6
LAM_INIT = 0.8 - 0.6 * math.exp(-0.3 * 1)
TN = 512
LEAD = 3
NT = SEQ // TN
RING = 8
K32 = 22
K16 = 4

PP_AN, PP_KVN, PP_BN = 0, 8, 16
PP_CW, PP_CB, PP_BR, PP_BI, PP_LAM = 24, 68, 79, 90, 101
PP_GK, PP_GKS, PP_GQ, PP_GQS, PP_HG = 112, 113, 114, 115, 116
PP_LQ1, PP_LK1, PP_LQ2, PP_LK2 = 117, 181, 245, 309
PP_N = 373


class Buf:
    __slots__ = ("name", "w", "r", "excl")

    def __init__(self, name, excl=False):
        self.name = name
        self.w = {}
        self.r = {}
        self.excl = excl


class Sched:
    CE = ("pe", "act", "dve", "pool")

    def __init__(self, nc, es):
        self.nc = nc
        self.es = es
        self.streams = {k: [] for k in ("pe", "act", "dve", "pool", "sp")}
        self.sems = {}
        self.cnt = {}
        self.waited = {k: {} for k in self.streams}
        for e in self.CE:
            self._sem(e)

    def _sem(self, name):
        if name not in self.sems:
            self.sems[name] = self.es.enter_context(self.nc.semaphore("s_" + name))
            self.cnt[name] = 0
        return self.sems[name]

    def emit(self, eng, fn, reads=(), writes=(), sig=True, dsem=None):
        need = {}

        def add(s, v, kind):
            if dsem is None and s == eng:
                if eng == "pe" or kind != "raw":
                    return
            if need.get(s, 0) < v:
                need[s] = v

        for b in reads:
            for s, v in b.w.items():
                add(s, v, "raw")
            if b.excl:
                for s, v in b.r.items():
                    add(s, v, "rar")
        for b in writes:
            for s, v in b.w.items():
                add(s, v, "waw")
            for s, v in b.r.items():
                add(s, v, "war")
        waits = []
        wd = self.waited[eng]
        for s, v in need.items():
            if wd.get(s, 0) < v:
                assert self.cnt[s] >= v, f"wait on future event {s}>={v} (cnt {self.cnt[s]}) from {eng}"
                wd[s] = v
                waits.append((s, v))
        if dsem is not None:
            self._sem(dsem)
            self.cnt[dsem] += 16
            ev = (dsem, self.cnt[dsem])
            sg = (dsem, 16)
        elif sig:
            self.cnt[eng] += 1
            ev = (eng, self.cnt[eng])
            sg = (eng, 1)
        else:
            ev = (eng, self.cnt[eng] + 1)
            sg = None
        self.streams[eng].append((waits, fn, sg))
        for b in writes:
            b.w = {ev[0]: ev[1]}
            b.r = {}
        for b in reads:
            if b.r.get(ev[0], 0) < ev[1]:
                b.r[ev[0]] = ev[1]
        return ev

    def final_waits(self, eng, names):
        waits = [(s, self.cnt[s]) for s in names if self.cnt.get(s, 0) > 0]
        self.streams[eng].append((waits, None, None))

    def replay(self, block):
        def run(stream):
            def body(e):
                for waits, fn, sg in stream:
                    for s, v in waits:
                        e.wait_ge(self.sems[s], v)
                    if fn is None:
                        continue
                    ins = fn(e)
                    if sg is not None:
                        ins.then_inc(self.sems[sg[0]], sg[1])
            return body

        block.tensor(run(self.streams["pe"]))
        block.scalar(run(self.streams["act"]))
        block.vector(run(self.streams["dve"]))
        block.gpsimd(run(self.streams["pool"]))
        block.sync(run(self.streams["sp"]))


class Pool32:
    def __init__(self, aps, name):
        self.items = [(ap, Buf(f"{name}{i}")) for i, ap in enumerate(aps)]
        self.free = list(range(len(aps)))

    def alloc(self):
        assert self.free, "temp pool exhausted"
        i = self.free.pop(0)
        return i

    def ap(self, i):
        return self.items[i][0]

    def buf(self, i):
        return self.items[i][1]

    def release(self, i):
        self.free.append(i)


def build_program():
    nc = bass.Bass("TRN2", target_bir_lowering=False)

    def din(name, shape, dt=F32):
        return nc.dram_tensor(name, list(shape), dt, kind="ExternalInput").ap()

    def dout(name, shape, dt=F32):
        return nc.dram_tensor(name, list(shape), dt, kind="ExternalOutput").ap()

    def dint(name, shape, dt=BF16):
        return nc.dram_tensor(name, list(shape), dt, kind="Internal").ap()

    xT_p = din("xT_p", [D, SEQ])
    xT_s = din("xT_s", [D, NS])
    conv_s_in = din("conv_s_in", [128, NCH * 3])
    h_s_in = din("h_s_in", [128, NCH])
    ckT = din("ckT", [NH, 128, PAST])
    cv = din("cv", [PAST, D])
    wnames = [("win_u", NCH, 1024), ("win_g", NCH, 1024), ("gts", NCH, 768), ("wout", 8, 1408),
              ("wk", 8, 1024), ("wv", 8, 1024), ("wq", 8, 1024), ("wbg", 8, 1024), ("wbo", 8, 1024)]
    wf32 = {n: din(n, [ns, 128, w]) for n, ns, w in wnames}
    wb16 = {n: dint(n + "_b", [ns, 128, w]) for n, ns, w in wnames}
    wwidth = {n: w for n, ns, w in wnames}
    pp_in = din("pp", [128, PP_N])
    cmat_in = din("cmat", [128, 384])
    rope_in = din("rope", [128, 2, SEQ + NS])

    yT_p = dout("yT_p", [D, SEQ])
    yT_s = dout("yT_s", [D, NS])
    conv_p_o = dout("conv_p_o", [128, NCH * 3])
    h_p_o = dout("h_p_o", [128, NCH])
    kT_p = dout("kT_p", [NH, 128, SEQ])
    v_p = dout("v_p", [SEQ, D])
    conv_s_o = dout("conv_s_o", [128, NCH * 3])
    h_s_o = dout("h_s_o", [128, NCH])
    kT_s = dout("kT_s", [NH, 128, NS])
    v_s = dout("v_s", [NS, D])

    kscr = dint("kscr", [NH, 128, SEQ])
    vscr = dint("vscr", [NH, 128, SEQ // 128, 128])

    with ExitStack() as es:
        S = Sched(nc, es)

        def sb(name, shape, dt):
            return es.enter_context(nc.sbuf_tensor(name, list(shape), dt))

        xin = sb("xin", [128, 8, TN], F32)
        x1 = sb("x1", [128, 8, TN], F32)
        xn = sb("xn", [128, 8, TN], BF16)
        ring = sb("ring", [128, RING, 1408], BF16)
        kTh = sb("kTh", [128, 2, SEQ], BF16)
        vh = sb("vh", [128, 2, SEQ // 128, 128], BF16)
        pp = sb("pp_sb", [128, PP_N], F32)
        cmf = sb("cmf", [128, 384], F32)
        cmb = sb("cmb", [128, 384], BF16)
        xn2 = sb("xn2", [128, 8, TN], BF16)
        pacc = sb("pacc", [128, 2, 2, TN], F32)
        rope = sb("rope_sb", [128, 2, 2, TN], F32)
        t32 = sb("t32", [128, K32, TN + 4], F32)
        b16 = sb("b16", [128, 22, TN], BF16)
        on = sb("on", [128, 8, TN], BF16)
        pt = sb("pt", [128, 3, 2, TN], BF16)
        t16 = sb("t16", [128, K16, TN], BF16)
        small = sb("small", [128, 64], F32)
        lprod = sb("lprod", [128, 128], F32)
        ucarry_p = sb("ucarry_p", [128, NCH, 3], F32)
        hcarry_p = sb("hcarry_p", [128, NCH], F32)
        ucarry_s = sb("ucarry_s", [128, NCH, 3], F32)
        hcarry_s = sb("hcarry_s", [128, NCH], F32)
        psum = [es.enter_context(nc.psum_tensor(f"ps{i}", [128, 2, 512], F32)) for i in range(4)]

        B = {n: Buf(n) for n in ("xin", "x1", "xn", "xn2", "pp", "cmf", "cmb", "consts", "small", "lprod", "on",
                                 "ucarry_p", "hcarry_p", "ucarry_s", "hcarry_s", "kscr", "vscr",
                                 "conv_p_o", "h_p_o", "conv_s_o", "h_s_o")}
        Bring = [Buf(f"ring{i}") for i in range(RING)]
        BkTh = [Buf("kTh0"), Buf("kTh1")]
        Bvh = [Buf("vh0"), Buf("vh1")]
        Brope = [Buf("rope0"), Buf("rope1")]
        Bb16 = [Buf(f"b16_{i}") for i in range(22)]
        Bpt = [Buf(f"pt{i}") for i in range(3)]
        Bps = [Buf(f"psb{i}", excl=True) for i in range(8)]
        Bon = [Buf(f"on{i}") for i in range(8)]
        Bpacc = [[Buf(f"pacc{i}{j}") for j in range(2)] for i in range(2)]
        Bxn = [Buf(f"xn{i}") for i in range(8)]
        Bxn2 = [Buf(f"xn2_{i}") for i in range(8)]
        Bw = {n: Buf("w_" + n) for n, _, _ in wnames}
        T32 = Pool32([t32[:, i, :] for i in range(K32)], "t32_")
        T16 = Pool32([t16[:, i, :] for i in range(K16)], "t16_")
        store_sems = set()

        def psb(i):
            return psum[i // 2][:, i % 2, :]

        bank_rr = [0]
        bank_held = set()

        def bank():
            while True:
                i = bank_rr[0]
                bank_rr[0] = (i + 1) % 8
                if i not in bank_held:
                    return i

        ones_b = cmb[:, 0:128]
        blk_b = cmb[:, 128:256]
        perm_b = cmb[:, 256:384]
        ones_f = cmf[:, 0:128]

        def ppc(i):
            return pp[:, i:i + 1]

        def smc(i):
            return small[:, i:i + 1]
        SM_HBR, SM_HBI, SM_HC, SM_NLAM, SM_HGS = 0, 11, 22, 33, 34
        epsc = small[:, 62:63]
        onec = small[:, 63:64]

        S.emit("sp", lambda e: e.dma_start(out=pp[:], in_=pp_in), writes=[B["pp"]], dsem="ld_pp")
        S.emit("sp", lambda e: e.dma_start(out=cmf[:], in_=cmat_in), writes=[B["cmf"]], dsem="ld_cm")
        S.emit("sp", lambda e: e.dma_start(out=xin[:, :, 0:TN], in_=xT_p.rearrange("(kc p) n -> p kc n", p=128)[:, :, 0:TN]),
               writes=[B["xin"]], dsem="ld_x")
        for n, ns, w in wnames:
            S.emit("pool", lambda e, n=n: e.dma_start(out=wb16[n], in_=wf32[n]), reads=[B["xin"], B["pp"], B["cmf"]],
                   writes=[Bw[n]], dsem="cv_" + n)
        S.emit("pool", lambda e: e.tensor_copy(out=cmb[:], in_=cmf[:]), reads=[B["cmf"]], writes=[B["cmb"]])
        S.emit("pool", lambda e: e.memset(ucarry_p[:], 0.0), writes=[B["ucarry_p"]])
        S.emit("pool", lambda e: e.memset(hcarry_p[:], 0.0), writes=[B["hcarry_p"]])
        S.emit("sp", lambda e: e.dma_start(out=ucarry_s[:].rearrange("p c j -> p (c j)"), in_=conv_s_in),
               writes=[B["ucarry_s"]], dsem="ld_cs")
        S.emit("sp", lambda e: e.dma_start(out=hcarry_s[:], in_=h_s_in), writes=[B["hcarry_s"]], dsem="ld_hs")
        Bsm = B["small"]
        S.emit("pool", lambda e: e.memset(small[:, 62:63], EPS), writes=[Bsm])
        S.emit("pool", lambda e: e.memset(small[:, 63:64], 1.0), writes=[Bsm])
        S.emit("dve", lambda e: e.tensor_scalar(out=small[:, SM_HBR:SM_HBR + 11], in0=pp[:, PP_BR:PP_BR + 11],
                                                scalar1=-1.0, scalar2=None, op0=ALU.mult),
               reads=[B["pp"]], writes=[Bsm])
        S.emit("dve", lambda e: e.tensor_scalar(out=small[:, SM_HBI:SM_HBI + 11], in0=pp[:, PP_BI:PP_BI + 11],
                                                scalar1=-1.0, scalar2=None, op0=ALU.mult),
               reads=[B["pp"]], writes=[Bsm])
        S.emit("act", lambda e: e.activation(out=small[:, 35:46], in_=pp[:, PP_LAM:PP_LAM + 11], func=AF.Abs),
               reads=[B["pp"]], writes=[Bsm])
        S.emit("act", lambda e: e.activation(out=small[:, 35:46], in_=small[:, 35:46], func=AF.Exp, scale=-1.0),
               reads=[Bsm], writes=[Bsm])
        S.emit("act", lambda e: e.activation(out=small[:, 35:46], in_=small[:, 35:46], func=AF.Ln, bias=onec, scale=1.0),
               reads=[Bsm], writes=[Bsm])
        S.emit("dve", lambda e: e.tensor_scalar(out=small[:, 46:57], in0=pp[:, PP_LAM:PP_LAM + 11],
                                                scalar1=-1.0, scalar2=0.0, op0=ALU.mult, op1=ALU.max),
               reads=[B["pp"]], writes=[Bsm])
        S.emit("dve", lambda e: e.tensor_tensor(out=small[:, 46:57], in0=small[:, 46:57], in1=small[:, 35:46], op=ALU.add),
               reads=[Bsm], writes=[Bsm])
        S.emit("dve", lambda e: e.tensor_scalar(out=small[:, SM_HC:SM_HC + 11], in0=small[:, 46:57],
                                                scalar1=-8.0, scalar2=None, op0=ALU.mult),
               reads=[Bsm], writes=[Bsm])
        S.emit("dve", lambda e: e.tensor_tensor(out=lprod[:, 0:64], in0=pp[:, PP_LQ1:PP_LQ1 + 64],
                                                in1=pp[:, PP_LK1:PP_LK1 + 64], op=ALU.mult),
               reads=[B["pp"]], writes=[B["lprod"]])
        S.emit("dve", lambda e: e.tensor_tensor(out=lprod[:, 64:128], in0=pp[:, PP_LQ2:PP_LQ2 + 64],
                                                in1=pp[:, PP_LK2:PP_LK2 + 64], op=ALU.mult),
               reads=[B["pp"]], writes=[B["lprod"]])
        S.emit("dve", lambda e: e.reduce_sum(out=small[:, 57:58], in_=lprod[:, 0:64], axis=AX.X),
               reads=[B["lprod"]], writes=[Bsm])
        S.emit("dve", lambda e: e.reduce_sum(out=small[:, 58:59], in_=lprod[:, 64:128], axis=AX.X),
               reads=[B["lprod"]], writes=[Bsm])
        S.emit("act", lambda e: e.activation(out=small[:, 59:61], in_=small[:, 57:59], func=AF.Exp),
               reads=[Bsm], writes=[Bsm])
        S.emit("dve", lambda e: e.tensor_tensor(out=small[:, 61:62], in0=small[:, 60:61], in1=small[:, 59:60], op=ALU.subtract),
               reads=[Bsm], writes=[Bsm])
        S.emit("dve", lambda e: e.tensor_scalar(out=small[:, SM_NLAM:SM_NLAM + 1], in0=small[:, 61:62],
                                                scalar1=-LAM_INIT, scalar2=None, op0=ALU.add),
               reads=[Bsm], writes=[Bsm])
        S.emit("dve", lambda e: e.tensor_scalar(out=small[:, SM_HGS:SM_HGS + 1], in0=pp[:, PP_HG:PP_HG + 1],
                                                scalar1=(1.0 - LAM_INIT), scalar2=None, op0=ALU.mult),
               reads=[B["pp"]], writes=[Bsm])

        ring_i = [0]

        def load_slab(name, idx):
            slot = ring_i[0] % RING
            ring_i[0] += 1
            w = wwidth[name]
            S.emit("sp", lambda e: e.dma_start(out=ring[:, slot, 0:w], in_=wb16[name][idx]),
                   reads=[Bw[name]], writes=[Bring[slot]], dsem=f"rg{slot}")
            return slot

        def tile_slab_list():
            L = []
            for k in range(NCH + 3):
                if k < NCH:
                    L.append(("win_u", k))
                if 0 <= k - 2 < NCH:
                    L.append(("win_g", k - 2))
                    L.append(("gts", k - 2))
                if k - 3 == NCH - 3:
                    for m in range(4):
                        L.append(("wout", m))
            for m in range(4, 8):
                L.append(("wout", m))
            for h in range(8):
                L.append(("wk", h))
            for s in range(8):
                L.append(("wv", s))
            for h in range(8):
                L.append(("wq", h))
            for m in range(8):
                L.append(("wbg", m))
            for m in range(8):
                L.append(("wbo", m))
            return L

        import os as _os
        _lim = int(_os.environ.get("KTILES", str(NT)))
        tile_specs = list(range(_lim)) + (["s"] if _os.environ.get("KNOS", "0") == "0" else [])
        slab_seq = []
        for _ in tile_specs:
            slab_seq += tile_slab_list()
        slab_pos = [0]
        slab_loaded = [0]
        slab_slots = {}

        slab_pin = [None]

        def prefetch():
            base = slab_pos[0] if slab_pin[0] is None else min(slab_pos[0], slab_pin[0])
            while slab_loaded[0] < len(slab_seq) and slab_loaded[0] < base + RING - 1:
                n, i = slab_seq[slab_loaded[0]]
                slab_slots[slab_loaded[0]] = load_slab(n, i)
                slab_loaded[0] += 1

        def next_slab(name, idx):
            p = slab_pos[0]
            assert slab_seq[p] == (name, idx), (slab_seq[p], name, idx)
            if p >= slab_loaded[0]:
                prefetch_force(p)
            slot = slab_slots.pop(p)
            slab_pos[0] += 1
            return slot

        def prefetch_force(p):
            while slab_loaded[0] <= p:
                n, i = slab_seq[slab_loaded[0]]
                slab_slots[slab_loaded[0]] = load_slab(n, i)
                slab_loaded[0] += 1

        deferred = []

        def flush_stores():
            for f in deferred:
                f()
            deferred.clear()

        def store(out_ap, in_ap, rbufs, sem, wbufs=()):
            store_sems.add(sem)

            def f():
                S.emit("sp", lambda e: e.dma_start(out=out_ap, in_=in_ap), reads=rbufs, writes=list(wbufs), dsem=sem)
            deferred.append(f)

        def norm_square(src, bsrc, kc, N):
            S.emit("dve", lambda e: e.tensor_tensor(out=on[:, kc, 0:N], in0=src[:, kc, 0:N], in1=src[:, kc, 0:N], op=ALU.mult),
                   reads=[bsrc], writes=[Bon[kc]])

        def norm_sum_mm(bk, kc, N):
            S.emit("pe", lambda e: e.matmul(psb(bk)[:, 0:N], lhsT=ones_b, rhs=on[:, kc, 0:N], start=(kc == 0), stop=(kc == 7)),
                   reads=[Bon[kc], B["cmb"]], writes=[Bps[bk]], sig=True)

        def norm_rstd(bk, N):
            t = T32.alloc()
            S.emit("act", lambda e: e.activation(out=T32.ap(t)[:, 0:N], in_=psb(bk)[:, 0:N], func=AF.Ln, scale=1.0 / D, bias=epsc),
                   reads=[Bps[bk], Bsm], writes=[T32.buf(t)])
            S.emit("act", lambda e: e.activation(out=T32.ap(t)[:, 0:N], in_=T32.ap(t)[:, 0:N], func=AF.Exp, scale=-0.5),
                   reads=[T32.buf(t)], writes=[T32.buf(t)])
            return t

        def norm_apply(src, bsrc, gcol, t, dst, bdst, N):
            for kc in range(8):
                S.emit("dve", lambda e, kc=kc: e.scalar_tensor_tensor(out=dst[:, kc, 0:N], in0=src[:, kc, 0:N],
                                                                      scalar=ppc(gcol + kc), in1=T32.ap(t)[:, 0:N],
                                                                      op0=ALU.mult, op1=ALU.mult),
                       reads=[bsrc, T32.buf(t), B["pp"]], writes=[bdst[kc]])

        def tile_params(spec):
            is_s = spec == "s"
            return dict(N=NS if is_s else TN, tok0=0 if is_s else spec * TN, xsrc=xT_s if is_s else xT_p)

        def load_x(spec):
            p_ = tile_params(spec)
            S.emit("sp", lambda e: e.dma_start(out=xin[:, :, 0:p_["N"]],
                                               in_=p_["xsrc"].rearrange("(kc p) n -> p kc n", p=128)[:, :, p_["tok0"]:p_["tok0"] + p_["N"]]),
                   writes=[B["xin"]], dsem="ld_x")

        def a_norm(spec):
            N = tile_params(spec)["N"]
            for kc in range(8):
                norm_square(xin, B["xin"], kc, N)
            bk = bank()
            for kc in range(8):
                norm_sum_mm(bk, kc, N)
            t = norm_rstd(bk, N)
            norm_apply(xin, B["xin"], PP_AN, t, xn, Bxn, N)
            T32.release(t)

        def proj_fm(slot, nk, rhs_of, rbufs, N, coff=0):
            bk = bank()
            for k in range(nk):
                S.emit("pe", lambda e, k=k: e.matmul(psb(bk)[:, 0:N], lhsT=ring[:, slot, coff + k * 128: coff + (k + 1) * 128],
                                                      rhs=rhs_of(k), start=(k == 0), stop=(k == nk - 1)),
                       reads=[Bring[slot]] + (rbufs(k) if callable(rbufs) else rbufs), writes=[Bps[bk]], sig=(k == nk - 1))
            return bk

        def qk_stage1(kind, h, N, xsrc_ap, bxsrc):
            slot = next_slab("wk" if kind == "k" else "wq", h)
            prefetch()
            bk = proj_fm(slot, 8, lambda k: xsrc_ap[:, k, 0:N], (lambda k: [bxsrc[k]]), N)
            kb = T16.alloc()
            sq = T16.alloc()
            S.emit("act", lambda e: e.activation(out=T16.ap(kb)[:, 0:N], in_=psb(bk)[:, 0:N], func=AF.Copy),
                   reads=[Bps[bk]], writes=[T16.buf(kb)])
            S.emit("act", lambda e: e.activation(out=T16.ap(sq)[:, 0:N], in_=psb(bk)[:, 0:N], func=AF.Square),
                   reads=[Bps[bk]], writes=[T16.buf(sq)])
            return (bk, kb, sq)

        def qk_stage2(kind, st, N, ropei, dst_fn):
            bk, kb, sq = st
            bsw = bank()
            S.emit("pe", lambda e: e.matmul(psb(bsw)[:, 0:N], lhsT=perm_b, rhs=T16.ap(kb)[:, 0:N], start=True, stop=True),
                   reads=[T16.buf(kb), B["cmb"]], writes=[Bps[bsw]])
            bss = bank()
            S.emit("pe", lambda e: e.matmul(psb(bss)[:, 0:N], lhsT=blk_b, rhs=T16.ap(sq)[:, 0:N], start=True, stop=True),
                   reads=[T16.buf(sq), B["cmb"]], writes=[Bps[bss]])
            T16.release(kb)
            T16.release(sq)
            rk = T32.alloc()
            S.emit("act", lambda e: e.activation(out=T32.ap(rk)[:, 0:N], in_=psb(bss)[:, 0:N], func=AF.Ln, scale=1.0 / 64, bias=epsc),
                   reads=[Bps[bss], Bsm], writes=[T32.buf(rk)])
            S.emit("act", lambda e: e.activation(out=T32.ap(rk)[:, 0:N], in_=T32.ap(rk)[:, 0:N], func=AF.Exp, scale=-0.5),
                   reads=[T32.buf(rk)], writes=[T32.buf(rk)])
            g0, g1 = (PP_GK, PP_GKS) if kind == "k" else (PP_GQ, PP_GQS)
            u1 = T32.alloc()
            u2 = T32.alloc()
            S.emit("dve", lambda e: e.scalar_tensor_tensor(out=T32.ap(u1)[:, 0:N], in0=psb(bk)[:, 0:N], scalar=ppc(g0),
                                                           in1=rope[:, ropei, 0, 0:N], op0=ALU.mult, op1=ALU.mult),
                   reads=[Bps[bk], Brope[ropei], B["pp"]], writes=[T32.buf(u1)])
            S.emit("dve", lambda e: e.scalar_tensor_tensor(out=T32.ap(u2)[:, 0:N], in0=psb(bsw)[:, 0:N], scalar=ppc(g1),
                                                           in1=rope[:, ropei, 1, 0:N], op0=ALU.mult, op1=ALU.mult),
                   reads=[Bps[bsw], Brope[ropei], B["pp"]], writes=[T32.buf(u2)])
            S.emit("dve", lambda e: e.tensor_tensor(out=T32.ap(u1)[:, 0:N], in0=T32.ap(u1)[:, 0:N], in1=T32.ap(u2)[:, 0:N], op=ALU.add),
                   reads=[T32.buf(u1), T32.buf(u2)], writes=[T32.buf(u1)])
            T32.release(u2)
            dst_fn(u1, rk)
            T32.release(u1)
            T32.release(rk)

        def qk_heads(kind, N, ropei, xsrc_ap, bxsrc, dst_of):
            st = qk_stage1(kind, 0, N, xsrc_ap, bxsrc)
            for h in range(NH):
                nxt = qk_stage1(kind, h + 1, N, xsrc_ap, bxsrc) if h + 1 < NH else None
                qk_stage2(kind, st, N, ropei, dst_of(h))
                st = nxt

        def run_tile(spec, next_spec, first):
            is_s = spec == "s"
            N = NS if is_s else TN
            tix = 0 if is_s else spec
            tok0 = 0 if is_s else spec * TN
            xsrc = xT_s if is_s else xT_p
            ysrc = yT_s if is_s else yT_p
            kTo = kT_s if is_s else kT_p
            vo = v_s if is_s else v_p
            ucarry, hcarry = (ucarry_s, hcarry_s) if is_s else (ucarry_p, hcarry_p)
            Buc, Bhc = (B["ucarry_s"], B["hcarry_s"]) if is_s else (B["ucarry_p"], B["hcarry_p"])
            rope_off = SEQ if is_s else tok0
            ropei = (0 if is_s else (spec + 1)) % 2
            NB = (N + 127) // 128

            S.emit("sp", lambda e: e.dma_start(out=rope[:, ropei, :, 0:N], in_=rope_in[:, :, rope_off:rope_off + N]),
                   writes=[Brope[ropei]], dsem=f"ld_rope{ropei}")
            prefetch()

            if first:
                a_norm(spec)
            ucf = {}

            def sigmoid_from_psum(tb_, bk_):
                S.emit("act", lambda e: e.activation(out=T32.ap(tb_)[:, 0:N], in_=psb(bk_)[:, 0:N], func=AF.Exp, scale=-1.0),
                       reads=[Bps[bk_]], writes=[T32.buf(tb_)])
                S.emit("act", lambda e: e.activation(out=T32.ap(tb_)[:, 0:N], in_=T32.ap(tb_)[:, 0:N], func=AF.Ln, scale=1.0, bias=onec),
                       reads=[T32.buf(tb_), Bsm], writes=[T32.buf(tb_)])
                S.emit("act", lambda e: e.activation(out=T32.ap(tb_)[:, 0:N], in_=T32.ap(tb_)[:, 0:N], func=AF.Exp, scale=-1.0),
                       reads=[T32.buf(tb_)], writes=[T32.buf(tb_)])

            def stage_u(c):
                slot = next_slab("win_u", c)
                prefetch()
                bk = proj_fm(slot, 8, lambda k: xn[:, k, 0:N], (lambda k: [Bxn[k]]), N)
                ur = T32.alloc()
                uf = T32.alloc()
                S.emit("dve", lambda e: e.tensor_copy(out=T32.ap(ur)[:, 0:3], in_=ucarry[:, c, :]),
                       reads=[Buc], writes=[T32.buf(ur)])
                S.emit("dve", lambda e: e.tensor_copy(out=T32.ap(ur)[:, 3:3 + N], in_=psb(bk)[:, 0:N]),
                       reads=[Bps[bk]], writes=[T32.buf(ur)])
                S.emit("act", lambda e: e.activation(out=T32.ap(uf)[:, 0:N], in_=T32.ap(ur)[:, 3:3 + N], func=AF.Identity,
                                                     scale=ppc(PP_CW + 4 * c + 3), bias=ppc(PP_CB + c)),
                       reads=[T32.buf(ur), B["pp"]], writes=[T32.buf(uf)])
                for j in (2, 1, 0):
                    S.emit("dve", lambda e, j=j: e.scalar_tensor_tensor(out=T32.ap(uf)[:, 0:N], in0=T32.ap(ur)[:, j:j + N],
                                                                        scalar=ppc(PP_CW + 4 * c + j), in1=T32.ap(uf)[:, 0:N],
                                                                        op0=ALU.mult, op1=ALU.add),
                           reads=[T32.buf(ur), T32.buf(uf), B["pp"]], writes=[T32.buf(uf)])
                S.emit("dve", lambda e: e.tensor_copy(out=ucarry[:, c, :], in_=T32.ap(ur)[:, N:N + 3]),
                       reads=[T32.buf(ur)], writes=[Buc])
                S.emit("pool", lambda e: e.tensor_copy(out=b16[:, c, 0:N], in_=T32.ap(uf)[:, 0:N]),
                       reads=[T32.buf(uf)], writes=[Bb16[c]])
                T32.release(ur)
                ucf[c] = uf

            gst = {}

            def g1(c):
                slot2 = next_slab("win_g", c)
                prefetch()
                bkg = proj_fm(slot2, 8, lambda k: xn[:, k, 0:N], (lambda k: [Bxn[k]]), N)
                tg = T32.alloc()
                sigmoid_from_psum(tg, bkg)
                S.emit("dve", lambda e: e.tensor_tensor(out=T32.ap(tg)[:, 0:N], in0=T32.ap(tg)[:, 0:N], in1=psb(bkg)[:, 0:N], op=ALU.mult),
                       reads=[T32.buf(tg), Bps[bkg]], writes=[T32.buf(tg)])
                slot = next_slab("gts", c)
                prefetch()
                dks = [dk for dk in range(3) if 0 <= c + dk - 1 < NCH]
                bkr = bank()
                bki = bank()
                for g, bk in ((0, bkr), (1, bki)):
                    for n_, dk in enumerate(dks):
                        S.emit("pe", lambda e, g=g, bk=bk, dk=dk, n_=n_: e.matmul(
                            psb(bk)[:, 0:N], lhsT=ring[:, slot, (dk * 2 + g) * 128:(dk * 2 + g + 1) * 128],
                            rhs=b16[:, c + dk - 1, 0:N], start=(n_ == 0), stop=(n_ == len(dks) - 1)),
                            reads=[Bring[slot], Bb16[c + dk - 1]], writes=[Bps[bk]], sig=(n_ == len(dks) - 1))
                tr = T32.alloc()
                ti = T32.alloc()
                a = T32.alloc()
                m = T32.alloc()
                S.emit("act", lambda e: e.activation(out=T32.ap(tr)[:, 0:N], in_=psb(bkr)[:, 0:N], func=AF.Exp,
                                                     scale=-1.0, bias=smc(SM_HBR + c)),
                       reads=[Bps[bkr], Bsm], writes=[T32.buf(tr)])
                S.emit("act", lambda e: e.activation(out=T32.ap(ti)[:, 0:N], in_=psb(bki)[:, 0:N], func=AF.Exp,
                                                     scale=-1.0, bias=smc(SM_HBI + c)),
                       reads=[Bps[bki], Bsm], writes=[T32.buf(ti)])
                S.emit("act", lambda e: e.activation(out=T32.ap(tr)[:, 0:N], in_=T32.ap(tr)[:, 0:N], func=AF.Ln, scale=1.0, bias=onec),
                       reads=[T32.buf(tr), Bsm], writes=[T32.buf(tr)])
                S.emit("act", lambda e: e.activation(out=T32.ap(ti)[:, 0:N], in_=T32.ap(ti)[:, 0:N], func=AF.Ln, scale=1.0, bias=onec),
                       reads=[T32.buf(ti), Bsm], writes=[T32.buf(ti)])
                S.emit("act", lambda e: e.activation(out=T32.ap(tr)[:, 0:N], in_=T32.ap(tr)[:, 0:N], func=AF.Exp, scale=-1.0),
                       reads=[T32.buf(tr)], writes=[T32.buf(tr)])
                S.emit("act", lambda e: e.activation(out=T32.ap(ti)[:, 0:N], in_=T32.ap(ti)[:, 0:N], func=AF.Exp, scale=-1.0),
                       reads=[T32.buf(ti)], writes=[T32.buf(ti)])
                S.emit("act", lambda e: e.activation(out=T32.ap(a)[:, 0:N], in_=T32.ap(tr)[:, 0:N], func=AF.Exp, scale=smc(SM_HC + c)),
                       reads=[T32.buf(tr), Bsm], writes=[T32.buf(a)])
                T32.release(tr)
                S.emit("pool", lambda e: e.tensor_tensor(out=T32.ap(m)[:, 0:N], in0=T32.ap(a)[:, 0:N], in1=T32.ap(a)[:, 0:N], op=ALU.mult),
                       reads=[T32.buf(a)], writes=[T32.buf(m)])
                gst[c] = (tg, ti, a, m)

            def g2_act(c):
                tg, ti, a, m = gst[c]
                S.emit("act", lambda e: e.activation(out=T32.ap(m)[:, 0:N], in_=T32.ap(m)[:, 0:N], func=AF.Ln, scale=-1.0, bias=onec),
                       reads=[T32.buf(m), Bsm], writes=[T32.buf(m)])
                S.emit("act", lambda e: e.activation(out=T32.ap(m)[:, 0:N], in_=T32.ap(m)[:, 0:N], func=AF.Exp, scale=0.5),
                       reads=[T32.buf(m)], writes=[T32.buf(m)])
                if (not is_s) and tix == 0:
                    S.emit("pool", lambda e: e.memset(T32.ap(m)[:, 0:1], 1.0), writes=[T32.buf(m)])

            def g2_dve1(c):
                tg, ti, a, m = gst[c]
                uf = ucf.pop(c)
                S.emit("dve", lambda e: e.tensor_tensor(out=T32.ap(ti)[:, 0:N], in0=T32.ap(ti)[:, 0:N], in1=T32.ap(uf)[:, 0:N], op=ALU.mult),
                       reads=[T32.buf(ti), T32.buf(uf)], writes=[T32.buf(ti)])
                T32.release(uf)
                S.emit("dve", lambda e: e.tensor_tensor(out=T32.ap(ti)[:, 0:N], in0=T32.ap(ti)[:, 0:N], in1=T32.ap(m)[:, 0:N], op=ALU.mult),
                       reads=[T32.buf(ti), T32.buf(m)], writes=[T32.buf(ti)])
                T32.release(m)
                hh = T32.alloc()
                S.emit("dve", lambda e: e.tensor_tensor_scan(out=T32.ap(hh)[:, 0:N], data0=T32.ap(a)[:, 0:N], data1=T32.ap(ti)[:, 0:N],
                                                             initial=hcarry[:, c:c + 1], op0=ALU.mult, op1=ALU.add),
                       reads=[T32.buf(a), T32.buf(ti), Bhc], writes=[T32.buf(hh)])
                T32.release(a)
                T32.release(ti)
                S.emit("dve", lambda e: e.tensor_copy(out=hcarry[:, c:c + 1], in_=T32.ap(hh)[:, N - 1:N]),
                       reads=[T32.buf(hh)], writes=[Bhc])
                gst[c] = (tg, hh)

            def g2_dve2(c):
                tg, hh = gst.pop(c)
                S.emit("dve", lambda e: e.tensor_tensor(out=b16[:, 11 + c, 0:N], in0=T32.ap(hh)[:, 0:N], in1=T32.ap(tg)[:, 0:N], op=ALU.mult),
                       reads=[T32.buf(hh), T32.buf(tg)], writes=[Bb16[11 + c]])
                T32.release(hh)
                T32.release(tg)

            aout = {}

            def aout_part(cs):
                if "slots" not in aout:
                    slab_pin[0] = slab_pos[0]
                    aout["slots"] = [next_slab("wout", j) for j in range(4)]
                    prefetch()
                    aout["banks"] = [bank() for _ in range(4)]
                    bank_held.update(aout["banks"])
                for c in cs:
                    for j in range(4):
                        S.emit("pe", lambda e, c=c, j=j: e.matmul(psb(aout["banks"][j])[:, 0:N],
                                                                  lhsT=ring[:, aout["slots"][j], c * 128:(c + 1) * 128],
                                                                  rhs=b16[:, 11 + c, 0:N], start=(c == 0), stop=(c == NCH - 1)),
                               reads=[Bring[aout["slots"][j]], Bb16[11 + c]], writes=[Bps[aout["banks"][j]]], sig=(j == 3))

            for k in range(NCH + 3):
                c1, c2 = k - 2, k - 3
                if 0 <= c2 < NCH:
                    g2_act(c2)
                if k < NCH:
                    stage_u(k)
                if 0 <= c2 < NCH:
                    g2_dve1(c2)
                if 0 <= c1 < NCH:
                    g1(c1)
                if 0 <= c2 < NCH:
                    g2_dve2(c2)
                if c2 == NCH - 3:
                    aout_part(range(0, NCH - 2))
                elif c2 == NCH - 2:
                    aout_part([NCH - 2])
                elif c2 == NCH - 1:
                    aout_part([NCH - 1])
                    slab_pin[0] = None

            bkn = bank()
            bank_held.add(bkn)
            for m_ in range(8):
                if m_ < 4:
                    bk = aout["banks"][m_]
                else:
                    slot = next_slab("wout", m_)
                    prefetch()
                    bk = proj_fm(slot, NCH, lambda k: b16[:, 11 + k, 0:N], [Bb16[11 + k] for k in range(NCH)], N)
                if m_ == 4:
                    for b_ in aout["banks"]:
                        bank_held.discard(b_)
                if m_ >= 1:
                    norm_sum_mm(bkn, m_ - 1, N)
                S.emit("dve", lambda e, m_=m_, bk=bk: e.tensor_tensor(out=x1[:, m_, 0:N], in0=xin[:, m_, 0:N], in1=psb(bk)[:, 0:N], op=ALU.add),
                       reads=[B["xin"], Bps[bk]], writes=[B["x1"]])
                norm_square(x1, B["x1"], m_, N)
            norm_sum_mm(bkn, 7, N)
            bank_held.discard(bkn)
            t_x1 = norm_rstd(bkn, N)
            if next_spec is not None:
                load_x(next_spec)
            if is_s:
                store(conv_s_o, ucarry_s[:].rearrange("p c j -> p (c j)"), [B["ucarry_s"]], "st_misc", [B["conv_s_o"]])
                store(h_s_o, hcarry_s[:], [B["hcarry_s"]], "st_misc", [B["h_s_o"]])
            elif tix == NT - 1:
                store(conv_p_o, ucarry_p[:].rearrange("p c j -> p (c j)"), [B["ucarry_p"]], "st_misc", [B["conv_p_o"]])
                store(h_p_o, hcarry_p[:], [B["hcarry_p"]], "st_misc", [B["h_p_o"]])
            flush_stores()

            norm_apply(x1, B["x1"], PP_KVN, t_x1, xn, Bxn, N)
            def kdst_of(h):
                def kdst(u1, rk, h=h):
                    kst = T32.alloc()
                    S.emit("dve", lambda e: e.tensor_tensor(out=T32.ap(kst)[:, 0:N], in0=T32.ap(u1)[:, 0:N], in1=T32.ap(rk)[:, 0:N], op=ALU.mult),
                           reads=[T32.buf(u1), T32.buf(rk)], writes=[T32.buf(kst)])
                    S.emit("pool", lambda e: e.tensor_copy(out=b16[:, h, 0:N], in_=T32.ap(kst)[:, 0:N]),
                           reads=[T32.buf(kst)], writes=[Bb16[h]])
                    store_sems.add(f"st32_{kst}")
                    S.emit("sp", lambda e: e.dma_start(out=kTo[h, :, tok0:tok0 + N], in_=T32.ap(kst)[:, 0:N]),
                           reads=[T32.buf(kst)], dsem=f"st32_{kst}")
                    T32.release(kst)
                return kdst
            qk_heads("k", N, ropei, xn, Bxn, kdst_of)
            if not is_s:
                S.emit("sp", lambda e: e.dma_start(out=kscr.rearrange("h p n -> p h n")[:, :, tok0:tok0 + N], in_=b16[:, 0:8, 0:N]),
                       reads=[Bb16[h] for h in range(8)], writes=[B["kscr"]], dsem="st_kscr")
            norm_apply(x1, B["x1"], PP_BN, t_x1, xn2, Bxn2, N)
            T32.release(t_x1)
            for fh in range(2):
                bks = [bank() for _ in range(NB)]
                for sp_ in range(4):
                    slot = next_slab("wv", fh * 4 + sp_)
                    prefetch()
                    for tb in range(NB):
                        nt = min(128, N - tb * 128)
                        for kk in range(2):
                            kc = 2 * sp_ + kk
                            last = (sp_ == 3 and kk == 1)
                            S.emit("pe", lambda e, tb=tb, nt=nt, kk=kk, kc=kc, last=last, slot=slot, sp_=sp_, bks=bks: e.matmul(
                                psb(bks[tb])[0:nt, :], lhsT=xn[:, kc, tb * 128: tb * 128 + nt],
                                rhs=ring[:, slot, kk * 512:(kk + 1) * 512], start=(sp_ == 0 and kk == 0), stop=last),
                                reads=[Bxn[kc], Bring[slot]], writes=[Bps[bks[tb]]], sig=(kk == 1))
                for tb in range(NB):
                    nt = min(128, N - tb * 128)
                    vs = T32.alloc()
                    S.emit("act", lambda e, tb=tb, nt=nt, vs=vs, bks=bks: e.activation(out=T32.ap(vs)[0:nt, 0:512], in_=psb(bks[tb])[0:nt, :], func=AF.Copy),
                           reads=[Bps[bks[tb]]], writes=[T32.buf(vs)])
                    S.emit("pool", lambda e, tb=tb, nt=nt, vs=vs, fh=fh: e.tensor_copy(
                        out=b16[0:nt, 11 + 4 * fh: 11 + 4 * fh + 4, tb * 128:(tb + 1) * 128],
                        in_=T32.ap(vs)[0:nt, 0:512].rearrange("p (h e) -> p h e", e=128)),
                        reads=[T32.buf(vs)], writes=[Bb16[11 + 4 * fh + i] for i in range(4)])
                    store_sems.add(f"st32_{vs}")
                    S.emit("sp", lambda e, tb=tb, nt=nt, vs=vs, fh=fh: e.dma_start(
                        out=vo[tok0 + tb * 128: tok0 + tb * 128 + nt, fh * 512:(fh + 1) * 512], in_=T32.ap(vs)[0:nt, 0:512]),
                        reads=[T32.buf(vs)], dsem=f"st32_{vs}")
                    T32.release(vs)
            if not is_s:
                S.emit("sp", lambda e: e.dma_start(
                    out=vscr.rearrange("h p kb e -> p h kb e")[:, :, 4 * tix:4 * tix + 4, :],
                    in_=b16[:, 11:19, :].rearrange("p h (tb e) -> p h tb e", e=128)),
                    reads=[Bb16[11 + h] for h in range(8)], writes=[B["vscr"]], dsem="st_vscr")

            nkeys_past = PAST if is_s else (tix + 1) * TN

            def load_head(h):
                bi = h % 2
                if is_s:
                    S.emit("pool", lambda e: e.dma_start(out=kTh[:, bi, 0:PAST], in_=ckT[h]),
                           writes=[BkTh[bi]], dsem=f"ldp_k{bi}")
                    S.emit("pool", lambda e: e.dma_start(
                        out=vh[:, bi, 0:8, :], in_=cv.rearrange("(kb p) (h e) -> h p kb e", p=128, e=128)[h]),
                        writes=[Bvh[bi]], dsem=f"ldp_v{bi}")
                else:
                    S.emit("sp", lambda e: e.dma_start(out=kTh[:, bi, 0:nkeys_past], in_=kscr[h, :, 0:nkeys_past]),
                           reads=[B["kscr"]], writes=[BkTh[bi]], dsem=f"ld_k{bi}")
                    S.emit("sp", lambda e: e.dma_start(out=vh[:, bi, 0:4 * (tix + 1), :], in_=vscr[h, :, 0:4 * (tix + 1), :]),
                           reads=[B["vscr"]], writes=[Bvh[bi]], dsem=f"ld_v{bi}")

            def finish_head_load(h):
                bi = h % 2
                if is_s:
                    S.emit("pool", lambda e: e.tensor_copy(out=kTh[:, bi, PAST:PAST + NS], in_=b16[:, h, 0:NS]),
                           reads=[Bb16[h]], writes=[BkTh[bi]])
                    S.emit("pool", lambda e: e.tensor_copy(out=vh[0:NS, bi, 8, :], in_=b16[0:NS, 11 + h, 0:128]),
                           reads=[Bb16[11 + h]], writes=[Bvh[bi]])

            if is_s:
                pass

            if is_s:
                qbase, gbase = 8, 19
            if is_s:
                def qT_ap(h):
                    return kTh[:, 0, 2048 + h * NS: 2048 + (h + 1) * NS]

                def sg_ap(m_):
                    return kTh[:, 0, 3072 + m_ * NS: 3072 + (m_ + 1) * NS]
                BqT = [Buf(f"qTs{h}") for h in range(8)]
                Bsg = [Buf(f"sgs{h}") for h in range(8)]
            else:
                def qT_ap(h):
                    return b16[:, h, 0:N]

                def sg_ap(m_):
                    return b16[:, 11 + m_, 0:N]
                BqT = [Bb16[h] for h in range(8)]
                Bsg = [Bb16[11 + h] for h in range(8)]

            if is_s:
                pass

            def qdst_of(h):
                def qdst(u1, rk, h=h):
                    S.emit("dve", lambda e: e.tensor_tensor(out=qT_ap(h), in0=T32.ap(u1)[:, 0:N], in1=T32.ap(rk)[:, 0:N], op=ALU.mult),
                           reads=[T32.buf(u1), T32.buf(rk)], writes=[BqT[h]])
                return qdst
            qk_heads("q", N, ropei, xn2, Bxn2, qdst_of)
            for m_ in range(8):
                slot = next_slab("wbg", m_)
                prefetch()
                bkg = proj_fm(slot, 8, lambda k: xn2[:, k, 0:N], (lambda k: [Bxn2[k]]), N)
                tg = T32.alloc()
                sigmoid_from_psum(tg, bkg)
                S.emit("dve", lambda e, tg=tg, bkg=bkg, m_=m_: e.tensor_tensor(out=sg_ap(m_), in0=T32.ap(tg)[:, 0:N], in1=psb(bkg)[:, 0:N], op=ALU.mult),
                       reads=[T32.buf(tg), Bps[bkg]], writes=[Bsg[m_]])
                T32.release(tg)
            flush_stores()

            if next_spec is not None:
                a_norm(next_spec)
            def blocks():
                L = []
                if is_s:
                    for kb_ in range(8):
                        L.append((kb_ * 128, 128, kb_, 0, False))
                    L.append((PAST, NS, 8, 0, False))
                else:
                    for kd in range(4):
                        L.append(((4 * tix + kd) * 128, 128, 4 * tix + kd, 128 * kd, True))
                    for kb_ in range(4 * tix):
                        L.append((kb_ * 128, 128, kb_, 0, False))
                return L

            blks = blocks()
            nblk = len(blks)
            steps = [(h, i) for h in range(NH) for i in range(nblk)]
            O1, O2, L1, L2 = 4, 5, 6, 7
            load_head(0)
            finish_head_load(0)
            load_head(1)
            finish_head_load(1)

            def emit_qk(s):
                h, i = steps[s]
                col0, nk, vb, q0, corner = blks[i]
                sp_ = s % 2
                bi = h % 2
                S.emit("pe", lambda e: e.matmul(psum[sp_][0:nk, 0, q0:N], lhsT=kTh[0:64, bi, col0:col0 + nk],
                                                rhs=qT_ap(h)[0:64, q0:N] if not is_s else qT_ap(h)[0:64, :], start=True, stop=True),
                       reads=[BkTh[bi], BqT[h]], writes=[Bps[2 * sp_]], sig=False)
                S.emit("pe", lambda e: e.matmul(psum[sp_][0:nk, 1, q0:N], lhsT=kTh[64:128, bi, col0:col0 + nk],
                                                rhs=qT_ap(h)[64:128, q0:N] if not is_s else qT_ap(h)[64:128, :], start=True, stop=True),
                       reads=[BkTh[bi], BqT[h]], writes=[Bps[2 * sp_ + 1]])

            def emit_exp_pv(s):
                h, i = steps[s]
                col0, nk, vb, q0, corner = blks[i]
                sp_ = s % 2
                pi = s % 3
                bi = h % 2
                first = (i == 0)
                last = (i == nblk - 1)
                S.emit("act", lambda e: e.activation(out=pt[0:nk, pi, :, q0:N], in_=psum[sp_][0:nk, :, q0:N], func=AF.Exp, scale=0.125),
                       reads=[Bps[2 * sp_], Bps[2 * sp_ + 1]], writes=[Bpt[pi]])
                if corner:
                    S.emit("pool", lambda e: e.memset(pt[64:128, pi, :, q0:q0 + 64], 0.0), writes=[Bpt[pi]])
                for c_, ob in ((0, O1), (1, O2)):
                    S.emit("pe", lambda e, c_=c_, ob=ob: e.matmul(psb(ob)[:, q0:N], lhsT=vh[0:nk, bi, vb, :], rhs=pt[0:nk, pi, c_, q0:N],
                                                                   start=first, stop=last),
                           reads=[Bvh[bi], Bpt[pi]], writes=[Bps[ob]], sig=(c_ == 1))
                par = h % 2
                for c_, eng_ in ((0, "dve"), (1, "pool")):
                    if first:
                        S.emit(eng_, lambda e, c_=c_: e.tensor_copy(out=pacc[0:nk, par, c_, q0:N], in_=pt[0:nk, pi, c_, q0:N]),
                               reads=[Bpt[pi]], writes=[Bpacc[par][c_]])
                    else:
                        S.emit(eng_, lambda e, c_=c_: e.tensor_tensor(out=pacc[0:nk, par, c_, q0:N], in0=pacc[0:nk, par, c_, q0:N],
                                                                      in1=pt[0:nk, pi, c_, q0:N], op=ALU.add),
                               reads=[Bpt[pi], Bpacc[par][c_]], writes=[Bpacc[par][c_]])

            fin_state = {}

            def fin_a(h):
                r1 = T32.alloc()
                r2 = T32.alloc()
                o1 = T32.alloc()
                o2 = T32.alloc()
                S.emit("dve", lambda e: e.tensor_copy(out=T32.ap(o1)[:, 0:N], in_=psb(O1)[:, 0:N]), reads=[Bps[O1]], writes=[T32.buf(o1)])
                S.emit("dve", lambda e: e.tensor_copy(out=T32.ap(o2)[:, 0:N], in_=psb(O2)[:, 0:N]), reads=[Bps[O2]], writes=[T32.buf(o2)])
                fin_state[h] = (r1, r2, o1, o2)

            def fin_L(h):
                r1, r2, o1, o2 = fin_state[h]
                par = h % 2
                S.emit("pe", lambda e: e.matmul(psb(L1)[:, 0:N], lhsT=ones_f, rhs=pacc[:, par, 0, 0:N], start=True, stop=True),
                       reads=[Bpacc[par][0], B["cmf"]], writes=[Bps[L1]])
                S.emit("pe", lambda e: e.matmul(psb(L2)[:, 0:N], lhsT=ones_f, rhs=pacc[:, par, 1, 0:N], start=True, stop=True),
                       reads=[Bpacc[par][1], B["cmf"]], writes=[Bps[L2]])
                S.emit("act", lambda e: e.activation(out=T32.ap(r1)[:, 0:N], in_=psb(L1)[:, 0:N], func=AF.Ln), reads=[Bps[L1]], writes=[T32.buf(r1)])
                S.emit("act", lambda e: e.activation(out=T32.ap(r2)[:, 0:N], in_=psb(L2)[:, 0:N], func=AF.Ln), reads=[Bps[L2]], writes=[T32.buf(r2)])

            def fin_a2(h):
                r1, r2, o1, o2 = fin_state.pop(h)
                S.emit("act", lambda e: e.activation(out=T32.ap(r1)[:, 0:N], in_=T32.ap(r1)[:, 0:N], func=AF.Exp, scale=-1.0),
                       reads=[T32.buf(r1)], writes=[T32.buf(r1)])
                S.emit("act", lambda e: e.activation(out=T32.ap(r2)[:, 0:N], in_=T32.ap(r2)[:, 0:N], func=AF.Exp, scale=-1.0),
                       reads=[T32.buf(r2)], writes=[T32.buf(r2)])
                S.emit("dve", lambda e: e.tensor_tensor(out=T32.ap(r1)[:, 0:N], in0=T32.ap(o1)[:, 0:N], in1=T32.ap(r1)[:, 0:N], op=ALU.mult),
                       reads=[T32.buf(o1), T32.buf(r1)], writes=[T32.buf(r1)])
                S.emit("dve", lambda e: e.tensor_tensor(out=T32.ap(r2)[:, 0:N], in0=T32.ap(o2)[:, 0:N], in1=T32.ap(r2)[:, 0:N], op=ALU.mult),
                       reads=[T32.buf(o2), T32.buf(r2)], writes=[T32.buf(r2)])
                T32.release(o1)
                T32.release(o2)
                S.emit("dve", lambda e: e.scalar_tensor_tensor(out=T32.ap(r1)[:, 0:N], in0=T32.ap(r2)[:, 0:N], scalar=smc(SM_NLAM),
                                                               in1=T32.ap(r1)[:, 0:N], op0=ALU.mult, op1=ALU.add),
                       reads=[T32.buf(r1), T32.buf(r2), Bsm], writes=[T32.buf(r1)])
                T32.release(r2)
                sq = T16.alloc()
                S.emit("dve", lambda e: e.tensor_tensor(out=T16.ap(sq)[:, 0:N], in0=T32.ap(r1)[:, 0:N], in1=T32.ap(r1)[:, 0:N], op=ALU.mult),
                       reads=[T32.buf(r1)], writes=[T16.buf(sq)])
                fin_state[h] = (r1, sq)

            def fin_b(h, bk):
                r1, sq = fin_state.pop(h)
                S.emit("pe", lambda e: e.matmul(psb(bk)[:, 0:N], lhsT=ones_b, rhs=T16.ap(sq)[:, 0:N], start=True, stop=True),
                       reads=[T16.buf(sq), B["cmb"]], writes=[Bps[bk]])
                T16.release(sq)
                rh = T32.alloc()
                S.emit("act", lambda e: e.activation(out=T32.ap(rh)[:, 0:N], in_=psb(bk)[:, 0:N], func=AF.Ln, scale=1.0 / 128, bias=epsc),
                       reads=[Bps[bk], Bsm], writes=[T32.buf(rh)])
                S.emit("act", lambda e: e.activation(out=T32.ap(rh)[:, 0:N], in_=T32.ap(rh)[:, 0:N], func=AF.Exp, scale=-0.5),
                       reads=[T32.buf(rh)], writes=[T32.buf(rh)])
                S.emit("dve", lambda e: e.scalar_tensor_tensor(out=T32.ap(r1)[:, 0:N], in0=T32.ap(r1)[:, 0:N], scalar=smc(SM_HGS),
                                                               in1=T32.ap(rh)[:, 0:N], op0=ALU.mult, op1=ALU.mult),
                       reads=[T32.buf(r1), T32.buf(rh), Bsm], writes=[T32.buf(r1)])
                S.emit("dve", lambda e: e.tensor_tensor(out=on[:, h, 0:N], in0=T32.ap(r1)[:, 0:N], in1=sg_ap(h), op=ALU.mult),
                       reads=[T32.buf(r1), Bsg[h]], writes=[Bon[h]])
                T32.release(r1)
                T32.release(rh)

            ns = len(steps)
            emit_qk(0)
            if ns > 1:
                emit_qk(1)
            pending = []

            def run_fin(kind, h_, s_):
                if kind == "L":
                    fin_L(h_)
                elif kind == "a2":
                    fin_a2(h_)
                else:
                    fin_b(h_, 2 * (s_ % 2))

            for s in range(ns):
                h, i = steps[s]
                emit_exp_pv(s)
                for it in [p_ for p_ in pending if p_[0] <= s]:
                    pending.remove(it)
                    run_fin(it[1], it[2], s)
                if i == nblk - 1:
                    fin_a(h)
                    pending += [(s + 1, "L", h), (s + 2, "a2", h), (s + 4, "b", h)]
                    if h + 2 < NH:
                        load_head(h + 2)
                        finish_head_load(h + 2)
                if s + 2 < ns:
                    emit_qk(s + 2)
            for it in pending:
                run_fin(it[1], it[2], 0)

            for m_ in range(8):
                slot = next_slab("wbo", m_)
                prefetch()
                bk = proj_fm(slot, 8, lambda k: on[:, k, 0:N], Bon, N)
                ys = T32.alloc()
                S.emit("dve", lambda e, m_=m_, bk=bk, ys=ys: e.tensor_tensor(out=T32.ap(ys)[:, 0:N], in0=x1[:, m_, 0:N], in1=psb(bk)[:, 0:N], op=ALU.add),
                       reads=[B["x1"], Bps[bk]], writes=[T32.buf(ys)])
                store_sems.add(f"st32_{ys}")
                S.emit("sp", lambda e, m_=m_, ys=ys: e.dma_start(out=ysrc[m_ * 128:(m_ + 1) * 128, tok0:tok0 + N], in_=T32.ap(ys)[:, 0:N]),
                       reads=[T32.buf(ys)], dsem=f"st32_{ys}")
                T32.release(ys)

        for i_, spec in enumerate(tile_specs):
            run_tile(spec, tile_specs[i_ + 1] if i_ + 1 < len(tile_specs) else None, i_ == 0)
        flush_stores()
        S.final_waits("sp", sorted(store_sems) + ["pe", "act", "dve", "pool"])

        block = es.enter_context(nc.Block())
        S.replay(block)
    return nc


def _tile_w(W, nk, nm):
    return np.ascontiguousarray(W.reshape(nk, 128, nm, 128).transpose(2, 1, 0, 3).reshape(nm, 128, nk * 128))


def _chunkcol(v):
    return np.ascontiguousarray(v.reshape(-1, 128).T)


def _prep_shared(inp):
    f = np.float32
    a_w_in = np.asarray(inp["a_w_in"][0], f)
    shared = {}
    shared["win_u"] = _tile_w(a_w_in[:, :DR], 8, NCH)
    shared["win_g"] = _tile_w(a_w_in[:, DR:], 8, NCH)
    gts = np.zeros((NCH, 128, 3, 2, 128), f)
    for g, key in enumerate(("a_gate_r_w", "a_gate_i_w")):
        Wg = np.asarray(inp[key][0], f)
        Dm = np.zeros((DR, DR), f)
        for n in range(16):
            Dm[88 * n:88 * n + 88, 88 * n:88 * n + 88] = Wg[n]
        for c in range(NCH):
            for dk in range(3):
                kc = c + dk - 1
                if 0 <= kc < NCH:
                    gts[c, :, dk, g, :] = Dm[kc * 128:(kc + 1) * 128, c * 128:(c + 1) * 128]
    shared["gts"] = gts.reshape(NCH, 128, 768)
    shared["wout"] = _tile_w(np.asarray(inp["a_w_out"][0], f), NCH, 8)
    kv_w = np.asarray(inp["kv_w"], f)
    shared["wk"] = _tile_w(kv_w[:, :D], 8, 8)
    Wv = kv_w[:, D:]
    wv = Wv.reshape(4, 2, 128, 2, 512).transpose(3, 0, 2, 1, 4)
    shared["wv"] = np.ascontiguousarray(wv.reshape(8, 128, 1024))
    b_w_in = np.asarray(inp["b_w_in"][0], f)
    shared["wq"] = _tile_w(b_w_in[:, :D], 8, 8)
    shared["wbg"] = _tile_w(b_w_in[:, D:], 8, 8)
    shared["wbo"] = _tile_w(np.asarray(inp["b_w_out"][0], f), 8, 8)
    pp = np.zeros((128, PP_N), f)
    pp[:, PP_AN:PP_AN + 8] = _chunkcol(np.asarray(inp["a_norm"][0], f))
    pp[:, PP_KVN:PP_KVN + 8] = _chunkcol(np.asarray(inp["kv_norm"], f))
    pp[:, PP_BN:PP_BN + 8] = _chunkcol(np.asarray(inp["b_norm"][0], f))
    cw = np.asarray(inp["a_conv_w"][0], f)
    pp[:, PP_CW:PP_CW + 44] = cw.reshape(4, NCH, 128).transpose(2, 1, 0).reshape(128, 44)
    pp[:, PP_CB:PP_CB + 11] = _chunkcol(np.asarray(inp["a_conv_b"][0], f))
    pp[:, PP_BR:PP_BR + 11] = _chunkcol(np.asarray(inp["a_gate_r_b"][0], f))
    pp[:, PP_BI:PP_BI + 11] = _chunkcol(np.asarray(inp["a_gate_i_b"][0], f))
    pp[:, PP_LAM:PP_LAM + 11] = _chunkcol(np.asarray(inp["a_lambda"][0], f))
    p = np.arange(128)
    kn = np.asarray(inp["k_norm"], f)
    qn = np.asarray(inp["b_q_norm"][0], f)
    pp[:, PP_GK] = kn[p % 64]
    pp[:, PP_GKS] = kn[(p ^ 32) % 64]
    pp[:, PP_GQ] = qn[p % 64]
    pp[:, PP_GQS] = qn[(p ^ 32) % 64]
    pp[:, PP_HG] = np.asarray(inp["b_head_norm"][0], f)
    pp[:, PP_LQ1:PP_LQ1 + 64] = np.asarray(inp["b_lambda_q1"][0], f)[None, :]
    pp[:, PP_LK1:PP_LK1 + 64] = np.asarray(inp["b_lambda_k1"][0], f)[None, :]
    pp[:, PP_LQ2:PP_LQ2 + 64] = np.asarray(inp["b_lambda_q2"][0], f)[None, :]
    pp[:, PP_LK2:PP_LK2 + 64] = np.asarray(inp["b_lambda_k2"][0], f)[None, :]
    shared["pp"] = pp
    cm = np.zeros((128, 384), f)
    cm[:, 0:128] = 1.0
    cm[:, 128:256] = (p[:, None] // 64 == p[None, :] // 64).astype(f)
    cm[:, 256:384] = (p[:, None] == (p[None, :] ^ 32)).astype(f)
    shared["cmat"] = cm
    half = 32
    inv = (np.float32(10000.0) ** (-np.arange(half, dtype=f) / np.float32(half))).astype(f)
    pos = np.concatenate([np.arange(SEQ), PAST + np.arange(NS)]).astype(f)
    ang = (pos[:, None] * inv[None, :]).astype(f)
    cos = np.cos(ang).astype(f).T
    sin = np.sin(ang).astype(f).T
    fi = p % 32
    sign = np.where((p % 64) < 32, -1.0, 1.0).astype(f)
    rope = np.empty((128, 2, SEQ + NS), f)
    rope[:, 0, :] = cos[fi]
    rope[:, 1, :] = sin[fi] * sign[:, None]
    shared["rope"] = rope
    return shared


_PROG = None


def kernel(**inp):
    global _PROG
    f = np.float32
    shared = _prep_shared(inp)
    x_prompt = np.asarray(inp["x_prompt"], f)
    x_sample = np.asarray(inp["x_sample"], f)
    state_conv = np.asarray(inp["state_conv"], f)
    state_h = np.asarray(inp["state_h"], f)
    cache_k = np.asarray(inp["cache_k"], f)
    cache_v = np.asarray(inp["cache_v"], f)
    in_maps = []
    for b in range(8):
        m = dict(shared)
        m["xT_p"] = np.ascontiguousarray(x_prompt[b].T)
        m["xT_s"] = np.ascontiguousarray(x_sample[b].T)
        m["conv_s_in"] = np.ascontiguousarray(state_conv[0, b].reshape(3, NCH, 128).transpose(2, 1, 0).reshape(128, NCH * 3))
        m["h_s_in"] = _chunkcol(state_h[0, b])
        m["ckT"] = np.ascontiguousarray(cache_k[b].transpose(1, 2, 0))
        m["cv"] = np.ascontiguousarray(cache_v[b].reshape(PAST, D))
        in_maps.append(m)
    if _PROG is None:
        _PROG = build_program()
    res = run_bass_kernel_spmd(_PROG, in_maps, core_ids=list(range(8)))
    R = res.results

    def unconv(a):
        return a.reshape(128, NCH, 3).transpose(2, 1, 0).reshape(3, DR)

    def unh(a):
        return a.T.reshape(DR)

    y_p = np.stack([R[b]["yT_p"].T for b in range(8)]).astype(f)
    y_s = np.stack([R[b]["yT_s"].T for b in range(8)]).astype(f)
    conv_p = np.stack([unconv(R[b]["conv_p_o"]) for b in range(8)])[None].astype(f)
    h_p = np.stack([unh(R[b]["h_p_o"]) for b in range(8)])[None].astype(f)
    k_p = np.stack([R[b]["kT_p"].transpose(2, 0, 1) for b in range(8)]).astype(f)
    v_p = np.stack([R[b]["v_p"].reshape(SEQ, NH, 128) for b in range(8)]).astype(f)
    conv_s = np.stack([unconv(R[b]["conv_s_o"]) for b in range(8)])[None].astype(f)
    h_s = np.stack([unh(R[b]["h_s_o"]) for b in range(8)])[None].astype(f)
    k_s = np.stack([R[b]["kT_s"].transpose(2, 0, 1) for b in range(8)]).astype(f)
    v_s = np.stack([R[b]["v_s"].reshape(NS, NH, 128) for b in range(8)]).astype(f)
    return (np.ascontiguousarray(y_p), np.ascontiguousarray(y_s), np.ascontiguousarray(conv_p), np.ascontiguousarray(h_p),
            np.ascontiguousarray(k_p), np.ascontiguousarray(v_p), np.ascontiguousarray(conv_s), np.ascontiguousarray(h_s),
            np.ascontiguousarray(k_s), np.ascontiguousarray(v_s))
```

```python
import math
from contextlib import ExitStack

import numpy as np
import concourse.bass as bass
import concourse.mybir as mybir
from concourse.bass_utils import run_bass_kernel_spmd

F32 = mybir.dt.float32
BF16 = mybir.dt.bfloat16
AF = mybir.ActivationFunctionType
ALU = mybir.AluOpType
AX = mybir.AxisListType

D = 1024
SEQ = 4096
NS = 16
PAST = 1024
DR = 1408
NCH = 11
NH = 8
EPS = 1e-6
LAM_INIT = 0.8 - 0.6 * math.exp(-0.3 * 1)
TN = 512
LEAD = 3
NT = SEQ // TN
RING = 8
K32 = 24
K16 = 4

PP_AN, PP_KVN, PP_BN = 0, 8, 16
PP_CW, PP_CB, PP_BR, PP_BI, PP_LAM = 24, 68, 79, 90, 101
PP_GK, PP_GKS, PP_GQ, PP_GQS, PP_HG = 112, 113, 114, 115, 116
PP_LQ1, PP_LK1, PP_LQ2, PP_LK2 = 117, 181, 245, 309
PP_N = 373


class Buf:
    __slots__ = ("name", "w", "r", "excl")

    def __init__(self, name, excl=False):
        self.name = name
        self.w = {}
        self.r = {}
        self.excl = excl


class Sched:
    CE = ("pe", "act", "dve", "pool")

    def __init__(self, nc, es):
        self.nc = nc
        self.es = es
        self.streams = {k: [] for k in ("pe", "act", "dve", "pool", "sp")}
        self.sems = {}
        self.cnt = {}
        self.waited = {k: {} for k in self.streams}
        for e in self.CE:
            self._sem(e)

    def _sem(self, name):
        if name not in self.sems:
            self.sems[name] = self.es.enter_context(self.nc.semaphore("s_" + name))
            self.cnt[name] = 0
        return self.sems[name]

    def emit(self, eng, fn, reads=(), writes=(), sig=True, dsem=None):
        need = {}

        def add(s, v, kind):
            if dsem is None and s == eng:
                if eng == "pe" or kind != "raw":
                    return
            if need.get(s, 0) < v:
                need[s] = v

        for b in reads:
            for s, v in b.w.items():
                add(s, v, "raw")
            if b.excl:
                for s, v in b.r.items():
                    add(s, v, "rar")
        for b in writes:
            for s, v in b.w.items():
                add(s, v, "waw")
            for s, v in b.r.items():
                add(s, v, "war")
        waits = []
        wd = self.waited[eng]
        for s, v in need.items():
            if wd.get(s, 0) < v:
                assert self.cnt[s] >= v, f"wait on future event {s}>={v} (cnt {self.cnt[s]}) from {eng}"
                wd[s] = v
                waits.append((s, v))
        if dsem is not None:
            self._sem(dsem)
            self.cnt[dsem] += 16
            ev = (dsem, self.cnt[dsem])
            sg = (dsem, 16)
        elif sig:
            self.cnt[eng] += 1
            ev = (eng, self.cnt[eng])
            sg = (eng, 1)
        else:
            ev = (eng, self.cnt[eng] + 1)
            sg = None
        self.streams[eng].append((waits, fn, sg))
        for b in writes:
            b.w = {ev[0]: ev[1]}
            b.r = {}
        for b in reads:
            if b.r.get(ev[0], 0) < ev[1]:
                b.r[ev[0]] = ev[1]
        return ev

    def final_waits(self, eng, names):
        waits = [(s, self.cnt[s]) for s in names if self.cnt.get(s, 0) > 0]
        self.streams[eng].append((waits, None, None))

    def replay(self, block):
        def run(stream):
            def body(e):
                for waits, fn, sg in stream:
                    for s, v in waits:
                        e.wait_ge(self.sems[s], v)
                    if fn is None:
                        continue
                    ins = fn(e)
                    if sg is not None:
                        ins.then_inc(self.sems[sg[0]], sg[1])
            return body

        block.tensor(run(self.streams["pe"]))
        block.scalar(run(self.streams["act"]))
        block.vector(run(self.streams["dve"]))
        block.gpsimd(run(self.streams["pool"]))
        block.sync(run(self.streams["sp"]))


class Pool32:
    def __init__(self, aps, name):
        self.items = [(ap, Buf(f"{name}{i}")) for i, ap in enumerate(aps)]
        self.free = list(range(len(aps)))

    def alloc(self):
        assert self.free, "temp pool exhausted"
        i = self.free.pop(0)
        return i

    def ap(self, i):
        return self.items[i][0]

    def buf(self, i):
        return self.items[i][1]

    def release(self, i):
        self.free.append(i)


def build_program():
    nc = bass.Bass("TRN2", target_bir_lowering=False)

    def din(name, shape, dt=F32):
        return nc.dram_tensor(name, list(shape), dt, kind="ExternalInput").ap()

    def dout(name, shape, dt=F32):
        return nc.dram_tensor(name, list(shape), dt, kind="ExternalOutput").ap()

    def dint(name, shape, dt=BF16):
        return nc.dram_tensor(name, list(shape), dt, kind="Internal").ap()

    xT_p = din("xT_p", [D, SEQ])
    xT_s = din("xT_s", [D, NS])
    conv_s_in = din("conv_s_in", [128, NCH * 3])
    h_s_in = din("h_s_in", [128, NCH])
    ckT = din("ckT", [NH, 128, PAST])
    cv = din("cv", [PAST, D])
    wnames = [("win_u", NCH, 1024), ("win_g", NCH, 1024), ("gts", NCH, 768), ("wout", 8, 1408),
              ("wk", 8, 1024), ("wv", 8, 1024), ("wq", 8, 1024), ("wbg", 8, 1024), ("wbo", 8, 1024)]
    wf32 = {n: din(n, [ns, 128, w]) for n, ns, w in wnames}
    wb16 = {n: dint(n + "_b", [ns, 128, w]) for n, ns, w in wnames}
    wwidth = {n: w for n, ns, w in wnames}
    pp_in = din("pp", [128, PP_N])
    cmat_in = din("cmat", [128, 384])
    rope_in = din("rope", [128, 2, SEQ + NS])

    yT_p = dout("yT_p", [D, SEQ])
    yT_s = dout("yT_s", [D, NS])
    conv_p_o = dout("conv_p_o", [128, NCH * 3])
    h_p_o = dout("h_p_o", [128, NCH])
    kT_p = dout("kT_p", [NH, 128, SEQ])
    v_p = dout("v_p", [SEQ, D])
    conv_s_o = dout("conv_s_o", [128, NCH * 3])
    h_s_o = dout("h_s_o", [128, NCH])
    kT_s = dout("kT_s", [NH, 128, NS])
    v_s = dout("v_s", [NS, D])

    kscr = dint("kscr", [NH, 128, SEQ])
    vscr = dint("vscr", [NH, 128, SEQ // 128, 128])

    with ExitStack() as es:
        S = Sched(nc, es)

        def sb(name, shape, dt):
            return es.enter_context(nc.sbuf_tensor(name, list(shape), dt))

        xin = sb("xin", [128, 8, TN], F32)
        x1 = sb("x1", [128, 8, TN], F32)
        xn = sb("xn", [128, 8, TN], BF16)
        ring = sb("ring", [128, RING, 1408], BF16)
        kTh = sb("kTh", [128, 2, SEQ], BF16)
        vh = sb("vh", [128, 2, SEQ // 128, 128], BF16)
        pp = sb("pp_sb", [128, PP_N], F32)
        cmf = sb("cmf", [128, 384], F32)
        cmb = sb("cmb", [128, 384], BF16)
        xn2 = sb("xn2", [128, 8, TN], BF16)
        rope = sb("rope_sb", [128, 2, 2, TN], F32)
        t32 = sb("t32", [128, K32, TN + 4], F32)
        b16 = sb("b16", [128, 22, TN], BF16)
        on = sb("on", [128, 8, TN], BF16)
        pt = sb("pt", [128, 3, 2, TN], BF16)
        t16 = sb("t16", [128, K16, TN], BF16)
        small = sb("small", [128, 64], F32)
        lprod = sb("lprod", [128, 128], F32)
        ucarry_p = sb("ucarry_p", [128, NCH, 3], F32)
        hcarry_p = sb("hcarry_p", [128, NCH], F32)
        ucarry_s = sb("ucarry_s", [128, NCH, 3], F32)
        hcarry_s = sb("hcarry_s", [128, NCH], F32)
        psum = [es.enter_context(nc.psum_tensor(f"ps{i}", [128, 2, 512], F32)) for i in range(4)]

        B = {n: Buf(n) for n in ("xin", "x1", "xn", "xn2", "pp", "cmf", "cmb", "consts", "small", "lprod", "on",
                                 "ucarry_p", "hcarry_p", "ucarry_s", "hcarry_s", "kscr", "vscr",
                                 "conv_p_o", "h_p_o", "conv_s_o", "h_s_o")}
        Bring = [Buf(f"ring{i}") for i in range(RING)]
        BkTh = [Buf("kTh0"), Buf("kTh1")]
        Bvh = [Buf("vh0"), Buf("vh1")]
        Brope = [Buf("rope0"), Buf("rope1")]
        Bb16 = [Buf(f"b16_{i}") for i in range(22)]
        Bpt = [Buf(f"pt{i}") for i in range(3)]
        Bps = [Buf(f"psb{i}", excl=True) for i in range(8)]
        Bon = [Buf(f"on{i}") for i in range(8)]
        Bxn = [Buf(f"xn{i}") for i in range(8)]
        Bxn2 = [Buf(f"xn2_{i}") for i in range(8)]
        Bw = {n: Buf("w_" + n) for n, _, _ in wnames}
        T32 = Pool32([t32[:, i, :] for i in range(K32)], "t32_")
        T16 = Pool32([t16[:, i, :] for i in range(K16)], "t16_")
        store_sems = set()

        def psb(i):
            return psum[i // 2][:, i % 2, :]

        bank_rr = [0]
        bank_held = set()

        def bank():
            while True:
                i = bank_rr[0]
                bank_rr[0] = (i + 1) % 8
                if i not in bank_held:
                    return i

        ones_b = cmb[:, 0:128]
        blk_b = cmb[:, 128:256]
        perm_b = cmb[:, 256:384]

        def ppc(i):
            return pp[:, i:i + 1]

        def smc(i):
            return small[:, i:i + 1]
        SM_HBR, SM_HBI, SM_HC, SM_NLAM, SM_HGS = 0, 11, 22, 33, 34
        epsc = small[:, 62:63]
        onec = small[:, 63:64]

        S.emit("sp", lambda e: e.dma_start(out=pp[:], in_=pp_in), writes=[B["pp"]], dsem="ld_pp")
        S.emit("sp", lambda e: e.dma_start(out=cmf[:], in_=cmat_in), writes=[B["cmf"]], dsem="ld_cm")
        S.emit("sp", lambda e: e.dma_start(out=xin[:, :, 0:TN], in_=xT_p.rearrange("(kc p) n -> p kc n", p=128)[:, :, 0:TN]),
               writes=[B["xin"]], dsem="ld_x")
        for n, ns, w in wnames:
            S.emit("pool", lambda e, n=n: e.dma_start(out=wb16[n], in_=wf32[n]), reads=[B["xin"], B["pp"], B["cmf"]],
                   writes=[Bw[n]], dsem="cv_" + n)
        S.emit("pool", lambda e: e.tensor_copy(out=cmb[:], in_=cmf[:]), reads=[B["cmf"]], writes=[B["cmb"]])
        S.emit("pool", lambda e: e.memset(ucarry_p[:], 0.0), writes=[B["ucarry_p"]])
        S.emit("pool", lambda e: e.memset(hcarry_p[:], 0.0), writes=[B["hcarry_p"]])
        S.emit("sp", lambda e: e.dma_start(out=ucarry_s[:].rearrange("p c j -> p (c j)"), in_=conv_s_in),
               writes=[B["ucarry_s"]], dsem="ld_cs")
        S.emit("sp", lambda e: e.dma_start(out=hcarry_s[:], in_=h_s_in), writes=[B["hcarry_s"]], dsem="ld_hs")
        Bsm = B["small"]
        S.emit("pool", lambda e: e.memset(small[:, 62:63], EPS), writes=[Bsm])
        S.emit("pool", lambda e: e.memset(small[:, 63:64], 1.0), writes=[Bsm])
        S.emit("dve", lambda e: e.tensor_scalar(out=small[:, SM_HBR:SM_HBR + 11], in0=pp[:, PP_BR:PP_BR + 11],
                                                scalar1=-1.0, scalar2=None, op0=ALU.mult),
               reads=[B["pp"]], writes=[Bsm])
        S.emit("dve", lambda e: e.tensor_scalar(out=small[:, SM_HBI:SM_HBI + 11], in0=pp[:, PP_BI:PP_BI + 11],
                                                scalar1=-1.0, scalar2=None, op0=ALU.mult),
               reads=[B["pp"]], writes=[Bsm])
        S.emit("act", lambda e: e.activation(out=small[:, 35:46], in_=pp[:, PP_LAM:PP_LAM + 11], func=AF.Abs),
               reads=[B["pp"]], writes=[Bsm])
        S.emit("act", lambda e: e.activation(out=small[:, 35:46], in_=small[:, 35:46], func=AF.Exp, scale=-1.0),
               reads=[Bsm], writes=[Bsm])
        S.emit("act", lambda e: e.activation(out=small[:, 35:46], in_=small[:, 35:46], func=AF.Ln, bias=onec, scale=1.0),
               reads=[Bsm], writes=[Bsm])
        S.emit("dve", lambda e: e.tensor_scalar(out=small[:, 46:57], in0=pp[:, PP_LAM:PP_LAM + 11],
                                                scalar1=-1.0, scalar2=0.0, op0=ALU.mult, op1=ALU.max),
               reads=[B["pp"]], writes=[Bsm])
        S.emit("dve", lambda e: e.tensor_tensor(out=small[:, 46:57], in0=small[:, 46:57], in1=small[:, 35:46], op=ALU.add),
               reads=[Bsm], writes=[Bsm])
        S.emit("dve", lambda e: e.tensor_scalar(out=small[:, SM_HC:SM_HC + 11], in0=small[:, 46:57],
                                                scalar1=-8.0, scalar2=None, op0=ALU.mult),
               reads=[Bsm], writes=[Bsm])
        S.emit("dve", lambda e: e.tensor_tensor(out=lprod[:, 0:64], in0=pp[:, PP_LQ1:PP_LQ1 + 64],
                                                in1=pp[:, PP_LK1:PP_LK1 + 64], op=ALU.mult),
               reads=[B["pp"]], writes=[B["lprod"]])
        S.emit("dve", lambda e: e.tensor_tensor(out=lprod[:, 64:128], in0=pp[:, PP_LQ2:PP_LQ2 + 64],
                                                in1=pp[:, PP_LK2:PP_LK2 + 64], op=ALU.mult),
               reads=[B["pp"]], writes=[B["lprod"]])
        S.emit("dve", lambda e: e.reduce_sum(out=small[:, 57:58], in_=lprod[:, 0:64], axis=AX.X),
               reads=[B["lprod"]], writes=[Bsm])
        S.emit("dve", lambda e: e.reduce_sum(out=small[:, 58:59], in_=lprod[:, 64:128], axis=AX.X),
               reads=[B["lprod"]], writes=[Bsm])
        S.emit("act", lambda e: e.activation(out=small[:, 59:61], in_=small[:, 57:59], func=AF.Exp),
               reads=[Bsm], writes=[Bsm])
        S.emit("dve", lambda e: e.tensor_tensor(out=small[:, 61:62], in0=small[:, 60:61], in1=small[:, 59:60], op=ALU.subtract),
               reads=[Bsm], writes=[Bsm])
        S.emit("dve", lambda e: e.tensor_scalar(out=small[:, SM_NLAM:SM_NLAM + 1], in0=small[:, 61:62],
                                                scalar1=-LAM_INIT, scalar2=None, op0=ALU.add),
               reads=[Bsm], writes=[Bsm])
        S.emit("dve", lambda e: e.tensor_scalar(out=small[:, SM_HGS:SM_HGS + 1], in0=pp[:, PP_HG:PP_HG + 1],
                                                scalar1=(1.0 - LAM_INIT), scalar2=None, op0=ALU.mult),
               reads=[B["pp"]], writes=[Bsm])

        ring_i = [0]

        def load_slab(name, idx):
            slot = ring_i[0] % RING
            ring_i[0] += 1
            w = wwidth[name]
            S.emit("sp", lambda e: e.dma_start(out=ring[:, slot, 0:w], in_=wb16[name][idx]),
                   reads=[Bw[name]], writes=[Bring[slot]], dsem=f"rg{slot}")
            return slot

        def a_slabs(k0, k1):
            L = []
            for k in range(k0, k1):
                if k < NCH:
                    L.append(("win_u", k))
                if 0 <= k - 2 < NCH:
                    L.append(("win_g", k - 2))
                    L.append(("gts", k - 2))
                if k - 3 == NCH - 3:
                    for m in range(4):
                        L.append(("wout", m))
            return L

        import os as _os
        _lim = int(_os.environ.get("KTILES", str(NT)))
        tile_specs = list(range(_lim)) + (["s"] if _os.environ.get("KNOS", "0") == "0" else [])
        HOIST_ = 3
        slab_seq = []
        for i_, _ in enumerate(tile_specs):
            hoisted_in = (i_ > 0)
            slab_seq += a_slabs(HOIST_ if hoisted_in else 0, NCH + 3)
            slab_seq += [("wout", m) for m in range(4, 8)]
            slab_seq += [("wk", h) for h in range(8)] + [("wv", x) for x in range(8)]
            slab_seq += [("wq", h) for h in range(8)] + [("wbg", m) for m in range(8)]
            if i_ + 1 < len(tile_specs):
                slab_seq += a_slabs(0, HOIST_)
            slab_seq += [("wbo", m) for m in range(8)]
        slab_pos = [0]
        slab_loaded = [0]
        slab_slots = {}

        slab_pin = [None]

        def prefetch():
            base = slab_pos[0] if slab_pin[0] is None else min(slab_pos[0], slab_pin[0])
            while slab_loaded[0] < len(slab_seq) and slab_loaded[0] < base + RING - 1:
                n, i = slab_seq[slab_loaded[0]]
                slab_slots[slab_loaded[0]] = load_slab(n, i)
                slab_loaded[0] += 1

        def next_slab(name, idx):
            p = slab_pos[0]
            assert slab_seq[p] == (name, idx), (slab_seq[p], name, idx)
            if p >= slab_loaded[0]:
                prefetch_force(p)
            slot = slab_slots.pop(p)
            slab_pos[0] += 1
            return slot

        def prefetch_force(p):
            while slab_loaded[0] <= p:
                n, i = slab_seq[slab_loaded[0]]
                slab_slots[slab_loaded[0]] = load_slab(n, i)
                slab_loaded[0] += 1

        deferred = []

        def flush_stores():
            for f in deferred:
                f()
            deferred.clear()

        def store(out_ap, in_ap, rbufs, sem, wbufs=()):
            store_sems.add(sem)

            def f():
                S.emit("sp", lambda e: e.dma_start(out=out_ap, in_=in_ap), reads=rbufs, writes=list(wbufs), dsem=sem)
            deferred.append(f)

        def norm_square(src, bsrc, kc, N):
            S.emit("dve", lambda e: e.tensor_tensor(out=on[:, kc, 0:N], in0=src[:, kc, 0:N], in1=src[:, kc, 0:N], op=ALU.mult),
                   reads=[bsrc], writes=[Bon[kc]])

        def norm_sum_mm(bk, kc, N):
            S.emit("pe", lambda e: e.matmul(psb(bk)[:, 0:N], lhsT=ones_b, rhs=on[:, kc, 0:N], start=(kc == 0), stop=(kc == 7)),
                   reads=[Bon[kc], B["cmb"]], writes=[Bps[bk]], sig=True)

        def norm_rstd(bk, N):
            t = T32.alloc()
            S.emit("act", lambda e: e.activation(out=T32.ap(t)[:, 0:N], in_=psb(bk)[:, 0:N], func=AF.Ln, scale=1.0 / D, bias=epsc),
                   reads=[Bps[bk], Bsm], writes=[T32.buf(t)])
            S.emit("act", lambda e: e.activation(out=T32.ap(t)[:, 0:N], in_=T32.ap(t)[:, 0:N], func=AF.Exp, scale=-0.5),
                   reads=[T32.buf(t)], writes=[T32.buf(t)])
            return t

        def norm_apply(src, bsrc, gcol, t, dst, bdst, N):
            for kc in range(8):
                S.emit("dve", lambda e, kc=kc: e.scalar_tensor_tensor(out=dst[:, kc, 0:N], in0=src[:, kc, 0:N],
                                                                      scalar=ppc(gcol + kc), in1=T32.ap(t)[:, 0:N],
                                                                      op0=ALU.mult, op1=ALU.mult),
                       reads=[bsrc, T32.buf(t), B["pp"]], writes=[bdst[kc]])

        def tile_params(spec):
            is_s = spec == "s"
            return dict(N=NS if is_s else TN, tok0=0 if is_s else spec * TN, xsrc=xT_s if is_s else xT_p)

        def load_x(spec):
            p_ = tile_params(spec)
            S.emit("sp", lambda e: e.dma_start(out=xin[:, :, 0:p_["N"]],
                                               in_=p_["xsrc"].rearrange("(kc p) n -> p kc n", p=128)[:, :, p_["tok0"]:p_["tok0"] + p_["N"]]),
                   writes=[B["xin"]], dsem="ld_x")

        def a_norm(spec):
            N = tile_params(spec)["N"]
            for kc in range(8):
                norm_square(xin, B["xin"], kc, N)
            bk = bank()
            for kc in range(8):
                norm_sum_mm(bk, kc, N)
            t = norm_rstd(bk, N)
            norm_apply(xin, B["xin"], PP_AN, t, xn, Bxn, N)
            T32.release(t)

        def proj_fm(slot, nk, rhs_of, rbufs, N, coff=0):
            bk = bank()
            for k in range(nk):
                S.emit("pe", lambda e, k=k: e.matmul(psb(bk)[:, 0:N], lhsT=ring[:, slot, coff + k * 128: coff + (k + 1) * 128],
                                                      rhs=rhs_of(k), start=(k == 0), stop=(k == nk - 1)),
                       reads=[Bring[slot]] + (rbufs(k) if callable(rbufs) else rbufs), writes=[Bps[bk]], sig=(k == nk - 1))
            return bk

        def qk_stage1(kind, h, N, xsrc_ap, bxsrc):
            slot = next_slab("wk" if kind == "k" else "wq", h)
            prefetch()
            bk = proj_fm(slot, 8, lambda k: xsrc_ap[:, k, 0:N], (lambda k: [bxsrc[k]]), N)
            kb = T16.alloc()
            sq = T16.alloc()
            S.emit("act", lambda e: e.activation(out=T16.ap(kb)[:, 0:N], in_=psb(bk)[:, 0:N], func=AF.Copy),
                   reads=[Bps[bk]], writes=[T16.buf(kb)])
            S.emit("act", lambda e: e.activation(out=T16.ap(sq)[:, 0:N], in_=psb(bk)[:, 0:N], func=AF.Square),
                   reads=[Bps[bk]], writes=[T16.buf(sq)])
            return (bk, kb, sq)

        def qk_stage2(kind, st, N, ropei, dst_fn):
            bk, kb, sq = st
            bsw = bank()
            S.emit("pe", lambda e: e.matmul(psb(bsw)[:, 0:N], lhsT=perm_b, rhs=T16.ap(kb)[:, 0:N], start=True, stop=True),
                   reads=[T16.buf(kb), B["cmb"]], writes=[Bps[bsw]])
            bss = bank()
            S.emit("pe", lambda e: e.matmul(psb(bss)[:, 0:N], lhsT=blk_b, rhs=T16.ap(sq)[:, 0:N], start=True, stop=True),
                   reads=[T16.buf(sq), B["cmb"]], writes=[Bps[bss]])
            T16.release(kb)
            T16.release(sq)
            rk = T32.alloc()
            S.emit("act", lambda e: e.activation(out=T32.ap(rk)[:, 0:N], in_=psb(bss)[:, 0:N], func=AF.Ln, scale=1.0 / 64, bias=epsc),
                   reads=[Bps[bss], Bsm], writes=[T32.buf(rk)])
            S.emit("act", lambda e: e.activation(out=T32.ap(rk)[:, 0:N], in_=T32.ap(rk)[:, 0:N], func=AF.Exp, scale=-0.5),
                   reads=[T32.buf(rk)], writes=[T32.buf(rk)])
            g0, g1 = (PP_GK, PP_GKS) if kind == "k" else (PP_GQ, PP_GQS)
            u1 = T32.alloc()
            u2 = T32.alloc()
            S.emit("dve", lambda e: e.scalar_tensor_tensor(out=T32.ap(u1)[:, 0:N], in0=psb(bk)[:, 0:N], scalar=ppc(g0),
                                                           in1=rope[:, ropei, 0, 0:N], op0=ALU.mult, op1=ALU.mult),
                   reads=[Bps[bk], Brope[ropei], B["pp"]], writes=[T32.buf(u1)])
            S.emit("dve", lambda e: e.scalar_tensor_tensor(out=T32.ap(u2)[:, 0:N], in0=psb(bsw)[:, 0:N], scalar=ppc(g1),
                                                           in1=rope[:, ropei, 1, 0:N], op0=ALU.mult, op1=ALU.mult),
                   reads=[Bps[bsw], Brope[ropei], B["pp"]], writes=[T32.buf(u2)])
            S.emit("dve", lambda e: e.tensor_tensor(out=T32.ap(u1)[:, 0:N], in0=T32.ap(u1)[:, 0:N], in1=T32.ap(u2)[:, 0:N], op=ALU.add),
                   reads=[T32.buf(u1), T32.buf(u2)], writes=[T32.buf(u1)])
            T32.release(u2)
            dst_fn(u1, rk)
            T32.release(u1)
            T32.release(rk)

        def qk_heads(kind, N, ropei, xsrc_ap, bxsrc, dst_of):
            st = qk_stage1(kind, 0, N, xsrc_ap, bxsrc)
            for h in range(NH):
                nxt = qk_stage1(kind, h + 1, N, xsrc_ap, bxsrc) if h + 1 < NH else None
                qk_stage2(kind, st, N, ropei, dst_of(h))
                st = nxt

        def make_aphase(spec):
            is_s = spec == "s"
            N = NS if is_s else TN
            tix = 0 if is_s else spec
            ucarry, hcarry = (ucarry_s, hcarry_s) if is_s else (ucarry_p, hcarry_p)
            Buc, Bhc = (B["ucarry_s"], B["hcarry_s"]) if is_s else (B["ucarry_p"], B["hcarry_p"])
            ucf = {}

            def sigmoid_from_psum(tb_, bk_):
                S.emit("act", lambda e: e.activation(out=T32.ap(tb_)[:, 0:N], in_=psb(bk_)[:, 0:N], func=AF.Exp, scale=-1.0),
                       reads=[Bps[bk_]], writes=[T32.buf(tb_)])
                S.emit("act", lambda e: e.activation(out=T32.ap(tb_)[:, 0:N], in_=T32.ap(tb_)[:, 0:N], func=AF.Ln, scale=1.0, bias=onec),
                       reads=[T32.buf(tb_), Bsm], writes=[T32.buf(tb_)])
                S.emit("act", lambda e: e.activation(out=T32.ap(tb_)[:, 0:N], in_=T32.ap(tb_)[:, 0:N], func=AF.Exp, scale=-1.0),
                       reads=[T32.buf(tb_)], writes=[T32.buf(tb_)])

            def stage_u(c):
                slot = next_slab("win_u", c)
                prefetch()
                bk = proj_fm(slot, 8, lambda k: xn[:, k, 0:N], (lambda k: [Bxn[k]]), N)
                ur = T32.alloc()
                uf = T32.alloc()
                S.emit("dve", lambda e: e.tensor_copy(out=T32.ap(ur)[:, 0:3], in_=ucarry[:, c, :]),
                       reads=[Buc], writes=[T32.buf(ur)])
                S.emit("dve", lambda e: e.tensor_copy(out=T32.ap(ur)[:, 3:3 + N], in_=psb(bk)[:, 0:N]),
                       reads=[Bps[bk]], writes=[T32.buf(ur)])
                S.emit("act", lambda e: e.activation(out=T32.ap(uf)[:, 0:N], in_=T32.ap(ur)[:, 3:3 + N], func=AF.Identity,
                                                     scale=ppc(PP_CW + 4 * c + 3), bias=ppc(PP_CB + c)),
                       reads=[T32.buf(ur), B["pp"]], writes=[T32.buf(uf)])
                for j in (2, 1, 0):
                    S.emit("dve", lambda e, j=j: e.scalar_tensor_tensor(out=T32.ap(uf)[:, 0:N], in0=T32.ap(ur)[:, j:j + N],
                                                                        scalar=ppc(PP_CW + 4 * c + j), in1=T32.ap(uf)[:, 0:N],
                                                                        op0=ALU.mult, op1=ALU.add),
                           reads=[T32.buf(ur), T32.buf(uf), B["pp"]], writes=[T32.buf(uf)])
                S.emit("dve", lambda e: e.tensor_copy(out=ucarry[:, c, :], in_=T32.ap(ur)[:, N:N + 3]),
                       reads=[T32.buf(ur)], writes=[Buc])
                S.emit("pool", lambda e: e.tensor_copy(out=b16[:, c, 0:N], in_=T32.ap(uf)[:, 0:N]),
                       reads=[T32.buf(uf)], writes=[Bb16[c]])
                T32.release(ur)
                ucf[c] = uf

            gst = {}

            def g1(c):
                slot2 = next_slab("win_g", c)
                prefetch()
                bkg = proj_fm(slot2, 8, lambda k: xn[:, k, 0:N], (lambda k: [Bxn[k]]), N)
                tg = T32.alloc()
                sigmoid_from_psum(tg, bkg)
                S.emit("dve", lambda e: e.tensor_tensor(out=T32.ap(tg)[:, 0:N], in0=T32.ap(tg)[:, 0:N], in1=psb(bkg)[:, 0:N], op=ALU.mult),
                       reads=[T32.buf(tg), Bps[bkg]], writes=[T32.buf(tg)])
                slot = next_slab("gts", c)
                prefetch()
                dks = [dk for dk in range(3) if 0 <= c + dk - 1 < NCH]
                bkr = bank()
                bki = bank()
                for g, bk in ((0, bkr), (1, bki)):
                    for n_, dk in enumerate(dks):
                        S.emit("pe", lambda e, g=g, bk=bk, dk=dk, n_=n_: e.matmul(
                            psb(bk)[:, 0:N], lhsT=ring[:, slot, (dk * 2 + g) * 128:(dk * 2 + g + 1) * 128],
                            rhs=b16[:, c + dk - 1, 0:N], start=(n_ == 0), stop=(n_ == len(dks) - 1)),
                            reads=[Bring[slot], Bb16[c + dk - 1]], writes=[Bps[bk]], sig=(n_ == len(dks) - 1))
                tr = T32.alloc()
                ti = T32.alloc()
                a = T32.alloc()
                m = T32.alloc()
                S.emit("act", lambda e: e.activation(out=T32.ap(tr)[:, 0:N], in_=psb(bkr)[:, 0:N], func=AF.Exp,
                                                     scale=-1.0, bias=smc(SM_HBR + c)),
                       reads=[Bps[bkr], Bsm], writes=[T32.buf(tr)])
                S.emit("act", lambda e: e.activation(out=T32.ap(ti)[:, 0:N], in_=psb(bki)[:, 0:N], func=AF.Exp,
                                                     scale=-1.0, bias=smc(SM_HBI + c)),
                       reads=[Bps[bki], Bsm], writes=[T32.buf(ti)])
                S.emit("act", lambda e: e.activation(out=T32.ap(tr)[:, 0:N], in_=T32.ap(tr)[:, 0:N], func=AF.Ln, scale=1.0, bias=onec),
                       reads=[T32.buf(tr), Bsm], writes=[T32.buf(tr)])
                S.emit("act", lambda e: e.activation(out=T32.ap(ti)[:, 0:N], in_=T32.ap(ti)[:, 0:N], func=AF.Ln, scale=1.0, bias=onec),
                       reads=[T32.buf(ti), Bsm], writes=[T32.buf(ti)])
                S.emit("act", lambda e: e.activation(out=T32.ap(tr)[:, 0:N], in_=T32.ap(tr)[:, 0:N], func=AF.Exp, scale=-1.0),
                       reads=[T32.buf(tr)], writes=[T32.buf(tr)])
                S.emit("act", lambda e: e.activation(out=T32.ap(ti)[:, 0:N], in_=T32.ap(ti)[:, 0:N], func=AF.Exp, scale=-1.0),
                       reads=[T32.buf(ti)], writes=[T32.buf(ti)])
                S.emit("act", lambda e: e.activation(out=T32.ap(a)[:, 0:N], in_=T32.ap(tr)[:, 0:N], func=AF.Exp, scale=smc(SM_HC + c)),
                       reads=[T32.buf(tr), Bsm], writes=[T32.buf(a)])
                T32.release(tr)
                S.emit("pool", lambda e: e.tensor_tensor(out=T32.ap(m)[:, 0:N], in0=T32.ap(a)[:, 0:N], in1=T32.ap(a)[:, 0:N], op=ALU.mult),
                       reads=[T32.buf(a)], writes=[T32.buf(m)])
                gst[c] = (tg, ti, a, m)

            def g2_act(c):
                tg, ti, a, m = gst[c]
                S.emit("act", lambda e: e.activation(out=T32.ap(m)[:, 0:N], in_=T32.ap(m)[:, 0:N], func=AF.Ln, scale=-1.0, bias=onec),
                       reads=[T32.buf(m), Bsm], writes=[T32.buf(m)])
                S.emit("act", lambda e: e.activation(out=T32.ap(m)[:, 0:N], in_=T32.ap(m)[:, 0:N], func=AF.Exp, scale=0.5),
                       reads=[T32.buf(m)], writes=[T32.buf(m)])
                if (not is_s) and tix == 0:
                    S.emit("pool", lambda e: e.memset(T32.ap(m)[:, 0:1], 1.0), writes=[T32.buf(m)])

            def g2_dve1(c):
                tg, ti, a, m = gst[c]
                uf = ucf.pop(c)
                S.emit("dve", lambda e: e.tensor_tensor(out=T32.ap(ti)[:, 0:N], in0=T32.ap(ti)[:, 0:N], in1=T32.ap(uf)[:, 0:N], op=ALU.mult),
                       reads=[T32.buf(ti), T32.buf(uf)], writes=[T32.buf(ti)])
                T32.release(uf)
                S.emit("dve", lambda e: e.tensor_tensor(out=T32.ap(ti)[:, 0:N], in0=T32.ap(ti)[:, 0:N], in1=T32.ap(m)[:, 0:N], op=ALU.mult),
                       reads=[T32.buf(ti), T32.buf(m)], writes=[T32.buf(ti)])
                T32.release(m)
                hh = T32.alloc()
                S.emit("dve", lambda e: e.tensor_tensor_scan(out=T32.ap(hh)[:, 0:N], data0=T32.ap(a)[:, 0:N], data1=T32.ap(ti)[:, 0:N],
                                                             initial=hcarry[:, c:c + 1], op0=ALU.mult, op1=ALU.add),
                       reads=[T32.buf(a), T32.buf(ti), Bhc], writes=[T32.buf(hh)])
                T32.release(a)
                T32.release(ti)
                S.emit("dve", lambda e: e.tensor_copy(out=hcarry[:, c:c + 1], in_=T32.ap(hh)[:, N - 1:N]),
                       reads=[T32.buf(hh)], writes=[Bhc])
                gst[c] = (tg, hh)

            def g2_dve2(c):
                tg, hh = gst.pop(c)
                S.emit("dve", lambda e: e.tensor_tensor(out=b16[:, 11 + c, 0:N], in0=T32.ap(hh)[:, 0:N], in1=T32.ap(tg)[:, 0:N], op=ALU.mult),
                       reads=[T32.buf(hh), T32.buf(tg)], writes=[Bb16[11 + c]])
                T32.release(hh)
                T32.release(tg)

            aout = {}

            def aout_part(cs):
                if "slots" not in aout:
                    slab_pin[0] = slab_pos[0]
                    aout["slots"] = [next_slab("wout", j) for j in range(4)]
                    prefetch()
                    aout["banks"] = [bank() for _ in range(4)]
                    bank_held.update(aout["banks"])
                for c in cs:
                    for j in range(4):
                        S.emit("pe", lambda e, c=c, j=j: e.matmul(psb(aout["banks"][j])[:, 0:N],
                                                                  lhsT=ring[:, aout["slots"][j], c * 128:(c + 1) * 128],
                                                                  rhs=b16[:, 11 + c, 0:N], start=(c == 0), stop=(c == NCH - 1)),
                               reads=[Bring[aout["slots"][j]], Bb16[11 + c]], writes=[Bps[aout["banks"][j]]], sig=(j == 3))

            def run_k(k):
                c1, c2 = k - 2, k - 3
                if 0 <= c2 < NCH:
                    g2_act(c2)
                if k < NCH:
                    stage_u(k)
                if 0 <= c2 < NCH:
                    g2_dve1(c2)
                if 0 <= c1 < NCH:
                    g1(c1)
                if 0 <= c2 < NCH:
                    g2_dve2(c2)
                if c2 == NCH - 3:
                    aout_part(range(0, NCH - 2))
                elif c2 == NCH - 2:
                    aout_part([NCH - 2])
                elif c2 == NCH - 1:
                    aout_part([NCH - 1])
                    slab_pin[0] = None

            return {"run_k": run_k, "aout": aout, "sigmoid": sigmoid_from_psum, "k_done": 0}

        aph_pending = {}
        HOIST = 3

        def run_tile(spec, next_spec, first):
            is_s = spec == "s"
            N = NS if is_s else TN
            tix = 0 if is_s else spec
            tok0 = 0 if is_s else spec * TN
            xsrc = xT_s if is_s else xT_p
            ysrc = yT_s if is_s else yT_p
            kTo = kT_s if is_s else kT_p
            vo = v_s if is_s else v_p
            ucarry, hcarry = (ucarry_s, hcarry_s) if is_s else (ucarry_p, hcarry_p)
            Buc, Bhc = (B["ucarry_s"], B["hcarry_s"]) if is_s else (B["ucarry_p"], B["hcarry_p"])
            rope_off = SEQ if is_s else tok0
            ropei = (0 if is_s else (spec + 1)) % 2
            NB = (N + 127) // 128

            S.emit("sp", lambda e: e.dma_start(out=rope[:, ropei, :, 0:N], in_=rope_in[:, :, rope_off:rope_off + N]),
                   writes=[Brope[ropei]], dsem=f"ld_rope{ropei}")
            prefetch()

            if first:
                a_norm(spec)
            aph = aph_pending.pop(spec, None) or make_aphase(spec)
            aout = aph["aout"]
            sigmoid_from_psum = aph["sigmoid"]
            for k in range(aph["k_done"], NCH + 3):
                aph["run_k"](k)

            bkn = bank()
            bank_held.add(bkn)
            for m_ in range(8):
                if m_ < 4:
                    bk = aout["banks"][m_]
                else:
                    slot = next_slab("wout", m_)
                    prefetch()
                    bk = proj_fm(slot, NCH, lambda k: b16[:, 11 + k, 0:N], [Bb16[11 + k] for k in range(NCH)], N)
                if m_ == 4:
                    for b_ in aout["banks"]:
                        bank_held.discard(b_)
                if m_ >= 1:
                    norm_sum_mm(bkn, m_ - 1, N)
                S.emit("dve", lambda e, m_=m_, bk=bk: e.tensor_tensor(out=x1[:, m_, 0:N], in0=xin[:, m_, 0:N], in1=psb(bk)[:, 0:N], op=ALU.add),
                       reads=[B["xin"], Bps[bk]], writes=[B["x1"]])
                norm_square(x1, B["x1"], m_, N)
            norm_sum_mm(bkn, 7, N)
            bank_held.discard(bkn)
            t_x1 = norm_rstd(bkn, N)
            if next_spec is not None:
                load_x(next_spec)
            if is_s:
                store(conv_s_o, ucarry_s[:].rearrange("p c j -> p (c j)"), [B["ucarry_s"]], "st_misc", [B["conv_s_o"]])
                store(h_s_o, hcarry_s[:], [B["hcarry_s"]], "st_misc", [B["h_s_o"]])
            elif tix == NT - 1:
                store(conv_p_o, ucarry_p[:].rearrange("p c j -> p (c j)"), [B["ucarry_p"]], "st_misc", [B["conv_p_o"]])
                store(h_p_o, hcarry_p[:], [B["hcarry_p"]], "st_misc", [B["h_p_o"]])
            flush_stores()

            norm_apply(x1, B["x1"], PP_KVN, t_x1, xn, Bxn, N)
            def kdst_of(h):
                def kdst(u1, rk, h=h):
                    kst = T32.alloc()
                    S.emit("dve", lambda e: e.tensor_tensor(out=T32.ap(kst)[:, 0:N], in0=T32.ap(u1)[:, 0:N], in1=T32.ap(rk)[:, 0:N], op=ALU.mult),
                           reads=[T32.buf(u1), T32.buf(rk)], writes=[T32.buf(kst)])
                    S.emit("pool", lambda e: e.tensor_copy(out=b16[:, h, 0:N], in_=T32.ap(kst)[:, 0:N]),
                           reads=[T32.buf(kst)], writes=[Bb16[h]])
                    store_sems.add(f"st32_{kst}")
                    S.emit("sp", lambda e: e.dma_start(out=kTo[h, :, tok0:tok0 + N], in_=T32.ap(kst)[:, 0:N]),
                           reads=[T32.buf(kst)], dsem=f"st32_{kst}")
                    T32.release(kst)
                return kdst
            qk_heads("k", N, ropei, xn, Bxn, kdst_of)
            if not is_s:
                S.emit("sp", lambda e: e.dma_start(out=kscr.rearrange("h p n -> p h n")[:, :, tok0:tok0 + N], in_=b16[:, 0:8, 0:N]),
                       reads=[Bb16[h] for h in range(8)], writes=[B["kscr"]], dsem="st_kscr")
            norm_apply(x1, B["x1"], PP_BN, t_x1, xn2, Bxn2, N)
            T32.release(t_x1)
            for fh in range(2):
                bks = [bank() for _ in range(NB)]
                for sp_ in range(4):
                    slot = next_slab("wv", fh * 4 + sp_)
                    prefetch()
                    for tb in range(NB):
                        nt = min(128, N - tb * 128)
                        for kk in range(2):
                            kc = 2 * sp_ + kk
                            last = (sp_ == 3 and kk == 1)
                            S.emit("pe", lambda e, tb=tb, nt=nt, kk=kk, kc=kc, last=last, slot=slot, sp_=sp_, bks=bks: e.matmul(
                                psb(bks[tb])[0:nt, :], lhsT=xn[:, kc, tb * 128: tb * 128 + nt],
                                rhs=ring[:, slot, kk * 512:(kk + 1) * 512], start=(sp_ == 0 and kk == 0), stop=last),
                                reads=[Bxn[kc], Bring[slot]], writes=[Bps[bks[tb]]], sig=(kk == 1))
                for tb in range(NB):
                    nt = min(128, N - tb * 128)
                    vs = T32.alloc()
                    S.emit("act", lambda e, tb=tb, nt=nt, vs=vs, bks=bks: e.activation(out=T32.ap(vs)[0:nt, 0:512], in_=psb(bks[tb])[0:nt, :], func=AF.Copy),
                           reads=[Bps[bks[tb]]], writes=[T32.buf(vs)])
                    S.emit("pool", lambda e, tb=tb, nt=nt, vs=vs, fh=fh: e.tensor_copy(
                        out=b16[0:nt, 11 + 4 * fh: 11 + 4 * fh + 4, tb * 128:(tb + 1) * 128],
                        in_=T32.ap(vs)[0:nt, 0:512].rearrange("p (h e) -> p h e", e=128)),
                        reads=[T32.buf(vs)], writes=[Bb16[11 + 4 * fh + i] for i in range(4)])
                    store_sems.add(f"st32_{vs}")
                    S.emit("sp", lambda e, tb=tb, nt=nt, vs=vs, fh=fh: e.dma_start(
                        out=vo[tok0 + tb * 128: tok0 + tb * 128 + nt, fh * 512:(fh + 1) * 512], in_=T32.ap(vs)[0:nt, 0:512]),
                        reads=[T32.buf(vs)], dsem=f"st32_{vs}")
                    T32.release(vs)
            if not is_s:
                S.emit("sp", lambda e: e.dma_start(
                    out=vscr.rearrange("h p kb e -> p h kb e")[:, :, 4 * tix:4 * tix + 4, :],
                    in_=b16[:, 11:19, :].rearrange("p h (tb e) -> p h tb e", e=128)),
                    reads=[Bb16[11 + h] for h in range(8)], writes=[B["vscr"]], dsem="st_vscr")

            nkeys_past = PAST if is_s else (tix + 1) * TN

            def load_head(h):
                bi = h % 2
                if is_s:
                    S.emit("pool", lambda e: e.dma_start(out=kTh[:, bi, 0:PAST], in_=ckT[h]),
                           writes=[BkTh[bi]], dsem=f"ldp_k{bi}")
                    S.emit("pool", lambda e: e.dma_start(
                        out=vh[:, bi, 0:8, :], in_=cv.rearrange("(kb p) (h e) -> h p kb e", p=128, e=128)[h]),
                        writes=[Bvh[bi]], dsem=f"ldp_v{bi}")
                else:
                    S.emit("sp", lambda e: e.dma_start(out=kTh[:, bi, 0:nkeys_past], in_=kscr[h, :, 0:nkeys_past]),
                           reads=[B["kscr"]], writes=[BkTh[bi]], dsem=f"ld_k{bi}")
                    S.emit("sp", lambda e: e.dma_start(out=vh[:, bi, 0:4 * (tix + 1), :], in_=vscr[h, :, 0:4 * (tix + 1), :]),
                           reads=[B["vscr"]], writes=[Bvh[bi]], dsem=f"ld_v{bi}")

            def finish_head_load(h):
                bi = h % 2
                if is_s:
                    S.emit("pool", lambda e: e.tensor_copy(out=kTh[:, bi, PAST:PAST + NS], in_=b16[:, h, 0:NS]),
                           reads=[Bb16[h]], writes=[BkTh[bi]])
                    S.emit("pool", lambda e: e.tensor_copy(out=vh[0:NS, bi, 8, :], in_=b16[0:NS, 11 + h, 0:128]),
                           reads=[Bb16[11 + h]], writes=[Bvh[bi]])

            if is_s:
                pass

            if is_s:
                qbase, gbase = 8, 19
            if is_s:
                def qT_ap(h):
                    return kTh[:, 0, 2048 + h * NS: 2048 + (h + 1) * NS]

                def sg_ap(m_):
                    return kTh[:, 0, 3072 + m_ * NS: 3072 + (m_ + 1) * NS]
                BqT = [Buf(f"qTs{h}") for h in range(8)]
                Bsg = [Buf(f"sgs{h}") for h in range(8)]
            else:
                def qT_ap(h):
                    return b16[:, h, 0:N]

                def sg_ap(m_):
                    return b16[:, 11 + m_, 0:N]
                BqT = [Bb16[h] for h in range(8)]
                Bsg = [Bb16[11 + h] for h in range(8)]

            if is_s:
                pass

            def qdst_of(h):
                def qdst(u1, rk, h=h):
                    S.emit("dve", lambda e: e.tensor_tensor(out=qT_ap(h), in0=T32.ap(u1)[:, 0:N], in1=T32.ap(rk)[:, 0:N], op=ALU.mult),
                           reads=[T32.buf(u1), T32.buf(rk)], writes=[BqT[h]])
                return qdst
            qk_heads("q", N, ropei, xn2, Bxn2, qdst_of)
            for m_ in range(8):
                slot = next_slab("wbg", m_)
                prefetch()
                bkg = proj_fm(slot, 8, lambda k: xn2[:, k, 0:N], (lambda k: [Bxn2[k]]), N)
                tg = T32.alloc()
                sigmoid_from_psum(tg, bkg)
                S.emit("dve", lambda e, tg=tg, bkg=bkg, m_=m_: e.tensor_tensor(out=sg_ap(m_), in0=T32.ap(tg)[:, 0:N], in1=psb(bkg)[:, 0:N], op=ALU.mult),
                       reads=[T32.buf(tg), Bps[bkg]], writes=[Bsg[m_]])
                T32.release(tg)
            flush_stores()

            if next_spec is not None:
                a_norm(next_spec)
            def blocks():
                L = []
                if is_s:
                    for kb_ in range(8):
                        L.append((kb_ * 128, 128, kb_, 0, False))
                    L.append((PAST, NS, 8, 0, False))
                else:
                    for kd in range(4):
                        L.append(((4 * tix + kd) * 128, 128, 4 * tix + kd, 128 * kd, True))
                    for kb_ in range(4 * tix):
                        L.append((kb_ * 128, 128, kb_, 0, False))
                return L

            blks = blocks()
            nblk = len(blks)
            steps = [(h, i) for h in range(NH) for i in range(nblk)]
            O1, O2, L1, L2 = 4, 5, 6, 7
            load_head(0)
            finish_head_load(0)
            load_head(1)
            finish_head_load(1)

            def emit_qk(s):
                h, i = steps[s]
                col0, nk, vb, q0, corner = blks[i]
                sp_ = s % 2
                bi = h % 2
                S.emit("pe", lambda e: e.matmul(psum[sp_][0:nk, 0, q0:N], lhsT=kTh[0:64, bi, col0:col0 + nk],
                                                rhs=qT_ap(h)[0:64, q0:N] if not is_s else qT_ap(h)[0:64, :], start=True, stop=True),
                       reads=[BkTh[bi], BqT[h]], writes=[Bps[2 * sp_]], sig=False)
                S.emit("pe", lambda e: e.matmul(psum[sp_][0:nk, 1, q0:N], lhsT=kTh[64:128, bi, col0:col0 + nk],
                                                rhs=qT_ap(h)[64:128, q0:N] if not is_s else qT_ap(h)[64:128, :], start=True, stop=True),
                       reads=[BkTh[bi], BqT[h]], writes=[Bps[2 * sp_ + 1]])

            def emit_exp_pv(s):
                h, i = steps[s]
                col0, nk, vb, q0, corner = blks[i]
                sp_ = s % 2
                pi = s % 3
                bi = h % 2
                first = (i == 0)
                last = (i == nblk - 1)
                S.emit("act", lambda e: e.activation(out=pt[0:nk, pi, :, q0:N], in_=psum[sp_][0:nk, :, q0:N], func=AF.Exp, scale=0.125),
                       reads=[Bps[2 * sp_], Bps[2 * sp_ + 1]], writes=[Bpt[pi]])
                if corner:
                    S.emit("pool", lambda e: e.memset(pt[64:128, pi, :, q0:q0 + 64], 0.0), writes=[Bpt[pi]])
                for c_, ob in ((0, O1), (1, O2)):
                    S.emit("pe", lambda e, c_=c_, ob=ob: e.matmul(psb(ob)[:, q0:N], lhsT=vh[0:nk, bi, vb, :], rhs=pt[0:nk, pi, c_, q0:N],
                                                                   start=first, stop=last),
                           reads=[Bvh[bi], Bpt[pi]], writes=[Bps[ob]], sig=False)
                for c_, lb in ((0, L1), (1, L2)):
                    S.emit("pe", lambda e, c_=c_, lb=lb: e.matmul(psb(lb)[:, q0:N], lhsT=ones_b[0:nk, :], rhs=pt[0:nk, pi, c_, q0:N],
                                                                   start=first, stop=last),
                           reads=[B["cmb"], Bpt[pi]], writes=[Bps[lb]], sig=(c_ == 1))

            fin_state = {}

            def fin_a(h):
                r1 = T32.alloc()
                r2 = T32.alloc()
                o1 = T32.alloc()
                o2 = T32.alloc()
                S.emit("dve", lambda e: e.tensor_copy(out=T32.ap(o1)[:, 0:N], in_=psb(O1)[:, 0:N]), reads=[Bps[O1]], writes=[T32.buf(o1)])
                S.emit("dve", lambda e: e.tensor_copy(out=T32.ap(o2)[:, 0:N], in_=psb(O2)[:, 0:N]), reads=[Bps[O2]], writes=[T32.buf(o2)])
                S.emit("act", lambda e: e.activation(out=T32.ap(r1)[:, 0:N], in_=psb(L1)[:, 0:N], func=AF.Ln), reads=[Bps[L1]], writes=[T32.buf(r1)])
                S.emit("act", lambda e: e.activation(out=T32.ap(r2)[:, 0:N], in_=psb(L2)[:, 0:N], func=AF.Ln), reads=[Bps[L2]], writes=[T32.buf(r2)])
                fin_state[h] = (r1, r2, o1, o2)

            def fin_a2(h):
                r1, r2, o1, o2 = fin_state.pop(h)
                S.emit("act", lambda e: e.activation(out=T32.ap(r1)[:, 0:N], in_=T32.ap(r1)[:, 0:N], func=AF.Exp, scale=-1.0),
                       reads=[T32.buf(r1)], writes=[T32.buf(r1)])
                S.emit("act", lambda e: e.activation(out=T32.ap(r2)[:, 0:N], in_=T32.ap(r2)[:, 0:N], func=AF.Exp, scale=-1.0),
                       reads=[T32.buf(r2)], writes=[T32.buf(r2)])
                S.emit("dve", lambda e: e.tensor_tensor(out=T32.ap(r1)[:, 0:N], in0=T32.ap(o1)[:, 0:N], in1=T32.ap(r1)[:, 0:N], op=ALU.mult),
                       reads=[T32.buf(o1), T32.buf(r1)], writes=[T32.buf(r1)])
                S.emit("dve", lambda e: e.tensor_tensor(out=T32.ap(r2)[:, 0:N], in0=T32.ap(o2)[:, 0:N], in1=T32.ap(r2)[:, 0:N], op=ALU.mult),
                       reads=[T32.buf(o2), T32.buf(r2)], writes=[T32.buf(r2)])
                T32.release(o1)
                T32.release(o2)
                S.emit("dve", lambda e: e.scalar_tensor_tensor(out=T32.ap(r1)[:, 0:N], in0=T32.ap(r2)[:, 0:N], scalar=smc(SM_NLAM),
                                                               in1=T32.ap(r1)[:, 0:N], op0=ALU.mult, op1=ALU.add),
                       reads=[T32.buf(r1), T32.buf(r2), Bsm], writes=[T32.buf(r1)])
                T32.release(r2)
                sq = T16.alloc()
                S.emit("dve", lambda e: e.tensor_tensor(out=T16.ap(sq)[:, 0:N], in0=T32.ap(r1)[:, 0:N], in1=T32.ap(r1)[:, 0:N], op=ALU.mult),
                       reads=[T32.buf(r1)], writes=[T16.buf(sq)])
                fin_state[h] = (r1, sq)

            def fin_b(h, bk):
                r1, sq = fin_state.pop(h)
                S.emit("pe", lambda e: e.matmul(psb(bk)[:, 0:N], lhsT=ones_b, rhs=T16.ap(sq)[:, 0:N], start=True, stop=True),
                       reads=[T16.buf(sq), B["cmb"]], writes=[Bps[bk]])
                T16.release(sq)
                rh = T32.alloc()
                S.emit("act", lambda e: e.activation(out=T32.ap(rh)[:, 0:N], in_=psb(bk)[:, 0:N], func=AF.Ln, scale=1.0 / 128, bias=epsc),
                       reads=[Bps[bk], Bsm], writes=[T32.buf(rh)])
                S.emit("act", lambda e: e.activation(out=T32.ap(rh)[:, 0:N], in_=T32.ap(rh)[:, 0:N], func=AF.Exp, scale=-0.5),
                       reads=[T32.buf(rh)], writes=[T32.buf(rh)])
                S.emit("dve", lambda e: e.scalar_tensor_tensor(out=T32.ap(r1)[:, 0:N], in0=T32.ap(r1)[:, 0:N], scalar=smc(SM_HGS),
                                                               in1=T32.ap(rh)[:, 0:N], op0=ALU.mult, op1=ALU.mult),
                       reads=[T32.buf(r1), T32.buf(rh), Bsm], writes=[T32.buf(r1)])
                S.emit("dve", lambda e: e.tensor_tensor(out=on[:, h, 0:N], in0=T32.ap(r1)[:, 0:N], in1=sg_ap(h), op=ALU.mult),
                       reads=[T32.buf(r1), Bsg[h]], writes=[Bon[h]])
                T32.release(r1)
                T32.release(rh)

            ns = len(steps)
            emit_qk(0)
            if ns > 1:
                emit_qk(1)
            pend_a2 = None
            pend_b = None
            for s in range(ns):
                h, i = steps[s]
                emit_exp_pv(s)
                if pend_a2 is not None and s >= pend_a2[1]:
                    fin_a2(pend_a2[0])
                    pend_b = (pend_a2[0], s + 2)
                    pend_a2 = None
                elif pend_b is not None and s >= pend_b[1]:
                    fin_b(pend_b[0], 2 * (s % 2))
                    pend_b = None
                if i == nblk - 1:
                    fin_a(h)
                    pend_a2 = (h, s + 1)
                    if h + 2 < NH:
                        load_head(h + 2)
                        finish_head_load(h + 2)
                if s + 2 < ns:
                    emit_qk(s + 2)
            if next_spec is not None and HOIST > 0:
                aphn = make_aphase(next_spec)
                for k in range(HOIST):
                    aphn["run_k"](k)
                aphn["k_done"] = HOIST
                aph_pending[next_spec] = aphn
            if pend_a2 is not None:
                fin_a2(pend_a2[0])
                pend_b = (pend_a2[0], 0)
            if pend_b is not None:
                fin_b(pend_b[0], 0)

            for m_ in range(8):
                slot = next_slab("wbo", m_)
                prefetch()
                bk = proj_fm(slot, 8, lambda k: on[:, k, 0:N], Bon, N)
                ys = T32.alloc()
                S.emit("dve", lambda e, m_=m_, bk=bk, ys=ys: e.tensor_tensor(out=T32.ap(ys)[:, 0:N], in0=x1[:, m_, 0:N], in1=psb(bk)[:, 0:N], op=ALU.add),
                       reads=[B["x1"], Bps[bk]], writes=[T32.buf(ys)])
                store_sems.add(f"st32_{ys}")
                S.emit("sp", lambda e, m_=m_, ys=ys: e.dma_start(out=ysrc[m_ * 128:(m_ + 1) * 128, tok0:tok0 + N], in_=T32.ap(ys)[:, 0:N]),
                       reads=[T32.buf(ys)], dsem=f"st32_{ys}")
                T32.release(ys)

        for i_, spec in enumerate(tile_specs):
            run_tile(spec, tile_specs[i_ + 1] if i_ + 1 < len(tile_specs) else None, i_ == 0)
        flush_stores()
        S.final_waits("sp", sorted(store_sems) + ["pe", "act", "dve", "pool"])

        block = es.enter_context(nc.Block())
        S.replay(block)
    return nc


def _tile_w(W, nk, nm):
    return np.ascontiguousarray(W.reshape(nk, 128, nm, 128).transpose(2, 1, 0, 3).reshape(nm, 128, nk * 128))


def _chunkcol(v):
    return np.ascontiguousarray(v.reshape(-1, 128).T)


def _prep_shared(inp):
    f = np.float32
    a_w_in = np.asarray(inp["a_w_in"][0], f)
    shared = {}
    shared["win_u"] = _tile_w(a_w_in[:, :DR], 8, NCH)
    shared["win_g"] = _tile_w(a_w_in[:, DR:], 8, NCH)
    gts = np.zeros((NCH, 128, 3, 2, 128), f)
    for g, key in enumerate(("a_gate_r_w", "a_gate_i_w")):
        Wg = np.asarray(inp[key][0], f)
        Dm = np.zeros((DR, DR), f)
        for n in range(16):
            Dm[88 * n:88 * n + 88, 88 * n:88 * n + 88] = Wg[n]
        for c in range(NCH):
            for dk in range(3):
                kc = c + dk - 1
                if 0 <= kc < NCH:
                    gts[c, :, dk, g, :] = Dm[kc * 128:(kc + 1) * 128, c * 128:(c + 1) * 128]
    shared["gts"] = gts.reshape(NCH, 128, 768)
    shared["wout"] = _tile_w(np.asarray(inp["a_w_out"][0], f), NCH, 8)
    kv_w = np.asarray(inp["kv_w"], f)
    shared["wk"] = _tile_w(kv_w[:, :D], 8, 8)
    Wv = kv_w[:, D:]
    wv = Wv.reshape(4, 2, 128, 2, 512).transpose(3, 0, 2, 1, 4)
    shared["wv"] = np.ascontiguousarray(wv.reshape(8, 128, 1024))
    b_w_in = np.asarray(inp["b_w_in"][0], f)
    shared["wq"] = _tile_w(b_w_in[:, :D], 8, 8)
    shared["wbg"] = _tile_w(b_w_in[:, D:], 8, 8)
    shared["wbo"] = _tile_w(np.asarray(inp["b_w_out"][0], f), 8, 8)
    pp = np.zeros((128, PP_N), f)
    pp[:, PP_AN:PP_AN + 8] = _chunkcol(np.asarray(inp["a_norm"][0], f))
    pp[:, PP_KVN:PP_KVN + 8] = _chunkcol(np.asarray(inp["kv_norm"], f))
    pp[:, PP_BN:PP_BN + 8] = _chunkcol(np.asarray(inp["b_norm"][0], f))
    cw = np.asarray(inp["a_conv_w"][0], f)
    pp[:, PP_CW:PP_CW + 44] = cw.reshape(4, NCH, 128).transpose(2, 1, 0).reshape(128, 44)
    pp[:, PP_CB:PP_CB + 11] = _chunkcol(np.asarray(inp["a_conv_b"][0], f))
    pp[:, PP_BR:PP_BR + 11] = _chunkcol(np.asarray(inp["a_gate_r_b"][0], f))
    pp[:, PP_BI:PP_BI + 11] = _chunkcol(np.asarray(inp["a_gate_i_b"][0], f))
    pp[:, PP_LAM:PP_LAM + 11] = _chunkcol(np.asarray(inp["a_lambda"][0], f))
    p = np.arange(128)
    kn = np.asarray(inp["k_norm"], f)
    qn = np.asarray(inp["b_q_norm"][0], f)
    pp[:, PP_GK] = kn[p % 64]
    pp[:, PP_GKS] = kn[(p ^ 32) % 64]
    pp[:, PP_GQ] = qn[p % 64]
    pp[:, PP_GQS] = qn[(p ^ 32) % 64]
    pp[:, PP_HG] = np.asarray(inp["b_head_norm"][0], f)
    pp[:, PP_LQ1:PP_LQ1 + 64] = np.asarray(inp["b_lambda_q1"][0], f)[None, :]
    pp[:, PP_LK1:PP_LK1 + 64] = np.asarray(inp["b_lambda_k1"][0], f)[None, :]
    pp[:, PP_LQ2:PP_LQ2 + 64] = np.asarray(inp["b_lambda_q2"][0], f)[None, :]
    pp[:, PP_LK2:PP_LK2 + 64] = np.asarray(inp["b_lambda_k2"][0], f)[None, :]
    shared["pp"] = pp
    cm = np.zeros((128, 384), f)
    cm[:, 0:128] = 1.0
    cm[:, 128:256] = (p[:, None] // 64 == p[None, :] // 64).astype(f)
    cm[:, 256:384] = (p[:, None] == (p[None, :] ^ 32)).astype(f)
    shared["cmat"] = cm
    half = 32
    inv = (np.float32(10000.0) ** (-np.arange(half, dtype=f) / np.float32(half))).astype(f)
    pos = np.concatenate([np.arange(SEQ), PAST + np.arange(NS)]).astype(f)
    ang = (pos[:, None] * inv[None, :]).astype(f)
    cos = np.cos(ang).astype(f).T
    sin = np.sin(ang).astype(f).T
    fi = p % 32
    sign = np.where((p % 64) < 32, -1.0, 1.0).astype(f)
    rope = np.empty((128, 2, SEQ + NS), f)
    rope[:, 0, :] = cos[fi]
    rope[:, 1, :] = sin[fi] * sign[:, None]
    shared["rope"] = rope
    return shared


_PROG = None


def kernel(**inp):
    global _PROG
    f = np.float32
    shared = _prep_shared(inp)
    x_prompt = np.asarray(inp["x_prompt"], f)
    x_sample = np.asarray(inp["x_sample"], f)
    state_conv = np.asarray(inp["state_conv"], f)
    state_h = np.asarray(inp["state_h"], f)
    cache_k = np.asarray(inp["cache_k"], f)
    cache_v = np.asarray(inp["cache_v"], f)
    in_maps = []
    for b in range(8):
        m = dict(shared)
        m["xT_p"] = np.ascontiguousarray(x_prompt[b].T)
        m["xT_s"] = np.ascontiguousarray(x_sample[b].T)
        m["conv_s_in"] = np.ascontiguousarray(state_conv[0, b].reshape(3, NCH, 128).transpose(2, 1, 0).reshape(128, NCH * 3))
        m["h_s_in"] = _chunkcol(state_h[0, b])
        m["ckT"] = np.ascontiguousarray(cache_k[b].transpose(1, 2, 0))
        m["cv"] = np.ascontiguousarray(cache_v[b].reshape(PAST, D))
        in_maps.append(m)
    if _PROG is None:
        _PROG = build_program()
    res = run_bass_kernel_spmd(_PROG, in_maps, core_ids=list(range(8)))
    R = res.results

    def unconv(a):
        return a.reshape(128, NCH, 3).transpose(2, 1, 0).reshape(3, DR)

    def unh(a):
        return a.T.reshape(DR)

    y_p = np.stack([R[b]["yT_p"].T for b in range(8)]).astype(f)
    y_s = np.stack([R[b]["yT_s"].T for b in range(8)]).astype(f)
    conv_p = np.stack([unconv(R[b]["conv_p_o"]) for b in range(8)])[None].astype(f)
    h_p = np.stack([unh(R[b]["h_p_o"]) for b in range(8)])[None].astype(f)
    k_p = np.stack([R[b]["kT_p"].transpose(2, 0, 1) for b in range(8)]).astype(f)
    v_p = np.stack([R[b]["v_p"].reshape(SEQ, NH, 128) for b in range(8)]).astype(f)
    conv_s = np.stack([unconv(R[b]["conv_s_o"]) for b in range(8)])[None].astype(f)
    h_s = np.stack([unh(R[b]["h_s_o"]) for b in range(8)])[None].astype(f)
    k_s = np.stack([R[b]["kT_s"].transpose(2, 0, 1) for b in range(8)]).astype(f)
    v_s = np.stack([R[b]["v_s"].reshape(NS, NH, 128) for b in range(8)]).astype(f)
    return (np.ascontiguousarray(y_p), np.ascontiguousarray(y_s), np.ascontiguousarray(conv_p), np.ascontiguousarray(h_p),
            np.ascontiguousarray(k_p), np.ascontiguousarray(v_p), np.ascontiguousarray(conv_s), np.ascontiguousarray(h_s),
            np.ascontiguousarray(k_s), np.ascontiguousarray(v_s))
```

```python
import math
from contextlib import ExitStack

import numpy as np
import concourse.bass as bass
import concourse.mybir as mybir
from concourse.bass_utils import run_bass_kernel_spmd

F32 = mybir.dt.float32
BF16 = mybir.dt.bfloat16
AF = mybir.ActivationFunctionType
ALU = mybir.AluOpType
AX = mybir.AxisListType

D = 1024
SEQ = 4096
NS = 16
PAST = 1024
DR = 1408
NCH = 11
NH = 8
EPS = 1e-6
LAM_INIT = 0.8 - 0.6 * math.exp(-0.3 * 1)
TN = 512
LEAD = 3
NT = SEQ // TN
RING = 8
K32 = 24
K16 = 4

PP_AN, PP_KVN, PP_BN = 0, 8, 16
PP_CW, PP_CB, PP_BR, PP_BI, PP_LAM = 24, 68, 79, 90, 101
PP_GK, PP_GKS, PP_GQ, PP_GQS, PP_HG = 112, 113, 114, 115, 116
PP_LQ1, PP_LK1, PP_LQ2, PP_LK2 = 117, 181, 245, 309
PP_N = 373


class Buf:
    __slots__ = ("name", "w", "r", "excl")

    def __init__(self, name, excl=False):
        self.name = name
        self.w = {}
        self.r = {}
        self.excl = excl


class Sched:
    CE = ("pe", "act", "dve", "pool")

    def __init__(self, nc, es):
        self.nc = nc
        self.es = es
        self.streams = {k: [] for k in ("pe", "act", "dve", "pool", "sp")}
        self.sems = {}
        self.cnt = {}
        self.waited = {k: {} for k in self.streams}
        for e in self.CE:
            self._sem(e)

    def _sem(self, name):
        if name not in self.sems:
            self.sems[name] = self.es.enter_context(self.nc.semaphore("s_" + name))
            self.cnt[name] = 0
        return self.sems[name]

    def emit(self, eng, fn, reads=(), writes=(), sig=True, dsem=None):
        need = {}

        def add(s, v, kind):
            if dsem is None and s == eng:
                if eng == "pe" or kind != "raw":
                    return
            if need.get(s, 0) < v:
                need[s] = v

        for b in reads:
            for s, v in b.w.items():
                add(s, v, "raw")
            if b.excl:
                for s, v in b.r.items():
                    add(s, v, "rar")
        for b in writes:
            for s, v in b.w.items():
                add(s, v, "waw")
            for s, v in b.r.items():
                add(s, v, "war")
        waits = []
        wd = self.waited[eng]
        for s, v in need.items():
            if wd.get(s, 0) < v:
                assert self.cnt[s] >= v, f"wait on future event {s}>={v} (cnt {self.cnt[s]}) from {eng}"
                wd[s] = v
                waits.append((s, v))
        if dsem is not None:
            self._sem(dsem)
            self.cnt[dsem] += 16
            ev = (dsem, self.cnt[dsem])
            sg = (dsem, 16)
        elif sig:
            self.cnt[eng] += 1
            ev = (eng, self.cnt[eng])
            sg = (eng, 1)
        else:
            ev = (eng, self.cnt[eng] + 1)
            sg = None
        self.streams[eng].append((waits, fn, sg))
        for b in writes:
            b.w = {ev[0]: ev[1]}
            b.r = {}
        for b in reads:
            if b.r.get(ev[0], 0) < ev[1]:
                b.r[ev[0]] = ev[1]
        return ev

    def final_waits(self, eng, names):
        waits = [(s, self.cnt[s]) for s in names if self.cnt.get(s, 0) > 0]
        self.streams[eng].append((waits, None, None))

    def replay(self, block):
        def run(stream):
            def body(e):
                for waits, fn, sg in stream:
                    for s, v in waits:
                        e.wait_ge(self.sems[s], v)
                    if fn is None:
                        continue
                    ins = fn(e)
                    if sg is not None:
                        ins.then_inc(self.sems[sg[0]], sg[1])
            return body

        block.tensor(run(self.streams["pe"]))
        block.scalar(run(self.streams["act"]))
        block.vector(run(self.streams["dve"]))
        block.gpsimd(run(self.streams["pool"]))
        block.sync(run(self.streams["sp"]))


class Pool32:
    def __init__(self, aps, name):
        self.items = [(ap, Buf(f"{name}{i}")) for i, ap in enumerate(aps)]
        self.free = list(range(len(aps)))

    def alloc(self):
        assert self.free, "temp pool exhausted"
        i = self.free.pop(0)
        return i

    def ap(self, i):
        return self.items[i][0]

    def buf(self, i):
        return self.items[i][1]

    def release(self, i):
        self.free.append(i)


def build_program():
    nc = bass.Bass("TRN2", target_bir_lowering=False)

    def din(name, shape, dt=F32):
        return nc.dram_tensor(name, list(shape), dt, kind="ExternalInput").ap()

    def dout(name, shape, dt=F32):
        return nc.dram_tensor(name, list(shape), dt, kind="ExternalOutput").ap()

    def dint(name, shape, dt=BF16):
        return nc.dram_tensor(name, list(shape), dt, kind="Internal").ap()

    xT_p = din("xT_p", [D, SEQ])
    xT_s = din("xT_s", [D, NS])
    conv_s_in = din("conv_s_in", [128, NCH * 3])
    h_s_in = din("h_s_in", [128, NCH])
    ckT = din("ckT", [NH, 128, PAST])
    cv = din("cv", [PAST, D])
    wnames = [("win_u", NCH, 1024), ("win_g", NCH, 1024), ("gts", NCH, 768), ("wout", 8, 1408),
              ("wk", 8, 1024), ("wv", 8, 1024), ("wq", 8, 1024), ("wbg", 8, 1024), ("wbo", 8, 1024)]
    wf32 = {n: din(n, [ns, 128, w]) for n, ns, w in wnames}
    wb16 = {n: dint(n + "_b", [ns, 128, w]) for n, ns, w in wnames}
    wwidth = {n: w for n, ns, w in wnames}
    pp_in = din("pp", [128, PP_N])
    cmat_in = din("cmat", [128, 384])
    rope_in = din("rope", [128, 2, SEQ + NS])

    yT_p = dout("yT_p", [D, SEQ])
    yT_s = dout("yT_s", [D, NS])
    conv_p_o = dout("conv_p_o", [128, NCH * 3])
    h_p_o = dout("h_p_o", [128, NCH])
    kT_p = dout("kT_p", [NH, 128, SEQ])
    v_p = dout("v_p", [SEQ, D])
    conv_s_o = dout("conv_s_o", [128, NCH * 3])
    h_s_o = dout("h_s_o", [128, NCH])
    kT_s = dout("kT_s", [NH, 128, NS])
    v_s = dout("v_s", [NS, D])

    kscr = dint("kscr", [NH, 128, SEQ])
    vscr = dint("vscr", [NH, 128, SEQ // 128, 128])

    with ExitStack() as es:
        S = Sched(nc, es)

        def sb(name, shape, dt):
            return es.enter_context(nc.sbuf_tensor(name, list(shape), dt))

        xin = sb("xin", [128, 8, TN], F32)
        x1 = sb("x1", [128, 8, TN], F32)
        xn = sb("xn", [128, 8, TN], BF16)
        ring = sb("ring", [128, RING, 1408], BF16)
        kTh = sb("kTh", [128, 2, SEQ], BF16)
        vh = sb("vh", [128, 2, SEQ // 128, 128], BF16)
        pp = sb("pp_sb", [128, PP_N], F32)
        cmf = sb("cmf", [128, 384], F32)
        cmb = sb("cmb", [128, 384], BF16)
        xn2 = sb("xn2", [128, 8, TN], BF16)
        rope = sb("rope_sb", [128, 2, 2, TN], F32)
        t32 = sb("t32", [128, K32, TN + 4], F32)
        b16 = sb("b16", [128, 22, TN], BF16)
        on = sb("on", [128, 8, TN], BF16)
        pt = sb("pt", [128, 3, 2, TN], BF16)
        t16 = sb("t16", [128, K16, TN], BF16)
        small = sb("small", [128, 64], F32)
        lprod = sb("lprod", [128, 128], F32)
        ucarry_p = sb("ucarry_p", [128, NCH, 3], F32)
        hcarry_p = sb("hcarry_p", [128, NCH], F32)
        ucarry_s = sb("ucarry_s", [128, NCH, 3], F32)
        hcarry_s = sb("hcarry_s", [128, NCH], F32)
        psum = [es.enter_context(nc.psum_tensor(f"ps{i}", [128, 2, 512], F32)) for i in range(4)]

        B = {n: Buf(n) for n in ("xin", "x1", "xn", "xn2", "pp", "cmf", "cmb", "consts", "small", "lprod", "on",
                                 "ucarry_p", "hcarry_p", "ucarry_s", "hcarry_s", "kscr", "vscr",
                                 "conv_p_o", "h_p_o", "conv_s_o", "h_s_o")}
        Bring = [Buf(f"ring{i}") for i in range(RING)]
        BkTh = [Buf("kTh0"), Buf("kTh1")]
        Bvh = [Buf("vh0"), Buf("vh1")]
        Brope = [Buf("rope0"), Buf("rope1")]
        Bb16 = [Buf(f"b16_{i}") for i in range(22)]
        Bpt = [Buf(f"pt{i}") for i in range(3)]
        Bps = [Buf(f"psb{i}", excl=True) for i in range(8)]
        Bon = [Buf(f"on{i}") for i in range(8)]
        Bxn = [Buf(f"xn{i}") for i in range(8)]
        Bxn2 = [Buf(f"xn2_{i}") for i in range(8)]
        Bw = {n: Buf("w_" + n) for n, _, _ in wnames}
        T32 = Pool32([t32[:, i, :] for i in range(K32)], "t32_")
        T16 = Pool32([t16[:, i, :] for i in range(K16)], "t16_")
        store_sems = set()

        def psb(i):
            return psum[i // 2][:, i % 2, :]

        bank_rr = [0]
        bank_held = set()

        def bank():
            while True:
                i = bank_rr[0]
                bank_rr[0] = (i + 1) % 8
                if i not in bank_held:
                    return i

        ones_b = cmb[:, 0:128]
        blk_b = cmb[:, 128:256]
        perm_b = cmb[:, 256:384]

        def ppc(i):
            return pp[:, i:i + 1]

        def smc(i):
            return small[:, i:i + 1]
        SM_HBR, SM_HBI, SM_HC, SM_NLAM, SM_HGS = 0, 11, 22, 33, 34
        epsc = small[:, 62:63]
        onec = small[:, 63:64]

        S.emit("sp", lambda e: e.dma_start(out=pp[:], in_=pp_in), writes=[B["pp"]], dsem="ld_pp")
        S.emit("sp", lambda e: e.dma_start(out=cmf[:], in_=cmat_in), writes=[B["cmf"]], dsem="ld_cm")
        S.emit("sp", lambda e: e.dma_start(out=xin[:, :, 0:TN], in_=xT_p.rearrange("(kc p) n -> p kc n", p=128)[:, :, 0:TN]),
               writes=[B["xin"]], dsem="ld_x")
        for n, ns, w in wnames:
            S.emit("pool", lambda e, n=n: e.dma_start(out=wb16[n], in_=wf32[n]), reads=[B["xin"], B["pp"], B["cmf"]],
                   writes=[Bw[n]], dsem="cv_" + n)
        S.emit("pool", lambda e: e.tensor_copy(out=cmb[:], in_=cmf[:]), reads=[B["cmf"]], writes=[B["cmb"]])
        S.emit("pool", lambda e: e.memset(ucarry_p[:], 0.0), writes=[B["ucarry_p"]])
        S.emit("pool", lambda e: e.memset(hcarry_p[:], 0.0), writes=[B["hcarry_p"]])
        S.emit("sp", lambda e: e.dma_start(out=ucarry_s[:].rearrange("p c j -> p (c j)"), in_=conv_s_in),
               writes=[B["ucarry_s"]], dsem="ld_cs")
        S.emit("sp", lambda e: e.dma_start(out=hcarry_s[:], in_=h_s_in), writes=[B["hcarry_s"]], dsem="ld_hs")
        Bsm = B["small"]
        S.emit("pool", lambda e: e.memset(small[:, 62:63], EPS), writes=[Bsm])
        S.emit("pool", lambda e: e.memset(small[:, 63:64], 1.0), writes=[Bsm])
        S.emit("dve", lambda e: e.tensor_scalar(out=small[:, SM_HBR:SM_HBR + 11], in0=pp[:, PP_BR:PP_BR + 11],
                                                scalar1=-1.0, scalar2=None, op0=ALU.mult),
               reads=[B["pp"]], writes=[Bsm])
        S.emit("dve", lambda e: e.tensor_scalar(out=small[:, SM_HBI:SM_HBI + 11], in0=pp[:, PP_BI:PP_BI + 11],
                                                scalar1=-1.0, scalar2=None, op0=ALU.mult),
               reads=[B["pp"]], writes=[Bsm])
        S.emit("act", lambda e: e.activation(out=small[:, 35:46], in_=pp[:, PP_LAM:PP_LAM + 11], func=AF.Abs),
               reads=[B["pp"]], writes=[Bsm])
        S.emit("act", lambda e: e.activation(out=small[:, 35:46], in_=small[:, 35:46], func=AF.Exp, scale=-1.0),
               reads=[Bsm], writes=[Bsm])
        S.emit("act", lambda e: e.activation(out=small[:, 35:46], in_=small[:, 35:46], func=AF.Ln, bias=onec, scale=1.0),
               reads=[Bsm], writes=[Bsm])
        S.emit("dve", lambda e: e.tensor_scalar(out=small[:, 46:57], in0=pp[:, PP_LAM:PP_LAM + 11],
                                                scalar1=-1.0, scalar2=0.0, op0=ALU.mult, op1=ALU.max),
               reads=[B["pp"]], writes=[Bsm])
        S.emit("dve", lambda e: e.tensor_tensor(out=small[:, 46:57], in0=small[:, 46:57], in1=small[:, 35:46], op=ALU.add),
               reads=[Bsm], writes=[Bsm])
        S.emit("dve", lambda e: e.tensor_scalar(out=small[:, SM_HC:SM_HC + 11], in0=small[:, 46:57],
                                                scalar1=-8.0, scalar2=None, op0=ALU.mult),
               reads=[Bsm], writes=[Bsm])
        S.emit("dve", lambda e: e.tensor_tensor(out=lprod[:, 0:64], in0=pp[:, PP_LQ1:PP_LQ1 + 64],
                                                in1=pp[:, PP_LK1:PP_LK1 + 64], op=ALU.mult),
               reads=[B["pp"]], writes=[B["lprod"]])
        S.emit("dve", lambda e: e.tensor_tensor(out=lprod[:, 64:128], in0=pp[:, PP_LQ2:PP_LQ2 + 64],
                                                in1=pp[:, PP_LK2:PP_LK2 + 64], op=ALU.mult),
               reads=[B["pp"]], writes=[B["lprod"]])
        S.emit("dve", lambda e: e.reduce_sum(out=small[:, 57:58], in_=lprod[:, 0:64], axis=AX.X),
               reads=[B["lprod"]], writes=[Bsm])
        S.emit("dve", lambda e: e.reduce_sum(out=small[:, 58:59], in_=lprod[:, 64:128], axis=AX.X),
               reads=[B["lprod"]], writes=[Bsm])
        S.emit("act", lambda e: e.activation(out=small[:, 59:61], in_=small[:, 57:59], func=AF.Exp),
               reads=[Bsm], writes=[Bsm])
        S.emit("dve", lambda e: e.tensor_tensor(out=small[:, 61:62], in0=small[:, 60:61], in1=small[:, 59:60], op=ALU.subtract),
               reads=[Bsm], writes=[Bsm])
        S.emit("dve", lambda e: e.tensor_scalar(out=small[:, SM_NLAM:SM_NLAM + 1], in0=small[:, 61:62],
                                                scalar1=-LAM_INIT, scalar2=None, op0=ALU.add),
               reads=[Bsm], writes=[Bsm])
        S.emit("dve", lambda e: e.tensor_scalar(out=small[:, SM_HGS:SM_HGS + 1], in0=pp[:, PP_HG:PP_HG + 1],
                                                scalar1=(1.0 - LAM_INIT), scalar2=None, op0=ALU.mult),
               reads=[B["pp"]], writes=[Bsm])

        ring_i = [0]

        def load_slab(name, idx):
            slot = ring_i[0] % RING
            ring_i[0] += 1
            w = wwidth[name]
            S.emit("sp", lambda e: e.dma_start(out=ring[:, slot, 0:w], in_=wb16[name][idx]),
                   reads=[Bw[name]], writes=[Bring[slot]], dsem=f"rg{slot}")
            return slot

        def a_slabs(k0, k1):
            L = []
            for k in range(k0, k1):
                if k < NCH:
                    L.append(("win_u", k))
                if 0 <= k - 2 < NCH:
                    L.append(("win_g", k - 2))
                    L.append(("gts", k - 2))
                if k - 3 == NCH - 3:
                    for m in range(4):
                        L.append(("wout", m))
            return L

        import os as _os
        _lim = int(_os.environ.get("KTILES", str(NT)))
        tile_specs = list(range(_lim)) + (["s"] if _os.environ.get("KNOS", "0") == "0" else [])
        HOIST_ = 3
        slab_seq = []
        for i_, _ in enumerate(tile_specs):
            hoisted_in = (i_ > 0)
            slab_seq += a_slabs(HOIST_ if hoisted_in else 0, NCH + 3)
            slab_seq += [("wout", m) for m in range(4, 8)]
            slab_seq += [("wk", h) for h in range(8)] + [("wv", x) for x in range(8)]
            slab_seq += [("wq", h) for h in range(8)] + [("wbg", m) for m in range(8)]
            if i_ + 1 < len(tile_specs):
                slab_seq += a_slabs(0, HOIST_)
            slab_seq += [("wbo", m) for m in range(8)]
        slab_pos = [0]
        slab_loaded = [0]
        slab_slots = {}

        slab_pin = [None]

        def prefetch():
            base = slab_pos[0] if slab_pin[0] is None else min(slab_pos[0], slab_pin[0])
            while slab_loaded[0] < len(slab_seq) and slab_loaded[0] < base + RING - 1:
                n, i = slab_seq[slab_loaded[0]]
                slab_slots[slab_loaded[0]] = load_slab(n, i)
                slab_loaded[0] += 1

        def next_slab(name, idx):
            p = slab_pos[0]
            assert slab_seq[p] == (name, idx), (slab_seq[p], name, idx)
            if p >= slab_loaded[0]:
                prefetch_force(p)
            slot = slab_slots.pop(p)
            slab_pos[0] += 1
            return slot

        def prefetch_force(p):
            while slab_loaded[0] <= p:
                n, i = slab_seq[slab_loaded[0]]
                slab_slots[slab_loaded[0]] = load_slab(n, i)
                slab_loaded[0] += 1

        deferred = []

        def flush_stores():
            for f in deferred:
                f()
            deferred.clear()

        def store(out_ap, in_ap, rbufs, sem, wbufs=()):
            store_sems.add(sem)

            def f():
                S.emit("sp", lambda e: e.dma_start(out=out_ap, in_=in_ap), reads=rbufs, writes=list(wbufs), dsem=sem)
            deferred.append(f)

        def norm_square(src, bsrc, kc, N):
            S.emit("dve", lambda e: e.tensor_tensor(out=on[:, kc, 0:N], in0=src[:, kc, 0:N], in1=src[:, kc, 0:N], op=ALU.mult),
                   reads=[bsrc], writes=[Bon[kc]])

        def norm_sum_mm(bk, kc, N):
            S.emit("pe", lambda e: e.matmul(psb(bk)[:, 0:N], lhsT=ones_b, rhs=on[:, kc, 0:N], start=(kc == 0), stop=(kc == 7)),
                   reads=[Bon[kc], B["cmb"]], writes=[Bps[bk]], sig=True)

        def norm_rstd(bk, N):
            t = T32.alloc()
            S.emit("act", lambda e: e.activation(out=T32.ap(t)[:, 0:N], in_=psb(bk)[:, 0:N], func=AF.Ln, scale=1.0 / D, bias=epsc),
                   reads=[Bps[bk], Bsm], writes=[T32.buf(t)])
            S.emit("act", lambda e: e.activation(out=T32.ap(t)[:, 0:N], in_=T32.ap(t)[:, 0:N], func=AF.Exp, scale=-0.5),
                   reads=[T32.buf(t)], writes=[T32.buf(t)])
            return t

        def norm_apply(src, bsrc, gcol, t, dst, bdst, N):
            for kc in range(8):
                S.emit("dve", lambda e, kc=kc: e.scalar_tensor_tensor(out=dst[:, kc, 0:N], in0=src[:, kc, 0:N],
                                                                      scalar=ppc(gcol + kc), in1=T32.ap(t)[:, 0:N],
                                                                      op0=ALU.mult, op1=ALU.mult),
                       reads=[bsrc, T32.buf(t), B["pp"]], writes=[bdst[kc]])

        def tile_params(spec):
            is_s = spec == "s"
            return dict(N=NS if is_s else TN, tok0=0 if is_s else spec * TN, xsrc=xT_s if is_s else xT_p)

        def load_x(spec):
            p_ = tile_params(spec)
            S.emit("sp", lambda e: e.dma_start(out=xin[:, :, 0:p_["N"]],
                                               in_=p_["xsrc"].rearrange("(kc p) n -> p kc n", p=128)[:, :, p_["tok0"]:p_["tok0"] + p_["N"]]),
                   writes=[B["xin"]], dsem="ld_x")

        def a_norm(spec):
            N = tile_params(spec)["N"]
            for kc in range(8):
                norm_square(xin, B["xin"], kc, N)
            bk = bank()
            for kc in range(8):
                norm_sum_mm(bk, kc, N)
            t = norm_rstd(bk, N)
            norm_apply(xin, B["xin"], PP_AN, t, xn, Bxn, N)
            T32.release(t)

        def proj_fm(slot, nk, rhs_of, rbufs, N, coff=0):
            bk = bank()
            for k in range(nk):
                S.emit("pe", lambda e, k=k: e.matmul(psb(bk)[:, 0:N], lhsT=ring[:, slot, coff + k * 128: coff + (k + 1) * 128],
                                                      rhs=rhs_of(k), start=(k == 0), stop=(k == nk - 1)),
                       reads=[Bring[slot]] + (rbufs(k) if callable(rbufs) else rbufs), writes=[Bps[bk]], sig=(k == nk - 1))
            return bk

        def qk_stage1(kind, h, N, xsrc_ap, bxsrc):
            slot = next_slab("wk" if kind == "k" else "wq", h)
            prefetch()
            bk = proj_fm(slot, 8, lambda k: xsrc_ap[:, k, 0:N], (lambda k: [bxsrc[k]]), N)
            kb = T16.alloc()
            sq = T16.alloc()
            S.emit("act", lambda e: e.activation(out=T16.ap(kb)[:, 0:N], in_=psb(bk)[:, 0:N], func=AF.Copy),
                   reads=[Bps[bk]], writes=[T16.buf(kb)])
            S.emit("act", lambda e: e.activation(out=T16.ap(sq)[:, 0:N], in_=psb(bk)[:, 0:N], func=AF.Square),
                   reads=[Bps[bk]], writes=[T16.buf(sq)])
            return (bk, kb, sq)

        def qk_stage2(kind, st, N, ropei, dst_fn):
            bk, kb, sq = st
            bsw = bank()
            S.emit("pe", lambda e: e.matmul(psb(bsw)[:, 0:N], lhsT=perm_b, rhs=T16.ap(kb)[:, 0:N], start=True, stop=True),
                   reads=[T16.buf(kb), B["cmb"]], writes=[Bps[bsw]])
            bss = bank()
            S.emit("pe", lambda e: e.matmul(psb(bss)[:, 0:N], lhsT=blk_b, rhs=T16.ap(sq)[:, 0:N], start=True, stop=True),
                   reads=[T16.buf(sq), B["cmb"]], writes=[Bps[bss]])
            T16.release(kb)
            T16.release(sq)
            rk = T32.alloc()
            S.emit("act", lambda e: e.activation(out=T32.ap(rk)[:, 0:N], in_=psb(bss)[:, 0:N], func=AF.Ln, scale=1.0 / 64, bias=epsc),
                   reads=[Bps[bss], Bsm], writes=[T32.buf(rk)])
            S.emit("act", lambda e: e.activation(out=T32.ap(rk)[:, 0:N], in_=T32.ap(rk)[:, 0:N], func=AF.Exp, scale=-0.5),
                   reads=[T32.buf(rk)], writes=[T32.buf(rk)])
            g0, g1 = (PP_GK, PP_GKS) if kind == "k" else (PP_GQ, PP_GQS)
            u1 = T32.alloc()
            u2 = T32.alloc()
            S.emit("dve", lambda e: e.scalar_tensor_tensor(out=T32.ap(u1)[:, 0:N], in0=psb(bk)[:, 0:N], scalar=ppc(g0),
                                                           in1=rope[:, ropei, 0, 0:N], op0=ALU.mult, op1=ALU.mult),
                   reads=[Bps[bk], Brope[ropei], B["pp"]], writes=[T32.buf(u1)])
            S.emit("dve", lambda e: e.scalar_tensor_tensor(out=T32.ap(u2)[:, 0:N], in0=psb(bsw)[:, 0:N], scalar=ppc(g1),
                                                           in1=rope[:, ropei, 1, 0:N], op0=ALU.mult, op1=ALU.mult),
                   reads=[Bps[bsw], Brope[ropei], B["pp"]], writes=[T32.buf(u2)])
            S.emit("dve", lambda e: e.tensor_tensor(out=T32.ap(u1)[:, 0:N], in0=T32.ap(u1)[:, 0:N], in1=T32.ap(u2)[:, 0:N], op=ALU.add),
                   reads=[T32.buf(u1), T32.buf(u2)], writes=[T32.buf(u1)])
            T32.release(u2)
            dst_fn(u1, rk)
            T32.release(u1)
            T32.release(rk)

        def qk_heads(kind, N, ropei, xsrc_ap, bxsrc, dst_of):
            st = qk_stage1(kind, 0, N, xsrc_ap, bxsrc)
            for h in range(NH):
                nxt = qk_stage1(kind, h + 1, N, xsrc_ap, bxsrc) if h + 1 < NH else None
                qk_stage2(kind, st, N, ropei, dst_of(h))
                st = nxt

        def make_aphase(spec):
            is_s = spec == "s"
            N = NS if is_s else TN
            tix = 0 if is_s else spec
            ucarry, hcarry = (ucarry_s, hcarry_s) if is_s else (ucarry_p, hcarry_p)
            Buc, Bhc = (B["ucarry_s"], B["hcarry_s"]) if is_s else (B["ucarry_p"], B["hcarry_p"])
            ucf = {}

            def sigmoid_from_psum(tb_, bk_):
                S.emit("act", lambda e: e.activation(out=T32.ap(tb_)[:, 0:N], in_=psb(bk_)[:, 0:N], func=AF.Exp, scale=-1.0),
                       reads=[Bps[bk_]], writes=[T32.buf(tb_)])
                S.emit("act", lambda e: e.activation(out=T32.ap(tb_)[:, 0:N], in_=T32.ap(tb_)[:, 0:N], func=AF.Ln, scale=1.0, bias=onec),
                       reads=[T32.buf(tb_), Bsm], writes=[T32.buf(tb_)])
                S.emit("act", lambda e: e.activation(out=T32.ap(tb_)[:, 0:N], in_=T32.ap(tb_)[:, 0:N], func=AF.Exp, scale=-1.0),
                       reads=[T32.buf(tb_)], writes=[T32.buf(tb_)])

            def stage_u(c):
                slot = next_slab("win_u", c)
                prefetch()
                bk = proj_fm(slot, 8, lambda k: xn[:, k, 0:N], (lambda k: [Bxn[k]]), N)
                ur = T32.alloc()
                uf = T32.alloc()
                S.emit("dve", lambda e: e.tensor_copy(out=T32.ap(ur)[:, 0:3], in_=ucarry[:, c, :]),
                       reads=[Buc], writes=[T32.buf(ur)])
                S.emit("dve", lambda e: e.tensor_copy(out=T32.ap(ur)[:, 3:3 + N], in_=psb(bk)[:, 0:N]),
                       reads=[Bps[bk]], writes=[T32.buf(ur)])
                S.emit("act", lambda e: e.activation(out=T32.ap(uf)[:, 0:N], in_=T32.ap(ur)[:, 3:3 + N], func=AF.Identity,
                                                     scale=ppc(PP_CW + 4 * c + 3), bias=ppc(PP_CB + c)),
                       reads=[T32.buf(ur), B["pp"]], writes=[T32.buf(uf)])
                for j in (2, 1, 0):
                    S.emit("dve", lambda e, j=j: e.scalar_tensor_tensor(out=T32.ap(uf)[:, 0:N], in0=T32.ap(ur)[:, j:j + N],
                                                                        scalar=ppc(PP_CW + 4 * c + j), in1=T32.ap(uf)[:, 0:N],
                                                                        op0=ALU.mult, op1=ALU.add),
                           reads=[T32.buf(ur), T32.buf(uf), B["pp"]], writes=[T32.buf(uf)])
                S.emit("dve", lambda e: e.tensor_copy(out=ucarry[:, c, :], in_=T32.ap(ur)[:, N:N + 3]),
                       reads=[T32.buf(ur)], writes=[Buc])
                S.emit("pool", lambda e: e.tensor_copy(out=b16[:, c, 0:N], in_=T32.ap(uf)[:, 0:N]),
                       reads=[T32.buf(uf)], writes=[Bb16[c]])
                T32.release(ur)
                ucf[c] = uf

            gst = {}

            def g1(c):
                slot2 = next_slab("win_g", c)
                prefetch()
                bkg = proj_fm(slot2, 8, lambda k: xn[:, k, 0:N], (lambda k: [Bxn[k]]), N)
                tg = T32.alloc()
                sigmoid_from_psum(tg, bkg)
                S.emit("dve", lambda e: e.tensor_tensor(out=T32.ap(tg)[:, 0:N], in0=T32.ap(tg)[:, 0:N], in1=psb(bkg)[:, 0:N], op=ALU.mult),
                       reads=[T32.buf(tg), Bps[bkg]], writes=[T32.buf(tg)])
                slot = next_slab("gts", c)
                prefetch()
                dks = [dk for dk in range(3) if 0 <= c + dk - 1 < NCH]
                bkr = bank()
                bki = bank()
                for g, bk in ((0, bkr), (1, bki)):
                    for n_, dk in enumerate(dks):
                        S.emit("pe", lambda e, g=g, bk=bk, dk=dk, n_=n_: e.matmul(
                            psb(bk)[:, 0:N], lhsT=ring[:, slot, (dk * 2 + g) * 128:(dk * 2 + g + 1) * 128],
                            rhs=b16[:, c + dk - 1, 0:N], start=(n_ == 0), stop=(n_ == len(dks) - 1)),
                            reads=[Bring[slot], Bb16[c + dk - 1]], writes=[Bps[bk]], sig=(n_ == len(dks) - 1))
                tr = T32.alloc()
                ti = T32.alloc()
                a = T32.alloc()
                m = T32.alloc()
                S.emit("act", lambda e: e.activation(out=T32.ap(tr)[:, 0:N], in_=psb(bkr)[:, 0:N], func=AF.Exp,
                                                     scale=-1.0, bias=smc(SM_HBR + c)),
                       reads=[Bps[bkr], Bsm], writes=[T32.buf(tr)])
                S.emit("act", lambda e: e.activation(out=T32.ap(ti)[:, 0:N], in_=psb(bki)[:, 0:N], func=AF.Exp,
                                                     scale=-1.0, bias=smc(SM_HBI + c)),
                       reads=[Bps[bki], Bsm], writes=[T32.buf(ti)])
                S.emit("act", lambda e: e.activation(out=T32.ap(tr)[:, 0:N], in_=T32.ap(tr)[:, 0:N], func=AF.Ln, scale=1.0, bias=onec),
                       reads=[T32.buf(tr), Bsm], writes=[T32.buf(tr)])
                S.emit("act", lambda e: e.activation(out=T32.ap(ti)[:, 0:N], in_=T32.ap(ti)[:, 0:N], func=AF.Ln, scale=1.0, bias=onec),
                       reads=[T32.buf(ti), Bsm], writes=[T32.buf(ti)])
                S.emit("act", lambda e: e.activation(out=T32.ap(tr)[:, 0:N], in_=T32.ap(tr)[:, 0:N], func=AF.Exp, scale=-1.0),
                       reads=[T32.buf(tr)], writes=[T32.buf(tr)])
                S.emit("act", lambda e: e.activation(out=T32.ap(ti)[:, 0:N], in_=T32.ap(ti)[:, 0:N], func=AF.Exp, scale=-1.0),
                       reads=[T32.buf(ti)], writes=[T32.buf(ti)])
                S.emit("act", lambda e: e.activation(out=T32.ap(a)[:, 0:N], in_=T32.ap(tr)[:, 0:N], func=AF.Exp, scale=smc(SM_HC + c)),
                       reads=[T32.buf(tr), Bsm], writes=[T32.buf(a)])
                T32.release(tr)
                S.emit("pool", lambda e: e.tensor_tensor(out=T32.ap(m)[:, 0:N], in0=T32.ap(a)[:, 0:N], in1=T32.ap(a)[:, 0:N], op=ALU.mult),
                       reads=[T32.buf(a)], writes=[T32.buf(m)])
                gst[c] = (tg, ti, a, m)

            def g2_act(c):
                tg, ti, a, m = gst[c]
                S.emit("act", lambda e: e.activation(out=T32.ap(m)[:, 0:N], in_=T32.ap(m)[:, 0:N], func=AF.Ln, scale=-1.0, bias=onec),
                       reads=[T32.buf(m), Bsm], writes=[T32.buf(m)])
                S.emit("act", lambda e: e.activation(out=T32.ap(m)[:, 0:N], in_=T32.ap(m)[:, 0:N], func=AF.Exp, scale=0.5),
                       reads=[T32.buf(m)], writes=[T32.buf(m)])
                if (not is_s) and tix == 0:
                    S.emit("pool", lambda e: e.memset(T32.ap(m)[:, 0:1], 1.0), writes=[T32.buf(m)])

            def g2_dve1(c):
                tg, ti, a, m = gst[c]
                uf = ucf.pop(c)
                S.emit("dve", lambda e: e.tensor_tensor(out=T32.ap(ti)[:, 0:N], in0=T32.ap(ti)[:, 0:N], in1=T32.ap(uf)[:, 0:N], op=ALU.mult),
                       reads=[T32.buf(ti), T32.buf(uf)], writes=[T32.buf(ti)])
                T32.release(uf)
                S.emit("dve", lambda e: e.tensor_tensor(out=T32.ap(ti)[:, 0:N], in0=T32.ap(ti)[:, 0:N], in1=T32.ap(m)[:, 0:N], op=ALU.mult),
                       reads=[T32.buf(ti), T32.buf(m)], writes=[T32.buf(ti)])
                T32.release(m)
                hh = T32.alloc()
                S.emit("dve", lambda e: e.tensor_tensor_scan(out=T32.ap(hh)[:, 0:N], data0=T32.ap(a)[:, 0:N], data1=T32.ap(ti)[:, 0:N],
                                                             initial=hcarry[:, c:c + 1], op0=ALU.mult, op1=ALU.add),
                       reads=[T32.buf(a), T32.buf(ti), Bhc], writes=[T32.buf(hh)])
                T32.release(a)
                T32.release(ti)
                S.emit("dve", lambda e: e.tensor_copy(out=hcarry[:, c:c + 1], in_=T32.ap(hh)[:, N - 1:N]),
                       reads=[T32.buf(hh)], writes=[Bhc])
                gst[c] = (tg, hh)

            def g2_dve2(c):
                tg, hh = gst.pop(c)
                S.emit("dve", lambda e: e.tensor_tensor(out=b16[:, 11 + c, 0:N], in0=T32.ap(hh)[:, 0:N], in1=T32.ap(tg)[:, 0:N], op=ALU.mult),
                       reads=[T32.buf(hh), T32.buf(tg)], writes=[Bb16[11 + c]])
                T32.release(hh)
                T32.release(tg)

            aout = {}

            def aout_part(cs):
                if "slots" not in aout:
                    slab_pin[0] = slab_pos[0]
                    aout["slots"] = [next_slab("wout", j) for j in range(4)]
                    prefetch()
                    aout["banks"] = [bank() for _ in range(4)]
                    bank_held.update(aout["banks"])
                for c in cs:
                    for j in range(4):
                        S.emit("pe", lambda e, c=c, j=j: e.matmul(psb(aout["banks"][j])[:, 0:N],
                                                                  lhsT=ring[:, aout["slots"][j], c * 128:(c + 1) * 128],
                                                                  rhs=b16[:, 11 + c, 0:N], start=(c == 0), stop=(c == NCH - 1)),
                               reads=[Bring[aout["slots"][j]], Bb16[11 + c]], writes=[Bps[aout["banks"][j]]], sig=(j == 3))

            def run_k(k):
                c1, c2 = k - 2, k - 3
                if 0 <= c2 < NCH:
                    g2_act(c2)
                if k < NCH:
                    stage_u(k)
                if 0 <= c2 < NCH:
                    g2_dve1(c2)
                if 0 <= c1 < NCH:
                    g1(c1)
                if 0 <= c2 < NCH:
                    g2_dve2(c2)
                if c2 == NCH - 3:
                    aout_part(range(0, NCH - 2))
                elif c2 == NCH - 2:
                    aout_part([NCH - 2])
                elif c2 == NCH - 1:
                    aout_part([NCH - 1])
                    slab_pin[0] = None

            return {"run_k": run_k, "aout": aout, "sigmoid": sigmoid_from_psum, "k_done": 0}

        aph_pending = {}
        HOIST = 3

        def run_tile(spec, next_spec, first):
            is_s = spec == "s"
            N = NS if is_s else TN
            tix = 0 if is_s else spec
            tok0 = 0 if is_s else spec * TN
            xsrc = xT_s if is_s else xT_p
            ysrc = yT_s if is_s else yT_p
            kTo = kT_s if is_s else kT_p
            vo = v_s if is_s else v_p
            ucarry, hcarry = (ucarry_s, hcarry_s) if is_s else (ucarry_p, hcarry_p)
            Buc, Bhc = (B["ucarry_s"], B["hcarry_s"]) if is_s else (B["ucarry_p"], B["hcarry_p"])
            rope_off = SEQ if is_s else tok0
            ropei = (0 if is_s else (spec + 1)) % 2
            NB = (N + 127) // 128

            S.emit("sp", lambda e: e.dma_start(out=rope[:, ropei, :, 0:N], in_=rope_in[:, :, rope_off:rope_off + N]),
                   writes=[Brope[ropei]], dsem=f"ld_rope{ropei}")
            prefetch()

            if first:
                a_norm(spec)
            aph = aph_pending.pop(spec, None) or make_aphase(spec)
            aout = aph["aout"]
            sigmoid_from_psum = aph["sigmoid"]
            for k in range(aph["k_done"], NCH + 3):
                aph["run_k"](k)

            bkn = bank()
            bank_held.add(bkn)
            for m_ in range(8):
                if m_ < 4:
                    bk = aout["banks"][m_]
                else:
                    slot = next_slab("wout", m_)
                    prefetch()
                    bk = proj_fm(slot, NCH, lambda k: b16[:, 11 + k, 0:N], [Bb16[11 + k] for k in range(NCH)], N)
                if m_ == 4:
                    for b_ in aout["banks"]:
                        bank_held.discard(b_)
                if m_ >= 1:
                    norm_sum_mm(bkn, m_ - 1, N)
                S.emit("dve", lambda e, m_=m_, bk=bk: e.tensor_tensor(out=x1[:, m_, 0:N], in0=xin[:, m_, 0:N], in1=psb(bk)[:, 0:N], op=ALU.add),
                       reads=[B["xin"], Bps[bk]], writes=[B["x1"]])
                norm_square(x1, B["x1"], m_, N)
            norm_sum_mm(bkn, 7, N)
            bank_held.discard(bkn)
            t_x1 = norm_rstd(bkn, N)
            if next_spec is not None:
                load_x(next_spec)
            if is_s:
                store(conv_s_o, ucarry_s[:].rearrange("p c j -> p (c j)"), [B["ucarry_s"]], "st_misc", [B["conv_s_o"]])
                store(h_s_o, hcarry_s[:], [B["hcarry_s"]], "st_misc", [B["h_s_o"]])
            elif tix == NT - 1:
                store(conv_p_o, ucarry_p[:].rearrange("p c j -> p (c j)"), [B["ucarry_p"]], "st_misc", [B["conv_p_o"]])
                store(h_p_o, hcarry_p[:], [B["hcarry_p"]], "st_misc", [B["h_p_o"]])
            flush_stores()

            norm_apply(x1, B["x1"], PP_KVN, t_x1, xn, Bxn, N)
            def kdst_of(h):
                def kdst(u1, rk, h=h):
                    kst = T32.alloc()
                    S.emit("dve", lambda e: e.tensor_tensor(out=T32.ap(kst)[:, 0:N], in0=T32.ap(u1)[:, 0:N], in1=T32.ap(rk)[:, 0:N], op=ALU.mult),
                           reads=[T32.buf(u1), T32.buf(rk)], writes=[T32.buf(kst)])
                    S.emit("pool", lambda e: e.tensor_copy(out=b16[:, h, 0:N], in_=T32.ap(kst)[:, 0:N]),
                           reads=[T32.buf(kst)], writes=[Bb16[h]])
                    store_sems.add(f"st32_{kst}")
                    S.emit("sp", lambda e: e.dma_start(out=kTo[h, :, tok0:tok0 + N], in_=T32.ap(kst)[:, 0:N]),
                           reads=[T32.buf(kst)], dsem=f"st32_{kst}")
                    T32.release(kst)
                return kdst
            qk_heads("k", N, ropei, xn, Bxn, kdst_of)
            if not is_s:
                S.emit("sp", lambda e: e.dma_start(out=kscr.rearrange("h p n -> p h n")[:, :, tok0:tok0 + N], in_=b16[:, 0:8, 0:N]),
                       reads=[Bb16[h] for h in range(8)], writes=[B["kscr"]], dsem="st_kscr")
            norm_apply(x1, B["x1"], PP_BN, t_x1, xn2, Bxn2, N)
            T32.release(t_x1)
            for fh in range(2):
                bks = [bank() for _ in range(NB)]
                for sp_ in range(4):
                    slot = next_slab("wv", fh * 4 + sp_)
                    prefetch()
                    for tb in range(NB):
                        nt = min(128, N - tb * 128)
                        for kk in range(2):
                            kc = 2 * sp_ + kk
                            last = (sp_ == 3 and kk == 1)
                            S.emit("pe", lambda e, tb=tb, nt=nt, kk=kk, kc=kc, last=last, slot=slot, sp_=sp_, bks=bks: e.matmul(
                                psb(bks[tb])[0:nt, :], lhsT=xn[:, kc, tb * 128: tb * 128 + nt],
                                rhs=ring[:, slot, kk * 512:(kk + 1) * 512], start=(sp_ == 0 and kk == 0), stop=last),
                                reads=[Bxn[kc], Bring[slot]], writes=[Bps[bks[tb]]], sig=(kk == 1))
                for tb in range(NB):
                    nt = min(128, N - tb * 128)
                    vs = T32.alloc()
                    S.emit("act", lambda e, tb=tb, nt=nt, vs=vs, bks=bks: e.activation(out=T32.ap(vs)[0:nt, 0:512], in_=psb(bks[tb])[0:nt, :], func=AF.Copy),
                           reads=[Bps[bks[tb]]], writes=[T32.buf(vs)])
                    S.emit("pool", lambda e, tb=tb, nt=nt, vs=vs, fh=fh: e.tensor_copy(
                        out=b16[0:nt, 11 + 4 * fh: 11 + 4 * fh + 4, tb * 128:(tb + 1) * 128],
                        in_=T32.ap(vs)[0:nt, 0:512].rearrange("p (h e) -> p h e", e=128)),
                        reads=[T32.buf(vs)], writes=[Bb16[11 + 4 * fh + i] for i in range(4)])
                    store_sems.add(f"st32_{vs}")
                    S.emit("sp", lambda e, tb=tb, nt=nt, vs=vs, fh=fh: e.dma_start(
                        out=vo[tok0 + tb * 128: tok0 + tb * 128 + nt, fh * 512:(fh + 1) * 512], in_=T32.ap(vs)[0:nt, 0:512]),
                        reads=[T32.buf(vs)], dsem=f"st32_{vs}")
                    T32.release(vs)
            if not is_s:
                S.emit("sp", lambda e: e.dma_start(
                    out=vscr.rearrange("h p kb e -> p h kb e")[:, :, 4 * tix:4 * tix + 4, :],
                    in_=b16[:, 11:19, :].rearrange("p h (tb e) -> p h tb e", e=128)),
                    reads=[Bb16[11 + h] for h in range(8)], writes=[B["vscr"]], dsem="st_vscr")

            nkeys_past = PAST if is_s else (tix + 1) * TN

            def load_head(h):
                bi = h % 2
                if is_s:
                    S.emit("pool", lambda e: e.dma_start(out=kTh[:, bi, 0:PAST], in_=ckT[h]),
                           writes=[BkTh[bi]], dsem=f"ldp_k{bi}")
                    S.emit("pool", lambda e: e.dma_start(
                        out=vh[:, bi, 0:8, :], in_=cv.rearrange("(kb p) (h e) -> h p kb e", p=128, e=128)[h]),
                        writes=[Bvh[bi]], dsem=f"ldp_v{bi}")
                else:
                    S.emit("sp", lambda e: e.dma_start(out=kTh[:, bi, 0:nkeys_past], in_=kscr[h, :, 0:nkeys_past]),
                           reads=[B["kscr"]], writes=[BkTh[bi]], dsem=f"ld_k{bi}")
                    S.emit("sp", lambda e: e.dma_start(out=vh[:, bi, 0:4 * (tix + 1), :], in_=vscr[h, :, 0:4 * (tix + 1), :]),
                           reads=[B["vscr"]], writes=[Bvh[bi]], dsem=f"ld_v{bi}")

            def finish_head_load(h):
                bi = h % 2
                if is_s:
                    S.emit("pool", lambda e: e.tensor_copy(out=kTh[:, bi, PAST:PAST + NS], in_=b16[:, h, 0:NS]),
                           reads=[Bb16[h]], writes=[BkTh[bi]])
                    S.emit("pool", lambda e: e.tensor_copy(out=vh[0:NS, bi, 8, :], in_=b16[0:NS, 11 + h, 0:128]),
                           reads=[Bb16[11 + h]], writes=[Bvh[bi]])

            if is_s:
                pass

            if is_s:
                qbase, gbase = 8, 19
            if is_s:
                def qT_ap(h):
                    return kTh[:, 0, 2048 + h * NS: 2048 + (h + 1) * NS]

                def sg_ap(m_):
                    return kTh[:, 0, 3072 + m_ * NS: 3072 + (m_ + 1) * NS]
                BqT = [Buf(f"qTs{h}") for h in range(8)]
                Bsg = [Buf(f"sgs{h}") for h in range(8)]
            else:
                def qT_ap(h):
                    return b16[:, h, 0:N]

                def sg_ap(m_):
                    return b16[:, 11 + m_, 0:N]
                BqT = [Bb16[h] for h in range(8)]
                Bsg = [Bb16[11 + h] for h in range(8)]

            if is_s:
                pass

            def qdst_of(h):
                def qdst(u1, rk, h=h):
                    S.emit("dve", lambda e: e.tensor_tensor(out=qT_ap(h), in0=T32.ap(u1)[:, 0:N], in1=T32.ap(rk)[:, 0:N], op=ALU.mult),
                           reads=[T32.buf(u1), T32.buf(rk)], writes=[BqT[h]])
                return qdst
            qk_heads("q", N, ropei, xn2, Bxn2, qdst_of)
            for m_ in range(8):
                slot = next_slab("wbg", m_)
                prefetch()
                bkg = proj_fm(slot, 8, lambda k: xn2[:, k, 0:N], (lambda k: [Bxn2[k]]), N)
                tg = T32.alloc()
                sigmoid_from_psum(tg, bkg)
                S.emit("dve", lambda e, tg=tg, bkg=bkg, m_=m_: e.tensor_tensor(out=sg_ap(m_), in0=T32.ap(tg)[:, 0:N], in1=psb(bkg)[:, 0:N], op=ALU.mult),
                       reads=[T32.buf(tg), Bps[bkg]], writes=[Bsg[m_]])
                T32.release(tg)
            flush_stores()

            if next_spec is not None:
                a_norm(next_spec)
            def blocks():
                L = []
                if is_s:
                    for kb_ in range(8):
                        L.append((kb_ * 128, 128, kb_, 0, False))
                    L.append((PAST, NS, 8, 0, False))
                else:
                    for kd in range(4):
                        L.append(((4 * tix + kd) * 128, 128, 4 * tix + kd, 128 * kd, True))
                    for kb_ in range(4 * tix):
                        L.append((kb_ * 128, 128, kb_, 0, False))
                return L

            blks = blocks()
            nblk = len(blks)
            steps = [(h, i) for h in range(NH) for i in range(nblk)]
            O1, O2, L1, L2 = 4, 5, 6, 7
            load_head(0)
            finish_head_load(0)
            load_head(1)
            finish_head_load(1)

            def emit_qk(s):
                h, i = steps[s]
                col0, nk, vb, q0, corner = blks[i]
                sp_ = s % 2
                bi = h % 2
                S.emit("pe", lambda e: e.matmul(psum[sp_][0:nk, 0, q0:N], lhsT=kTh[0:64, bi, col0:col0 + nk],
                                                rhs=qT_ap(h)[0:64, q0:N] if not is_s else qT_ap(h)[0:64, :], start=True, stop=True),
                       reads=[BkTh[bi], BqT[h]], writes=[Bps[2 * sp_]], sig=False)
                S.emit("pe", lambda e: e.matmul(psum[sp_][0:nk, 1, q0:N], lhsT=kTh[64:128, bi, col0:col0 + nk],
                                                rhs=qT_ap(h)[64:128, q0:N] if not is_s else qT_ap(h)[64:128, :], start=True, stop=True),
                       reads=[BkTh[bi], BqT[h]], writes=[Bps[2 * sp_ + 1]])

            def emit_exp_pv(s):
                h, i = steps[s]
                col0, nk, vb, q0, corner = blks[i]
                sp_ = s % 2
                pi = s % 3
                bi = h % 2
                first = (i == 0)
                last = (i == nblk - 1)
                S.emit("act", lambda e: e.activation(out=pt[0:nk, pi, :, q0:N], in_=psum[sp_][0:nk, :, q0:N], func=AF.Exp, scale=0.125),
                       reads=[Bps[2 * sp_], Bps[2 * sp_ + 1]], writes=[Bpt[pi]])
                if corner:
                    S.emit("pool", lambda e: e.memset(pt[64:128, pi, :, q0:q0 + 64], 0.0), writes=[Bpt[pi]])
                for c_, ob in ((0, O1), (1, O2)):
                    S.emit("pe", lambda e, c_=c_, ob=ob: e.matmul(psb(ob)[:, q0:N], lhsT=vh[0:nk, bi, vb, :], rhs=pt[0:nk, pi, c_, q0:N],
                                                                   start=first, stop=last),
                           reads=[Bvh[bi], Bpt[pi]], writes=[Bps[ob]], sig=False)
                for c_, lb in ((0, L1), (1, L2)):
                    S.emit("pe", lambda e, c_=c_, lb=lb: e.matmul(psb(lb)[:, q0:N], lhsT=ones_b[0:nk, :], rhs=pt[0:nk, pi, c_, q0:N],
                                                                   start=first, stop=last),
                           reads=[B["cmb"], Bpt[pi]], writes=[Bps[lb]], sig=(c_ == 1))

            fin_state = {}

            def fin_a(h):
                r1 = T32.alloc()
                r2 = T32.alloc()
                o1 = T32.alloc()
                o2 = T32.alloc()
                S.emit("dve", lambda e: e.tensor_copy(out=T32.ap(o1)[:, 0:N], in_=psb(O1)[:, 0:N]), reads=[Bps[O1]], writes=[T32.buf(o1)])
                S.emit("dve", lambda e: e.tensor_copy(out=T32.ap(o2)[:, 0:N], in_=psb(O2)[:, 0:N]), reads=[Bps[O2]], writes=[T32.buf(o2)])
                S.emit("act", lambda e: e.activation(out=T32.ap(r1)[:, 0:N], in_=psb(L1)[:, 0:N], func=AF.Ln), reads=[Bps[L1]], writes=[T32.buf(r1)])
                S.emit("act", lambda e: e.activation(out=T32.ap(r2)[:, 0:N], in_=psb(L2)[:, 0:N], func=AF.Ln), reads=[Bps[L2]], writes=[T32.buf(r2)])
                fin_state[h] = (r1, r2, o1, o2)

            def fin_a2(h):
                r1, r2, o1, o2 = fin_state.pop(h)
                S.emit("act", lambda e: e.activation(out=T32.ap(r1)[:, 0:N], in_=T32.ap(r1)[:, 0:N], func=AF.Exp, scale=-1.0),
                       reads=[T32.buf(r1)], writes=[T32.buf(r1)])
                S.emit("act", lambda e: e.activation(out=T32.ap(r2)[:, 0:N], in_=T32.ap(r2)[:, 0:N], func=AF.Exp, scale=-1.0),
                       reads=[T32.buf(r2)], writes=[T32.buf(r2)])
                S.emit("dve", lambda e: e.tensor_tensor(out=T32.ap(r1)[:, 0:N], in0=T32.ap(o1)[:, 0:N], in1=T32.ap(r1)[:, 0:N], op=ALU.mult),
                       reads=[T32.buf(o1), T32.buf(r1)], writes=[T32.buf(r1)])
                S.emit("dve", lambda e: e.tensor_tensor(out=T32.ap(r2)[:, 0:N], in0=T32.ap(o2)[:, 0:N], in1=T32.ap(r2)[:, 0:N], op=ALU.mult),
                       reads=[T32.buf(o2), T32.buf(r2)], writes=[T32.buf(r2)])
                T32.release(o1)
                T32.release(o2)
                S.emit("dve", lambda e: e.scalar_tensor_tensor(out=T32.ap(r1)[:, 0:N], in0=T32.ap(r2)[:, 0:N], scalar=smc(SM_NLAM),
                                                               in1=T32.ap(r1)[:, 0:N], op0=ALU.mult, op1=ALU.add),
                       reads=[T32.buf(r1), T32.buf(r2), Bsm], writes=[T32.buf(r1)])
                T32.release(r2)
                sq = T16.alloc()
                S.emit("dve", lambda e: e.tensor_tensor(out=T16.ap(sq)[:, 0:N], in0=T32.ap(r1)[:, 0:N], in1=T32.ap(r1)[:, 0:N], op=ALU.mult),
                       reads=[T32.buf(r1)], writes=[T16.buf(sq)])
                fin_state[h] = (r1, sq)

            def fin_b(h, bk):
                r1, sq = fin_state.pop(h)
                S.emit("pe", lambda e: e.matmul(psb(bk)[:, 0:N], lhsT=ones_b, rhs=T16.ap(sq)[:, 0:N], start=True, stop=True),
                       reads=[T16.buf(sq), B["cmb"]], writes=[Bps[bk]])
                T16.release(sq)
                rh = T32.alloc()
                S.emit("act", lambda e: e.activation(out=T32.ap(rh)[:, 0:N], in_=psb(bk)[:, 0:N], func=AF.Ln, scale=1.0 / 128, bias=epsc),
                       reads=[Bps[bk], Bsm], writes=[T32.buf(rh)])
                S.emit("act", lambda e: e.activation(out=T32.ap(rh)[:, 0:N], in_=T32.ap(rh)[:, 0:N], func=AF.Exp, scale=-0.5),
                       reads=[T32.buf(rh)], writes=[T32.buf(rh)])
                S.emit("dve", lambda e: e.scalar_tensor_tensor(out=T32.ap(r1)[:, 0:N], in0=T32.ap(r1)[:, 0:N], scalar=smc(SM_HGS),
                                                               in1=T32.ap(rh)[:, 0:N], op0=ALU.mult, op1=ALU.mult),
                       reads=[T32.buf(r1), T32.buf(rh), Bsm], writes=[T32.buf(r1)])
                S.emit("dve", lambda e: e.tensor_tensor(out=on[:, h, 0:N], in0=T32.ap(r1)[:, 0:N], in1=sg_ap(h), op=ALU.mult),
                       reads=[T32.buf(r1), Bsg[h]], writes=[Bon[h]])
                T32.release(r1)
                T32.release(rh)

            ns = len(steps)
            emit_qk(0)
            if ns > 1:
                emit_qk(1)
            pend_a2 = None
            pend_b = None
            for s in range(ns):
                h, i = steps[s]
                emit_exp_pv(s)
                if pend_a2 is not None and s >= pend_a2[1]:
                    fin_a2(pend_a2[0])
                    pend_b = (pend_a2[0], s + 2)
                    pend_a2 = None
                elif pend_b is not None and s >= pend_b[1]:
                    fin_b(pend_b[0], 2 * (s % 2))
                    pend_b = None
                if i == nblk - 1:
                    fin_a(h)
                    pend_a2 = (h, s + 1)
                    if h + 2 < NH:
                        load_head(h + 2)
                        finish_head_load(h + 2)
                if s + 2 < ns:
                    emit_qk(s + 2)
            if pend_a2 is not None:
                fin_a2(pend_a2[0])
                pend_b = (pend_a2[0], 0)
            if pend_b is not None:
                fin_b(pend_b[0], 0)
            if next_spec is not None and HOIST > 0:
                aphn = make_aphase(next_spec)
                for k in range(HOIST):
                    aphn["run_k"](k)
                aphn["k_done"] = HOIST
                aph_pending[next_spec] = aphn

            for m_ in range(8):
                slot = next_slab("wbo", m_)
                prefetch()
                bk = proj_fm(slot, 8, lambda k: on[:, k, 0:N], Bon, N)
                ys = T32.alloc()
                S.emit("dve", lambda e, m_=m_, bk=bk, ys=ys: e.tensor_tensor(out=T32.ap(ys)[:, 0:N], in0=x1[:, m_, 0:N], in1=psb(bk)[:, 0:N], op=ALU.add),
                       reads=[B["x1"], Bps[bk]], writes=[T32.buf(ys)])
                store_sems.add(f"st32_{ys}")
                S.emit("sp", lambda e, m_=m_, ys=ys: e.dma_start(out=ysrc[m_ * 128:(m_ + 1) * 128, tok0:tok0 + N], in_=T32.ap(ys)[:, 0:N]),
                       reads=[T32.buf(ys)], dsem=f"st32_{ys}")
                T32.release(ys)

        for i_, spec in enumerate(tile_specs):
            run_tile(spec, tile_specs[i_ + 1] if i_ + 1 < len(tile_specs) else None, i_ == 0)
        flush_stores()
        S.final_waits("sp", sorted(store_sems) + ["pe", "act", "dve", "pool"])

        block = es.enter_context(nc.Block())
        S.replay(block)
    return nc


def _tile_w(W, nk, nm):
    return np.ascontiguousarray(W.reshape(nk, 128, nm, 128).transpose(2, 1, 0, 3).reshape(nm, 128, nk * 128))


def _chunkcol(v):
    return np.ascontiguousarray(v.reshape(-1, 128).T)


def _prep_shared(inp):
    f = np.float32
    a_w_in = np.asarray(inp["a_w_in"][0], f)
    shared = {}
    shared["win_u"] = _tile_w(a_w_in[:, :DR], 8, NCH)
    shared["win_g"] = _tile_w(a_w_in[:, DR:], 8, NCH)
    gts = np.zeros((NCH, 128, 3, 2, 128), f)
    for g, key in enumerate(("a_gate_r_w", "a_gate_i_w")):
        Wg = np.asarray(inp[key][0], f)
        Dm = np.zeros((DR, DR), f)
        for n in range(16):
            Dm[88 * n:88 * n + 88, 88 * n:88 * n + 88] = Wg[n]
        for c in range(NCH):
            for dk in range(3):
                kc = c + dk - 1
                if 0 <= kc < NCH:
                    gts[c, :, dk, g, :] = Dm[kc * 128:(kc + 1) * 128, c * 128:(c + 1) * 128]
    shared["gts"] = gts.reshape(NCH, 128, 768)
    shared["wout"] = _tile_w(np.asarray(inp["a_w_out"][0], f), NCH, 8)
    kv_w = np.asarray(inp["kv_w"], f)
    shared["wk"] = _tile_w(kv_w[:, :D], 8, 8)
    Wv = kv_w[:, D:]
    wv = Wv.reshape(4, 2, 128, 2, 512).transpose(3, 0, 2, 1, 4)
    shared["wv"] = np.ascontiguousarray(wv.reshape(8, 128, 1024))
    b_w_in = np.asarray(inp["b_w_in"][0], f)
    shared["wq"] = _tile_w(b_w_in[:, :D], 8, 8)
    shared["wbg"] = _tile_w(b_w_in[:, D:], 8, 8)
    shared["wbo"] = _tile_w(np.asarray(inp["b_w_out"][0], f), 8, 8)
    pp = np.zeros((128, PP_N), f)
    pp[:, PP_AN:PP_AN + 8] = _chunkcol(np.asarray(inp["a_norm"][0], f))
    pp[:, PP_KVN:PP_KVN + 8] = _chunkcol(np.asarray(inp["kv_norm"], f))
    pp[:, PP_BN:PP_BN + 8] = _chunkcol(np.asarray(inp["b_norm"][0], f))
    cw = np.asarray(inp["a_conv_w"][0], f)
    pp[:, PP_CW:PP_CW + 44] = cw.reshape(4, NCH, 128).transpose(2, 1, 0).reshape(128, 44)
    pp[:, PP_CB:PP_CB + 11] = _chunkcol(np.asarray(inp["a_conv_b"][0], f))
    pp[:, PP_BR:PP_BR + 11] = _chunkcol(np.asarray(inp["a_gate_r_b"][0], f))
    pp[:, PP_BI:PP_BI + 11] = _chunkcol(np.asarray(inp["a_gate_i_b"][0], f))
    pp[:, PP_LAM:PP_LAM + 11] = _chunkcol(np.asarray(inp["a_lambda"][0], f))
    p = np.arange(128)
    kn = np.asarray(inp["k_norm"], f)
    qn = np.asarray(inp["b_q_norm"][0], f)
    pp[:, PP_GK] = kn[p % 64]
    pp[:, PP_GKS] = kn[(p ^ 32) % 64]
    pp[:, PP_GQ] = qn[p % 64]
    pp[:, PP_GQS] = qn[(p ^ 32) % 64]
    pp[:, PP_HG] = np.asarray(inp["b_head_norm"][0], f)
    pp[:, PP_LQ1:PP_LQ1 + 64] = np.asarray(inp["b_lambda_q1"][0], f)[None, :]
    pp[:, PP_LK1:PP_LK1 + 64] = np.asarray(inp["b_lambda_k1"][0], f)[None, :]
    pp[:, PP_LQ2:PP_LQ2 + 64] = np.asarray(inp["b_lambda_q2"][0], f)[None, :]
    pp[:, PP_LK2:PP_LK2 + 64] = np.asarray(inp["b_lambda_k2"][0], f)[None, :]
    shared["pp"] = pp
    cm = np.zeros((128, 384), f)
    cm[:, 0:128] = 1.0
    cm[:, 128:256] = (p[:, None] // 64 == p[None, :] // 64).astype(f)
    cm[:, 256:384] = (p[:, None] == (p[None, :] ^ 32)).astype(f)
    shared["cmat"] = cm
    half = 32
    inv = (np.float32(10000.0) ** (-np.arange(half, dtype=f) / np.float32(half))).astype(f)
    pos = np.concatenate([np.arange(SEQ), PAST + np.arange(NS)]).astype(f)
    ang = (pos[:, None] * inv[None, :]).astype(f)
    cos = np.cos(ang).astype(f).T
    sin = np.sin(ang).astype(f).T
    fi = p % 32
    sign = np.where((p % 64) < 32, -1.0, 1.0).astype(f)
    rope = np.empty((128, 2, SEQ + NS), f)
    rope[:, 0, :] = cos[fi]
    rope[:, 1, :] = sin[fi] * sign[:, None]
    shared["rope"] = rope
    return shared


_PROG = None


def kernel(**inp):
    global _PROG
    f = np.float32
    shared = _prep_shared(inp)
    x_prompt = np.asarray(inp["x_prompt"], f)
    x_sample = np.asarray(inp["x_sample"], f)
    state_conv = np.asarray(inp["state_conv"], f)
    state_h = np.asarray(inp["state_h"], f)
    cache_k = np.asarray(inp["cache_k"], f)
    cache_v = np.asarray(inp["cache_v"], f)
    in_maps = []
    for b in range(8):
        m = dict(shared)
        m["xT_p"] = np.ascontiguousarray(x_prompt[b].T)
        m["xT_s"] = np.ascontiguousarray(x_sample[b].T)
        m["conv_s_in"] = np.ascontiguousarray(state_conv[0, b].reshape(3, NCH, 128).transpose(2, 1, 0).reshape(128, NCH * 3))
        m["h_s_in"] = _chunkcol(state_h[0, b])
        m["ckT"] = np.ascontiguousarray(cache_k[b].transpose(1, 2, 0))
        m["cv"] = np.ascontiguousarray(cache_v[b].reshape(PAST, D))
        in_maps.append(m)
    if _PROG is None:
        _PROG = build_program()
    res = run_bass_kernel_spmd(_PROG, in_maps, core_ids=list(range(8)))
    R = res.results

    def unconv(a):
        return a.reshape(128, NCH, 3).transpose(2, 1, 0).reshape(3, DR)

    def unh(a):
        return a.T.reshape(DR)

    y_p = np.stack([R[b]["yT_p"].T for b in range(8)]).astype(f)
    y_s = np.stack([R[b]["yT_s"].T for b in range(8)]).astype(f)
    conv_p = np.stack([unconv(R[b]["conv_p_o"]) for b in range(8)])[None].astype(f)
    h_p = np.stack([unh(R[b]["h_p_o"]) for b in range(8)])[None].astype(f)
    k_p = np.stack([R[b]["kT_p"].transpose(2, 0, 1) for b in range(8)]).astype(f)
    v_p = np.stack([R[b]["v_p"].reshape(SEQ, NH, 128) for b in range(8)]).astype(f)
    conv_s = np.stack([unconv(R[b]["conv_s_o"]) for b in range(8)])[None].astype(f)
    h_s = np.stack([unh(R[b]["h_s_o"]) for b in range(8)])[None].astype(f)
    k_s = np.stack([R[b]["kT_s"].transpose(2, 0, 1) for b in range(8)]).astype(f)
    v_s = np.stack([R[b]["v_s"].reshape(NS, NH, 128) for b in range(8)]).astype(f)
    return (np.ascontiguousarray(y_p), np.ascontiguousarray(y_s), np.ascontiguousarray(conv_p), np.ascontiguousarray(h_p),
            np.ascontiguousarray(k_p), np.ascontiguousarray(v_p), np.ascontiguousarray(conv_s), np.ascontiguousarray(h_s),
            np.ascontiguousarray(k_s), np.ascontiguousarray(v_s))
```

```python
import math
from contextlib import ExitStack

import numpy as np
import concourse.bass as bass
import concourse.mybir as mybir
from concourse.bass_utils import run_bass_kernel_spmd

F32 = mybir.dt.float32
BF16 = mybir.dt.bfloat16
AF = mybir.ActivationFunctionType
ALU = mybir.AluOpType
AX = mybir.AxisListType

D = 1024
SEQ = 4096
NS = 16
PAST = 1024
DR = 1408
NCH = 11
NH = 8
EPS = 1e-6
LAM_INIT = 0.8 - 0.6 * math.exp(-0.3 * 1)
TN = 512
LEAD = 3
NT = SEQ // TN
RING = 8
K32 = 24
K16 = 4

PP_AN, PP_KVN, PP_BN = 0, 8, 16
PP_CW, PP_CB, PP_BR, PP_BI, PP_LAM = 24, 68, 79, 90, 101
PP_GK, PP_GKS, PP_GQ, PP_GQS, PP_HG = 112, 113, 114, 115, 116
PP_LQ1, PP_LK1, PP_LQ2, PP_LK2 = 117, 181, 245, 309
PP_N = 373


class Buf:
    __slots__ = ("name", "w", "r", "excl")

    def __init__(self, name, excl=False):
        self.name = name
        self.w = {}
        self.r = {}
        self.excl = excl


class Sched:
    CE = ("pe", "act", "dve", "pool")

    def __init__(self, nc, es):
        self.nc = nc
        self.es = es
        self.streams = {k: [] for k in ("pe", "act", "dve", "pool", "sp")}
        self.sems = {}
        self.cnt = {}
        self.waited = {k: {} for k in self.streams}
        for e in self.CE:
            self._sem(e)

    def _sem(self, name):
        if name not in self.sems:
            self.sems[name] = self.es.enter_context(self.nc.semaphore("s_" + name))
            self.cnt[name] = 0
        return self.sems[name]

    def emit(self, eng, fn, reads=(), writes=(), sig=True, dsem=None):
        need = {}

        def add(s, v, kind):
            if dsem is None and s == eng:
                if eng == "pe" or kind != "raw":
                    return
            if need.get(s, 0) < v:
                need[s] = v

        for b in reads:
            for s, v in b.w.items():
                add(s, v, "raw")
            if b.excl:
                for s, v in b.r.items():
                    add(s, v, "rar")
        for b in writes:
            for s, v in b.w.items():
                add(s, v, "waw")
            for s, v in b.r.items():
                add(s, v, "war")
        waits = []
        wd = self.waited[eng]
        for s, v in need.items():
            if wd.get(s, 0) < v:
                assert self.cnt[s] >= v, f"wait on future event {s}>={v} (cnt {self.cnt[s]}) from {eng}"
                wd[s] = v
                waits.append((s, v))
        if dsem is not None:
            self._sem(dsem)
            self.cnt[dsem] += 16
            ev = (dsem, self.cnt[dsem])
            sg = (dsem, 16)
        elif sig:
            self.cnt[eng] += 1
            ev = (eng, self.cnt[eng])
            sg = (eng, 1)
        else:
            ev = (eng, self.cnt[eng] + 1)
            sg = None
        self.streams[eng].append((waits, fn, sg))
        for b in writes:
            b.w = {ev[0]: ev[1]}
            b.r = {}
        for b in reads:
            if b.r.get(ev[0], 0) < ev[1]:
                b.r[ev[0]] = ev[1]
        return ev

    def final_waits(self, eng, names):
        waits = [(s, self.cnt[s]) for s in names if self.cnt.get(s, 0) > 0]
        self.streams[eng].append((waits, None, None))

    def replay(self, block):
        def run(stream):
            def body(e):
                for waits, fn, sg in stream:
                    for s, v in waits:
                        e.wait_ge(self.sems[s], v)
                    if fn is None:
                        continue
                    ins = fn(e)
                    if sg is not None:
                        ins.then_inc(self.sems[sg[0]], sg[1])
            return body

        block.tensor(run(self.streams["pe"]))
        block.scalar(run(self.streams["act"]))
        block.vector(run(self.streams["dve"]))
        block.gpsimd(run(self.streams["pool"]))
        block.sync(run(self.streams["sp"]))


class Pool32:
    def __init__(self, aps, name):
        self.items = [(ap, Buf(f"{name}{i}")) for i, ap in enumerate(aps)]
        self.free = list(range(len(aps)))

    def alloc(self):
        assert self.free, "temp pool exhausted"
        i = self.free.pop(0)
        return i

    def ap(self, i):
        return self.items[i][0]

    def buf(self, i):
        return self.items[i][1]

    def release(self, i):
        self.free.append(i)


def build_program():
    nc = bass.Bass("TRN2", target_bir_lowering=False)

    def din(name, shape, dt=F32):
        return nc.dram_tensor(name, list(shape), dt, kind="ExternalInput").ap()

    def dout(name, shape, dt=F32):
        return nc.dram_tensor(name, list(shape), dt, kind="ExternalOutput").ap()

    def dint(name, shape, dt=BF16):
        return nc.dram_tensor(name, list(shape), dt, kind="Internal").ap()

    xT_p = din("xT_p", [D, SEQ])
    xT_s = din("xT_s", [D, NS])
    conv_s_in = din("conv_s_in", [128, NCH * 3])
    h_s_in = din("h_s_in", [128, NCH])
    ckT = din("ckT", [NH, 128, PAST])
    cv = din("cv", [PAST, D])
    wnames = [("win_u", NCH, 1024), ("win_g", NCH, 1024), ("gts", NCH, 768), ("wout", 8, 1408),
              ("wk", 8, 1024), ("wv", 8, 1024), ("wq", 8, 1024), ("wbg", 8, 1024), ("wbo", 8, 1024)]
    wf32 = {n: din(n, [ns, 128, w]) for n, ns, w in wnames}
    wb16 = {n: dint(n + "_b", [ns, 128, w]) for n, ns, w in wnames}
    wwidth = {n: w for n, ns, w in wnames}
    pp_in = din("pp", [128, PP_N])
    cmat_in = din("cmat", [128, 384])
    rope_in = din("rope", [128, 2, SEQ + NS])

    yT_p = dout("yT_p", [D, SEQ])
    yT_s = dout("yT_s", [D, NS])
    conv_p_o = dout("conv_p_o", [128, NCH * 3])
    h_p_o = dout("h_p_o", [128, NCH])
    kT_p = dout("kT_p", [NH, 128, SEQ])
    v_p = dout("v_p", [SEQ, D])
    conv_s_o = dout("conv_s_o", [128, NCH * 3])
    h_s_o = dout("h_s_o", [128, NCH])
    kT_s = dout("kT_s", [NH, 128, NS])
    v_s = dout("v_s", [NS, D])

    kscr = dint("kscr", [NH, 128, SEQ])
    vscr = dint("vscr", [NH, 128, SEQ // 128, 128])

    with ExitStack() as es:
        S = Sched(nc, es)

        def sb(name, shape, dt):
            return es.enter_context(nc.sbuf_tensor(name, list(shape), dt))

        xin = sb("xin", [128, 8, TN], F32)
        x1 = sb("x1", [128, 8, TN], F32)
        xn = sb("xn", [128, 8, TN], BF16)
        ring = sb("ring", [128, RING, 1408], BF16)
        kTh = sb("kTh", [128, 2, SEQ], BF16)
        vh = sb("vh", [128, 2, SEQ // 128, 128], BF16)
        pp = sb("pp_sb", [128, PP_N], F32)
        cmf = sb("cmf", [128, 384], F32)
        cmb = sb("cmb", [128, 384], BF16)
        xn2 = sb("xn2", [128, 8, TN], BF16)
        rope = sb("rope_sb", [128, 2, 2, TN], F32)
        t32 = sb("t32", [128, K32, TN + 4], F32)
        b16 = sb("b16", [128, 22, TN], BF16)
        on = sb("on", [128, 8, TN], BF16)
        pt = sb("pt", [128, 3, 2, TN], BF16)
        t16 = sb("t16", [128, K16, TN], BF16)
        small = sb("small", [128, 64], F32)
        lprod = sb("lprod", [128, 128], F32)
        ucarry_p = sb("ucarry_p", [128, NCH, 3], F32)
        hcarry_p = sb("hcarry_p", [128, NCH], F32)
        ucarry_s = sb("ucarry_s", [128, NCH, 3], F32)
        hcarry_s = sb("hcarry_s", [128, NCH], F32)
        psum = [es.enter_context(nc.psum_tensor(f"ps{i}", [128, 2, 512], F32)) for i in range(4)]

        B = {n: Buf(n) for n in ("xin", "x1", "xn", "xn2", "pp", "cmf", "cmb", "consts", "small", "lprod", "on",
                                 "ucarry_p", "hcarry_p", "ucarry_s", "hcarry_s", "kscr", "vscr",
                                 "conv_p_o", "h_p_o", "conv_s_o", "h_s_o")}
        Bring = [Buf(f"ring{i}") for i in range(RING)]
        BkTh = [Buf("kTh0"), Buf("kTh1")]
        Bvh = [Buf("vh0"), Buf("vh1")]
        Brope = [Buf("rope0"), Buf("rope1")]
        Bb16 = [Buf(f"b16_{i}") for i in range(22)]
        Bpt = [Buf(f"pt{i}") for i in range(3)]
        Bps = [Buf(f"psb{i}", excl=True) for i in range(8)]
        Bon = [Buf(f"on{i}") for i in range(8)]
        Bxn = [Buf(f"xn{i}") for i in range(8)]
        Bxn2 = [Buf(f"xn2_{i}") for i in range(8)]
        Bw = {n: Buf("w_" + n) for n, _, _ in wnames}
        T32 = Pool32([t32[:, i, :] for i in range(K32)], "t32_")
        T16 = Pool32([t16[:, i, :] for i in range(K16)], "t16_")
        store_sems = set()

        def psb(i):
            return psum[i // 2][:, i % 2, :]

        bank_rr = [0]
        bank_held = set()

        def bank():
            while True:
                i = bank_rr[0]
                bank_rr[0] = (i + 1) % 8
                if i not in bank_held:
                    return i

        ones_b = cmb[:, 0:128]
        blk_b = cmb[:, 128:256]
        perm_b = cmb[:, 256:384]

        def ppc(i):
            return pp[:, i:i + 1]

        def smc(i):
            return small[:, i:i + 1]
        SM_HBR, SM_HBI, SM_HC, SM_NLAM, SM_HGS = 0, 11, 22, 33, 34
        epsc = small[:, 62:63]
        onec = small[:, 63:64]

        S.emit("sp", lambda e: e.dma_start(out=pp[:], in_=pp_in), writes=[B["pp"]], dsem="ld_pp")
        S.emit("sp", lambda e: e.dma_start(out=cmf[:], in_=cmat_in), writes=[B["cmf"]], dsem="ld_cm")
        S.emit("sp", lambda e: e.dma_start(out=xin[:, :, 0:TN], in_=xT_p.rearrange("(kc p) n -> p kc n", p=128)[:, :, 0:TN]),
               writes=[B["xin"]], dsem="ld_x")
        for n, ns, w in wnames:
            S.emit("pool", lambda e, n=n: e.dma_start(out=wb16[n], in_=wf32[n]), reads=[B["xin"], B["pp"], B["cmf"]],
                   writes=[Bw[n]], dsem="cv_" + n)
        S.emit("pool", lambda e: e.tensor_copy(out=cmb[:], in_=cmf[:]), reads=[B["cmf"]], writes=[B["cmb"]])
        S.emit("pool", lambda e: e.memset(ucarry_p[:], 0.0), writes=[B["ucarry_p"]])
        S.emit("pool", lambda e: e.memset(hcarry_p[:], 0.0), writes=[B["hcarry_p"]])
        S.emit("sp", lambda e: e.dma_start(out=ucarry_s[:].rearrange("p c j -> p (c j)"), in_=conv_s_in),
               writes=[B["ucarry_s"]], dsem="ld_cs")
        S.emit("sp", lambda e: e.dma_start(out=hcarry_s[:], in_=h_s_in), writes=[B["hcarry_s"]], dsem="ld_hs")
        Bsm = B["small"]
        S.emit("pool", lambda e: e.memset(small[:, 62:63], EPS), writes=[Bsm])
        S.emit("pool", lambda e: e.memset(small[:, 63:64], 1.0), writes=[Bsm])
        S.emit("dve", lambda e: e.tensor_scalar(out=small[:, SM_HBR:SM_HBR + 11], in0=pp[:, PP_BR:PP_BR + 11],
                                                scalar1=-1.0, scalar2=None, op0=ALU.mult),
               reads=[B["pp"]], writes=[Bsm])
        S.emit("dve", lambda e: e.tensor_scalar(out=small[:, SM_HBI:SM_HBI + 11], in0=pp[:, PP_BI:PP_BI + 11],
                                                scalar1=-1.0, scalar2=None, op0=ALU.mult),
               reads=[B["pp"]], writes=[Bsm])
        S.emit("act", lambda e: e.activation(out=small[:, 35:46], in_=pp[:, PP_LAM:PP_LAM + 11], func=AF.Abs),
               reads=[B["pp"]], writes=[Bsm])
        S.emit("act", lambda e: e.activation(out=small[:, 35:46], in_=small[:, 35:46], func=AF.Exp, scale=-1.0),
               reads=[Bsm], writes=[Bsm])
        S.emit("act", lambda e: e.activation(out=small[:, 35:46], in_=small[:, 35:46], func=AF.Ln, bias=onec, scale=1.0),
               reads=[Bsm], writes=[Bsm])
        S.emit("dve", lambda e: e.tensor_scalar(out=small[:, 46:57], in0=pp[:, PP_LAM:PP_LAM + 11],
                                                scalar1=-1.0, scalar2=0.0, op0=ALU.mult, op1=ALU.max),
               reads=[B["pp"]], writes=[Bsm])
        S.emit("dve", lambda e: e.tensor_tensor(out=small[:, 46:57], in0=small[:, 46:57], in1=small[:, 35:46], op=ALU.add),
               reads=[Bsm], writes=[Bsm])
        S.emit("dve", lambda e: e.tensor_scalar(out=small[:, SM_HC:SM_HC + 11], in0=small[:, 46:57],
                                                scalar1=-8.0, scalar2=None, op0=ALU.mult),
               reads=[Bsm], writes=[Bsm])
        S.emit("dve", lambda e: e.tensor_tensor(out=lprod[:, 0:64], in0=pp[:, PP_LQ1:PP_LQ1 + 64],
                                                in1=pp[:, PP_LK1:PP_LK1 + 64], op=ALU.mult),
               reads=[B["pp"]], writes=[B["lprod"]])
        S.emit("dve", lambda e: e.tensor_tensor(out=lprod[:, 64:128], in0=pp[:, PP_LQ2:PP_LQ2 + 64],
                                                in1=pp[:, PP_LK2:PP_LK2 + 64], op=ALU.mult),
               reads=[B["pp"]], writes=[B["lprod"]])
        S.emit("dve", lambda e: e.reduce_sum(out=small[:, 57:58], in_=lprod[:, 0:64], axis=AX.X),
               reads=[B["lprod"]], writes=[Bsm])
        S.emit("dve", lambda e: e.reduce_sum(out=small[:, 58:59], in_=lprod[:, 64:128], axis=AX.X),
               reads=[B["lprod"]], writes=[Bsm])
        S.emit("act", lambda e: e.activation(out=small[:, 59:61], in_=small[:, 57:59], func=AF.Exp),
               reads=[Bsm], writes=[Bsm])
        S.emit("dve", lambda e: e.tensor_tensor(out=small[:, 61:62], in0=small[:, 60:61], in1=small[:, 59:60], op=ALU.subtract),
               reads=[Bsm], writes=[Bsm])
        S.emit("dve", lambda e: e.tensor_scalar(out=small[:, SM_NLAM:SM_NLAM + 1], in0=small[:, 61:62],
                                                scalar1=-LAM_INIT, scalar2=None, op0=ALU.add),
               reads=[Bsm], writes=[Bsm])
        S.emit("dve", lambda e: e.tensor_scalar(out=small[:, SM_HGS:SM_HGS + 1], in0=pp[:, PP_HG:PP_HG + 1],
                                                scalar1=(1.0 - LAM_INIT), scalar2=None, op0=ALU.mult),
               reads=[B["pp"]], writes=[Bsm])

        ring_i = [0]

        def load_slab(name, idx):
            slot = ring_i[0] % RING
            ring_i[0] += 1
            w = wwidth[name]
            S.emit("sp", lambda e: e.dma_start(out=ring[:, slot, 0:w], in_=wb16[name][idx]),
                   reads=[Bw[name]], writes=[Bring[slot]], dsem=f"rg{slot}")
            return slot

        def a_slabs(k0, k1):
            L = []
            for k in range(k0, k1):
                if k < NCH:
                    L.append(("win_u", k))
                if 0 <= k - 2 < NCH:
                    L.append(("win_g", k - 2))
                    L.append(("gts", k - 2))
                if k - 3 == NCH - 3:
                    for m in range(4):
                        L.append(("wout", m))
            return L

        import os as _os
        _lim = int(_os.environ.get("KTILES", str(NT)))
        tile_specs = list(range(_lim)) + (["s"] if _os.environ.get("KNOS", "0") == "0" else [])
        HOIST_ = 5
        slab_seq = []
        for i_, _ in enumerate(tile_specs):
            hoisted_in = (i_ > 0)
            slab_seq += a_slabs(HOIST_ if hoisted_in else 0, NCH + 3)
            slab_seq += [("wout", m) for m in range(4, 8)]
            slab_seq += [("wk", h) for h in range(8)] + [("wv", x) for x in range(8)]
            slab_seq += [("wq", h) for h in range(8)] + [("wbg", m) for m in range(8)]
            if i_ + 1 < len(tile_specs):
                slab_seq += a_slabs(0, HOIST_)
            slab_seq += [("wbo", m) for m in range(8)]
        slab_pos = [0]
        slab_loaded = [0]
        slab_slots = {}

        slab_pin = [None]

        def prefetch():
            base = slab_pos[0] if slab_pin[0] is None else min(slab_pos[0], slab_pin[0])
            while slab_loaded[0] < len(slab_seq) and slab_loaded[0] < base + RING - 1:
                n, i = slab_seq[slab_loaded[0]]
                slab_slots[slab_loaded[0]] = load_slab(n, i)
                slab_loaded[0] += 1

        def next_slab(name, idx):
            p = slab_pos[0]
            assert slab_seq[p] == (name, idx), (slab_seq[p], name, idx)
            if p >= slab_loaded[0]:
                prefetch_force(p)
            slot = slab_slots.pop(p)
            slab_pos[0] += 1
            return slot

        def prefetch_force(p):
            while slab_loaded[0] <= p:
                n, i = slab_seq[slab_loaded[0]]
                slab_slots[slab_loaded[0]] = load_slab(n, i)
                slab_loaded[0] += 1

        deferred = []

        def flush_stores():
            for f in deferred:
                f()
            deferred.clear()

        def store(out_ap, in_ap, rbufs, sem, wbufs=()):
            store_sems.add(sem)

            def f():
                S.emit("sp", lambda e: e.dma_start(out=out_ap, in_=in_ap), reads=rbufs, writes=list(wbufs), dsem=sem)
            deferred.append(f)

        def norm_square(src, bsrc, kc, N):
            S.emit("dve", lambda e: e.tensor_tensor(out=on[:, kc, 0:N], in0=src[:, kc, 0:N], in1=src[:, kc, 0:N], op=ALU.mult),
                   reads=[bsrc], writes=[Bon[kc]])

        def norm_sum_mm(bk, kc, N):
            S.emit("pe", lambda e: e.matmul(psb(bk)[:, 0:N], lhsT=ones_b, rhs=on[:, kc, 0:N], start=(kc == 0), stop=(kc == 7)),
                   reads=[Bon[kc], B["cmb"]], writes=[Bps[bk]], sig=True)

        def norm_rstd(bk, N):
            t = T32.alloc()
            S.emit("act", lambda e: e.activation(out=T32.ap(t)[:, 0:N], in_=psb(bk)[:, 0:N], func=AF.Ln, scale=1.0 / D, bias=epsc),
                   reads=[Bps[bk], Bsm], writes=[T32.buf(t)])
            S.emit("act", lambda e: e.activation(out=T32.ap(t)[:, 0:N], in_=T32.ap(t)[:, 0:N], func=AF.Exp, scale=-0.5),
                   reads=[T32.buf(t)], writes=[T32.buf(t)])
            return t

        def norm_apply(src, bsrc, gcol, t, dst, bdst, N):
            for kc in range(8):
                S.emit("dve", lambda e, kc=kc: e.scalar_tensor_tensor(out=dst[:, kc, 0:N], in0=src[:, kc, 0:N],
                                                                      scalar=ppc(gcol + kc), in1=T32.ap(t)[:, 0:N],
                                                                      op0=ALU.mult, op1=ALU.mult),
                       reads=[bsrc, T32.buf(t), B["pp"]], writes=[bdst[kc]])

        def tile_params(spec):
            is_s = spec == "s"
            return dict(N=NS if is_s else TN, tok0=0 if is_s else spec * TN, xsrc=xT_s if is_s else xT_p)

        def load_x(spec):
            p_ = tile_params(spec)
            S.emit("sp", lambda e: e.dma_start(out=xin[:, :, 0:p_["N"]],
                                               in_=p_["xsrc"].rearrange("(kc p) n -> p kc n", p=128)[:, :, p_["tok0"]:p_["tok0"] + p_["N"]]),
                   writes=[B["xin"]], dsem="ld_x")

        def a_norm(spec):
            N = tile_params(spec)["N"]
            for kc in range(8):
                norm_square(xin, B["xin"], kc, N)
            bk = bank()
            for kc in range(8):
                norm_sum_mm(bk, kc, N)
            t = norm_rstd(bk, N)
            norm_apply(xin, B["xin"], PP_AN, t, xn, Bxn, N)
            T32.release(t)

        def proj_fm(slot, nk, rhs_of, rbufs, N, coff=0):
            bk = bank()
            for k in range(nk):
                S.emit("pe", lambda e, k=k: e.matmul(psb(bk)[:, 0:N], lhsT=ring[:, slot, coff + k * 128: coff + (k + 1) * 128],
                                                      rhs=rhs_of(k), start=(k == 0), stop=(k == nk - 1)),
                       reads=[Bring[slot]] + (rbufs(k) if callable(rbufs) else rbufs), writes=[Bps[bk]], sig=(k == nk - 1))
            return bk

        def qk_stage1(kind, h, N, xsrc_ap, bxsrc):
            slot = next_slab("wk" if kind == "k" else "wq", h)
            prefetch()
            bk = proj_fm(slot, 8, lambda k: xsrc_ap[:, k, 0:N], (lambda k: [bxsrc[k]]), N)
            kb = T16.alloc()
            sq = T16.alloc()
            S.emit("act", lambda e: e.activation(out=T16.ap(kb)[:, 0:N], in_=psb(bk)[:, 0:N], func=AF.Copy),
                   reads=[Bps[bk]], writes=[T16.buf(kb)])
            S.emit("act", lambda e: e.activation(out=T16.ap(sq)[:, 0:N], in_=psb(bk)[:, 0:N], func=AF.Square),
                   reads=[Bps[bk]], writes=[T16.buf(sq)])
            return (bk, kb, sq)

        def qk_stage2(kind, st, N, ropei, dst_fn):
            bk, kb, sq = st
            bsw = bank()
            S.emit("pe", lambda e: e.matmul(psb(bsw)[:, 0:N], lhsT=perm_b, rhs=T16.ap(kb)[:, 0:N], start=True, stop=True),
                   reads=[T16.buf(kb), B["cmb"]], writes=[Bps[bsw]])
            bss = bank()
            S.emit("pe", lambda e: e.matmul(psb(bss)[:, 0:N], lhsT=blk_b, rhs=T16.ap(sq)[:, 0:N], start=True, stop=True),
                   reads=[T16.buf(sq), B["cmb"]], writes=[Bps[bss]])
            T16.release(kb)
            T16.release(sq)
            rk = T32.alloc()
            S.emit("act", lambda e: e.activation(out=T32.ap(rk)[:, 0:N], in_=psb(bss)[:, 0:N], func=AF.Ln, scale=1.0 / 64, bias=epsc),
                   reads=[Bps[bss], Bsm], writes=[T32.buf(rk)])
            S.emit("act", lambda e: e.activation(out=T32.ap(rk)[:, 0:N], in_=T32.ap(rk)[:, 0:N], func=AF.Exp, scale=-0.5),
                   reads=[T32.buf(rk)], writes=[T32.buf(rk)])
            g0, g1 = (PP_GK, PP_GKS) if kind == "k" else (PP_GQ, PP_GQS)
            u1 = T32.alloc()
            u2 = T32.alloc()
            S.emit("dve", lambda e: e.scalar_tensor_tensor(out=T32.ap(u1)[:, 0:N], in0=psb(bk)[:, 0:N], scalar=ppc(g0),
                                                           in1=rope[:, ropei, 0, 0:N], op0=ALU.mult, op1=ALU.mult),
                   reads=[Bps[bk], Brope[ropei], B["pp"]], writes=[T32.buf(u1)])
            S.emit("dve", lambda e: e.scalar_tensor_tensor(out=T32.ap(u2)[:, 0:N], in0=psb(bsw)[:, 0:N], scalar=ppc(g1),
                                                           in1=rope[:, ropei, 1, 0:N], op0=ALU.mult, op1=ALU.mult),
                   reads=[Bps[bsw], Brope[ropei], B["pp"]], writes=[T32.buf(u2)])
            S.emit("dve", lambda e: e.tensor_tensor(out=T32.ap(u1)[:, 0:N], in0=T32.ap(u1)[:, 0:N], in1=T32.ap(u2)[:, 0:N], op=ALU.add),
                   reads=[T32.buf(u1), T32.buf(u2)], writes=[T32.buf(u1)])
            T32.release(u2)
            dst_fn(u1, rk)
            T32.release(u1)
            T32.release(rk)

        def qk_heads(kind, N, ropei, xsrc_ap, bxsrc, dst_of):
            st = qk_stage1(kind, 0, N, xsrc_ap, bxsrc)
            for h in range(NH):
                nxt = qk_stage1(kind, h + 1, N, xsrc_ap, bxsrc) if h + 1 < NH else None
                qk_stage2(kind, st, N, ropei, dst_of(h))
                st = nxt

        def make_aphase(spec):
            is_s = spec == "s"
            N = NS if is_s else TN
            tix = 0 if is_s else spec
            ucarry, hcarry = (ucarry_s, hcarry_s) if is_s else (ucarry_p, hcarry_p)
            Buc, Bhc = (B["ucarry_s"], B["hcarry_s"]) if is_s else (B["ucarry_p"], B["hcarry_p"])
            ucf = {}

            def sigmoid_from_psum(tb_, bk_):
                S.emit("act", lambda e: e.activation(out=T32.ap(tb_)[:, 0:N], in_=psb(bk_)[:, 0:N], func=AF.Exp, scale=-1.0),
                       reads=[Bps[bk_]], writes=[T32.buf(tb_)])
                S.emit("act", lambda e: e.activation(out=T32.ap(tb_)[:, 0:N], in_=T32.ap(tb_)[:, 0:N], func=AF.Ln, scale=1.0, bias=onec),
                       reads=[T32.buf(tb_), Bsm], writes=[T32.buf(tb_)])
                S.emit("act", lambda e: e.activation(out=T32.ap(tb_)[:, 0:N], in_=T32.ap(tb_)[:, 0:N], func=AF.Exp, scale=-1.0),
                       reads=[T32.buf(tb_)], writes=[T32.buf(tb_)])

            def stage_u(c):
                slot = next_slab("win_u", c)
                prefetch()
                bk = proj_fm(slot, 8, lambda k: xn[:, k, 0:N], (lambda k: [Bxn[k]]), N)
                ur = T32.alloc()
                uf = T32.alloc()
                S.emit("dve", lambda e: e.tensor_copy(out=T32.ap(ur)[:, 0:3], in_=ucarry[:, c, :]),
                       reads=[Buc], writes=[T32.buf(ur)])
                S.emit("dve", lambda e: e.tensor_copy(out=T32.ap(ur)[:, 3:3 + N], in_=psb(bk)[:, 0:N]),
                       reads=[Bps[bk]], writes=[T32.buf(ur)])
                S.emit("act", lambda e: e.activation(out=T32.ap(uf)[:, 0:N], in_=T32.ap(ur)[:, 3:3 + N], func=AF.Identity,
                                                     scale=ppc(PP_CW + 4 * c + 3), bias=ppc(PP_CB + c)),
                       reads=[T32.buf(ur), B["pp"]], writes=[T32.buf(uf)])
                for j in (2, 1, 0):
                    S.emit("dve", lambda e, j=j: e.scalar_tensor_tensor(out=T32.ap(uf)[:, 0:N], in0=T32.ap(ur)[:, j:j + N],
                                                                        scalar=ppc(PP_CW + 4 * c + j), in1=T32.ap(uf)[:, 0:N],
                                                                        op0=ALU.mult, op1=ALU.add),
                           reads=[T32.buf(ur), T32.buf(uf), B["pp"]], writes=[T32.buf(uf)])
                S.emit("dve", lambda e: e.tensor_copy(out=ucarry[:, c, :], in_=T32.ap(ur)[:, N:N + 3]),
                       reads=[T32.buf(ur)], writes=[Buc])
                S.emit("pool", lambda e: e.tensor_copy(out=b16[:, c, 0:N], in_=T32.ap(uf)[:, 0:N]),
                       reads=[T32.buf(uf)], writes=[Bb16[c]])
                T32.release(ur)
                ucf[c] = uf

            gst = {}

            def g1(c):
                slot2 = next_slab("win_g", c)
                prefetch()
                bkg = proj_fm(slot2, 8, lambda k: xn[:, k, 0:N], (lambda k: [Bxn[k]]), N)
                tg = T32.alloc()
                sigmoid_from_psum(tg, bkg)
                S.emit("dve", lambda e: e.tensor_tensor(out=T32.ap(tg)[:, 0:N], in0=T32.ap(tg)[:, 0:N], in1=psb(bkg)[:, 0:N], op=ALU.mult),
                       reads=[T32.buf(tg), Bps[bkg]], writes=[T32.buf(tg)])
                slot = next_slab("gts", c)
                prefetch()
                dks = [dk for dk in range(3) if 0 <= c + dk - 1 < NCH]
                bkr = bank()
                bki = bank()
                for g, bk in ((0, bkr), (1, bki)):
                    for n_, dk in enumerate(dks):
                        S.emit("pe", lambda e, g=g, bk=bk, dk=dk, n_=n_: e.matmul(
                            psb(bk)[:, 0:N], lhsT=ring[:, slot, (dk * 2 + g) * 128:(dk * 2 + g + 1) * 128],
                            rhs=b16[:, c + dk - 1, 0:N], start=(n_ == 0), stop=(n_ == len(dks) - 1)),
                            reads=[Bring[slot], Bb16[c + dk - 1]], writes=[Bps[bk]], sig=(n_ == len(dks) - 1))
                tr = T32.alloc()
                ti = T32.alloc()
                a = T32.alloc()
                m = T32.alloc()
                S.emit("act", lambda e: e.activation(out=T32.ap(tr)[:, 0:N], in_=psb(bkr)[:, 0:N], func=AF.Exp,
                                                     scale=-1.0, bias=smc(SM_HBR + c)),
                       reads=[Bps[bkr], Bsm], writes=[T32.buf(tr)])
                S.emit("act", lambda e: e.activation(out=T32.ap(ti)[:, 0:N], in_=psb(bki)[:, 0:N], func=AF.Exp,
                                                     scale=-1.0, bias=smc(SM_HBI + c)),
                       reads=[Bps[bki], Bsm], writes=[T32.buf(ti)])
                S.emit("act", lambda e: e.activation(out=T32.ap(tr)[:, 0:N], in_=T32.ap(tr)[:, 0:N], func=AF.Ln, scale=1.0, bias=onec),
                       reads=[T32.buf(tr), Bsm], writes=[T32.buf(tr)])
                S.emit("act", lambda e: e.activation(out=T32.ap(ti)[:, 0:N], in_=T32.ap(ti)[:, 0:N], func=AF.Ln, scale=1.0, bias=onec),
                       reads=[T32.buf(ti), Bsm], writes=[T32.buf(ti)])
                S.emit("act", lambda e: e.activation(out=T32.ap(tr)[:, 0:N], in_=T32.ap(tr)[:, 0:N], func=AF.Exp, scale=-1.0),
                       reads=[T32.buf(tr)], writes=[T32.buf(tr)])
                S.emit("act", lambda e: e.activation(out=T32.ap(ti)[:, 0:N], in_=T32.ap(ti)[:, 0:N], func=AF.Exp, scale=-1.0),
                       reads=[T32.buf(ti)], writes=[T32.buf(ti)])
                S.emit("act", lambda e: e.activation(out=T32.ap(a)[:, 0:N], in_=T32.ap(tr)[:, 0:N], func=AF.Exp, scale=smc(SM_HC + c)),
                       reads=[T32.buf(tr), Bsm], writes=[T32.buf(a)])
                T32.release(tr)
                S.emit("pool", lambda e: e.tensor_tensor(out=T32.ap(m)[:, 0:N], in0=T32.ap(a)[:, 0:N], in1=T32.ap(a)[:, 0:N], op=ALU.mult),
                       reads=[T32.buf(a)], writes=[T32.buf(m)])
                gst[c] = (tg, ti, a, m)

            def g2_act(c):
                tg, ti, a, m = gst[c]
                S.emit("act", lambda e: e.activation(out=T32.ap(m)[:, 0:N], in_=T32.ap(m)[:, 0:N], func=AF.Ln, scale=-1.0, bias=onec),
                       reads=[T32.buf(m), Bsm], writes=[T32.buf(m)])
                S.emit("act", lambda e: e.activation(out=T32.ap(m)[:, 0:N], in_=T32.ap(m)[:, 0:N], func=AF.Exp, scale=0.5),
                       reads=[T32.buf(m)], writes=[T32.buf(m)])
                if (not is_s) and tix == 0:
                    S.emit("pool", lambda e: e.memset(T32.ap(m)[:, 0:1], 1.0), writes=[T32.buf(m)])

            def g2_dve1(c):
                tg, ti, a, m = gst[c]
                uf = ucf.pop(c)
                S.emit("dve", lambda e: e.tensor_tensor(out=T32.ap(ti)[:, 0:N], in0=T32.ap(ti)[:, 0:N], in1=T32.ap(uf)[:, 0:N], op=ALU.mult),
                       reads=[T32.buf(ti), T32.buf(uf)], writes=[T32.buf(ti)])
                T32.release(uf)
                S.emit("dve", lambda e: e.tensor_tensor(out=T32.ap(ti)[:, 0:N], in0=T32.ap(ti)[:, 0:N], in1=T32.ap(m)[:, 0:N], op=ALU.mult),
                       reads=[T32.buf(ti), T32.buf(m)], writes=[T32.buf(ti)])
                T32.release(m)
                hh = T32.alloc()
                S.emit("dve", lambda e: e.tensor_tensor_scan(out=T32.ap(hh)[:, 0:N], data0=T32.ap(a)[:, 0:N], data1=T32.ap(ti)[:, 0:N],
                                                             initial=hcarry[:, c:c + 1], op0=ALU.mult, op1=ALU.add),
                       reads=[T32.buf(a), T32.buf(ti), Bhc], writes=[T32.buf(hh)])
                T32.release(a)
                T32.release(ti)
                S.emit("dve", lambda e: e.tensor_copy(out=hcarry[:, c:c + 1], in_=T32.ap(hh)[:, N - 1:N]),
                       reads=[T32.buf(hh)], writes=[Bhc])
                gst[c] = (tg, hh)

            def g2_dve2(c):
                tg, hh = gst.pop(c)
                S.emit("dve", lambda e: e.tensor_tensor(out=b16[:, 11 + c, 0:N], in0=T32.ap(hh)[:, 0:N], in1=T32.ap(tg)[:, 0:N], op=ALU.mult),
                       reads=[T32.buf(hh), T32.buf(tg)], writes=[Bb16[11 + c]])
                T32.release(hh)
                T32.release(tg)

            aout = {}

            def aout_part(cs):
                if "slots" not in aout:
                    slab_pin[0] = slab_pos[0]
                    aout["slots"] = [next_slab("wout", j) for j in range(4)]
                    prefetch()
                    aout["banks"] = [bank() for _ in range(4)]
                    bank_held.update(aout["banks"])
                for c in cs:
                    for j in range(4):
                        S.emit("pe", lambda e, c=c, j=j: e.matmul(psb(aout["banks"][j])[:, 0:N],
                                                                  lhsT=ring[:, aout["slots"][j], c * 128:(c + 1) * 128],
                                                                  rhs=b16[:, 11 + c, 0:N], start=(c == 0), stop=(c == NCH - 1)),
                               reads=[Bring[aout["slots"][j]], Bb16[11 + c]], writes=[Bps[aout["banks"][j]]], sig=(j == 3))

            def run_k(k):
                c1, c2 = k - 2, k - 3
                if 0 <= c2 < NCH:
                    g2_act(c2)
                if k < NCH:
                    stage_u(k)
                if 0 <= c2 < NCH:
                    g2_dve1(c2)
                if 0 <= c1 < NCH:
                    g1(c1)
                if 0 <= c2 < NCH:
                    g2_dve2(c2)
                if c2 == NCH - 3:
                    aout_part(range(0, NCH - 2))
                elif c2 == NCH - 2:
                    aout_part([NCH - 2])
                elif c2 == NCH - 1:
                    aout_part([NCH - 1])
                    slab_pin[0] = None

            return {"run_k": run_k, "aout": aout, "sigmoid": sigmoid_from_psum, "k_done": 0}

        aph_pending = {}
        HOIST = 5

        def run_tile(spec, next_spec, first):
            is_s = spec == "s"
            N = NS if is_s else TN
            tix = 0 if is_s else spec
            tok0 = 0 if is_s else spec * TN
            xsrc = xT_s if is_s else xT_p
            ysrc = yT_s if is_s else yT_p
            kTo = kT_s if is_s else kT_p
            vo = v_s if is_s else v_p
            ucarry, hcarry = (ucarry_s, hcarry_s) if is_s else (ucarry_p, hcarry_p)
            Buc, Bhc = (B["ucarry_s"], B["hcarry_s"]) if is_s else (B["ucarry_p"], B["hcarry_p"])
            rope_off = SEQ if is_s else tok0
            ropei = (0 if is_s else (spec + 1)) % 2
            NB = (N + 127) // 128

            S.emit("sp", lambda e: e.dma_start(out=rope[:, ropei, :, 0:N], in_=rope_in[:, :, rope_off:rope_off + N]),
                   writes=[Brope[ropei]], dsem=f"ld_rope{ropei}")
            prefetch()

            if first:
                a_norm(spec)
            aph = aph_pending.pop(spec, None) or make_aphase(spec)
            aout = aph["aout"]
            sigmoid_from_psum = aph["sigmoid"]
            for k in range(aph["k_done"], NCH + 3):
                aph["run_k"](k)

            bkn = bank()
            bank_held.add(bkn)
            for m_ in range(8):
                if m_ < 4:
                    bk = aout["banks"][m_]
                else:
                    slot = next_slab("wout", m_)
                    prefetch()
                    bk = proj_fm(slot, NCH, lambda k: b16[:, 11 + k, 0:N], [Bb16[11 + k] for k in range(NCH)], N)
                if m_ == 4:
                    for b_ in aout["banks"]:
                        bank_held.discard(b_)
                if m_ >= 1:
                    norm_sum_mm(bkn, m_ - 1, N)
                S.emit("dve", lambda e, m_=m_, bk=bk: e.tensor_tensor(out=x1[:, m_, 0:N], in0=xin[:, m_, 0:N], in1=psb(bk)[:, 0:N], op=ALU.add),
                       reads=[B["xin"], Bps[bk]], writes=[B["x1"]])
                norm_square(x1, B["x1"], m_, N)
            norm_sum_mm(bkn, 7, N)
            bank_held.discard(bkn)
            t_x1 = norm_rstd(bkn, N)
            if next_spec is not None:
                load_x(next_spec)
            if is_s:
                store(conv_s_o, ucarry_s[:].rearrange("p c j -> p (c j)"), [B["ucarry_s"]], "st_misc", [B["conv_s_o"]])
                store(h_s_o, hcarry_s[:], [B["hcarry_s"]], "st_misc", [B["h_s_o"]])
            elif tix == NT - 1:
                store(conv_p_o, ucarry_p[:].rearrange("p c j -> p (c j)"), [B["ucarry_p"]], "st_misc", [B["conv_p_o"]])
                store(h_p_o, hcarry_p[:], [B["hcarry_p"]], "st_misc", [B["h_p_o"]])
            flush_stores()

            norm_apply(x1, B["x1"], PP_KVN, t_x1, xn, Bxn, N)
            def kdst_of(h):
                def kdst(u1, rk, h=h):
                    kst = T32.alloc()
                    S.emit("dve", lambda e: e.tensor_tensor(out=T32.ap(kst)[:, 0:N], in0=T32.ap(u1)[:, 0:N], in1=T32.ap(rk)[:, 0:N], op=ALU.mult),
                           reads=[T32.buf(u1), T32.buf(rk)], writes=[T32.buf(kst)])
                    S.emit("pool", lambda e: e.tensor_copy(out=b16[:, h, 0:N], in_=T32.ap(kst)[:, 0:N]),
                           reads=[T32.buf(kst)], writes=[Bb16[h]])
                    store_sems.add(f"st32_{kst}")
                    S.emit("sp", lambda e: e.dma_start(out=kTo[h, :, tok0:tok0 + N], in_=T32.ap(kst)[:, 0:N]),
                           reads=[T32.buf(kst)], dsem=f"st32_{kst}")
                    T32.release(kst)
                return kdst
            qk_heads("k", N, ropei, xn, Bxn, kdst_of)
            if not is_s:
                S.emit("sp", lambda e: e.dma_start(out=kscr.rearrange("h p n -> p h n")[:, :, tok0:tok0 + N], in_=b16[:, 0:8, 0:N]),
                       reads=[Bb16[h] for h in range(8)], writes=[B["kscr"]], dsem="st_kscr")
            norm_apply(x1, B["x1"], PP_BN, t_x1, xn2, Bxn2, N)
            T32.release(t_x1)
            for fh in range(2):
                bks = [bank() for _ in range(NB)]
                for sp_ in range(4):
                    slot = next_slab("wv", fh * 4 + sp_)
                    prefetch()
                    for tb in range(NB):
                        nt = min(128, N - tb * 128)
                        for kk in range(2):
                            kc = 2 * sp_ + kk
                            last = (sp_ == 3 and kk == 1)
                            S.emit("pe", lambda e, tb=tb, nt=nt, kk=kk, kc=kc, last=last, slot=slot, sp_=sp_, bks=bks: e.matmul(
                                psb(bks[tb])[0:nt, :], lhsT=xn[:, kc, tb * 128: tb * 128 + nt],
                                rhs=ring[:, slot, kk * 512:(kk + 1) * 512], start=(sp_ == 0 and kk == 0), stop=last),
                                reads=[Bxn[kc], Bring[slot]], writes=[Bps[bks[tb]]], sig=(kk == 1))
                for tb in range(NB):
                    nt = min(128, N - tb * 128)
                    vs = T32.alloc()
                    S.emit("act", lambda e, tb=tb, nt=nt, vs=vs, bks=bks: e.activation(out=T32.ap(vs)[0:nt, 0:512], in_=psb(bks[tb])[0:nt, :], func=AF.Copy),
                           reads=[Bps[bks[tb]]], writes=[T32.buf(vs)])
                    S.emit("pool", lambda e, tb=tb, nt=nt, vs=vs, fh=fh: e.tensor_copy(
                        out=b16[0:nt, 11 + 4 * fh: 11 + 4 * fh + 4, tb * 128:(tb + 1) * 128],
                        in_=T32.ap(vs)[0:nt, 0:512].rearrange("p (h e) -> p h e", e=128)),
                        reads=[T32.buf(vs)], writes=[Bb16[11 + 4 * fh + i] for i in range(4)])
                    store_sems.add(f"st32_{vs}")
                    S.emit("sp", lambda e, tb=tb, nt=nt, vs=vs, fh=fh: e.dma_start(
                        out=vo[tok0 + tb * 128: tok0 + tb * 128 + nt, fh * 512:(fh + 1) * 512], in_=T32.ap(vs)[0:nt, 0:512]),
                        reads=[T32.buf(vs)], dsem=f"st32_{vs}")
                    T32.release(vs)
            if not is_s:
                S.emit("sp", lambda e: e.dma_start(
                    out=vscr.rearrange("h p kb e -> p h kb e")[:, :, 4 * tix:4 * tix + 4, :],
                    in_=b16[:, 11:19, :].rearrange("p h (tb e) -> p h tb e", e=128)),
                    reads=[Bb16[11 + h] for h in range(8)], writes=[B["vscr"]], dsem="st_vscr")

            nkeys_past = PAST if is_s else (tix + 1) * TN

            def load_head(h):
                bi = h % 2
                if is_s:
                    S.emit("pool", lambda e: e.dma_start(out=kTh[:, bi, 0:PAST], in_=ckT[h]),
                           writes=[BkTh[bi]], dsem=f"ldp_k{bi}")
                    S.emit("pool", lambda e: e.dma_start(
                        out=vh[:, bi, 0:8, :], in_=cv.rearrange("(kb p) (h e) -> h p kb e", p=128, e=128)[h]),
                        writes=[Bvh[bi]], dsem=f"ldp_v{bi}")
                else:
                    S.emit("sp", lambda e: e.dma_start(out=kTh[:, bi, 0:nkeys_past], in_=kscr[h, :, 0:nkeys_past]),
                           reads=[B["kscr"]], writes=[BkTh[bi]], dsem=f"ld_k{bi}")
                    S.emit("sp", lambda e: e.dma_start(out=vh[:, bi, 0:4 * (tix + 1), :], in_=vscr[h, :, 0:4 * (tix + 1), :]),
                           reads=[B["vscr"]], writes=[Bvh[bi]], dsem=f"ld_v{bi}")

            def finish_head_load(h):
                bi = h % 2
                if is_s:
                    S.emit("pool", lambda e: e.tensor_copy(out=kTh[:, bi, PAST:PAST + NS], in_=b16[:, h, 0:NS]),
                           reads=[Bb16[h]], writes=[BkTh[bi]])
                    S.emit("pool", lambda e: e.tensor_copy(out=vh[0:NS, bi, 8, :], in_=b16[0:NS, 11 + h, 0:128]),
                           reads=[Bb16[11 + h]], writes=[Bvh[bi]])

            if is_s:
                pass

            if is_s:
                qbase, gbase = 8, 19
            if is_s:
                def qT_ap(h):
                    return kTh[:, 0, 2048 + h * NS: 2048 + (h + 1) * NS]

                def sg_ap(m_):
                    return kTh[:, 0, 3072 + m_ * NS: 3072 + (m_ + 1) * NS]
                BqT = [Buf(f"qTs{h}") for h in range(8)]
                Bsg = [Buf(f"sgs{h}") for h in range(8)]
            else:
                def qT_ap(h):
                    return b16[:, h, 0:N]

                def sg_ap(m_):
                    return b16[:, 11 + m_, 0:N]
                BqT = [Bb16[h] for h in range(8)]
                Bsg = [Bb16[11 + h] for h in range(8)]

            if is_s:
                pass

            def qdst_of(h):
                def qdst(u1, rk, h=h):
                    S.emit("dve", lambda e: e.tensor_tensor(out=qT_ap(h), in0=T32.ap(u1)[:, 0:N], in1=T32.ap(rk)[:, 0:N], op=ALU.mult),
                           reads=[T32.buf(u1), T32.buf(rk)], writes=[BqT[h]])
                return qdst
            qk_heads("q", N, ropei, xn2, Bxn2, qdst_of)
            for m_ in range(8):
                slot = next_slab("wbg", m_)
                prefetch()
                bkg = proj_fm(slot, 8, lambda k: xn2[:, k, 0:N], (lambda k: [Bxn2[k]]), N)
                tg = T32.alloc()
                sigmoid_from_psum(tg, bkg)
                S.emit("dve", lambda e, tg=tg, bkg=bkg, m_=m_: e.tensor_tensor(out=sg_ap(m_), in0=T32.ap(tg)[:, 0:N], in1=psb(bkg)[:, 0:N], op=ALU.mult),
                       reads=[T32.buf(tg), Bps[bkg]], writes=[Bsg[m_]])
                T32.release(tg)
            flush_stores()

            if next_spec is not None:
                a_norm(next_spec)
            def blocks():
                L = []
                if is_s:
                    for kb_ in range(8):
                        L.append((kb_ * 128, 128, kb_, 0, False))
                    L.append((PAST, NS, 8, 0, False))
                else:
                    for kd in range(4):
                        L.append(((4 * tix + kd) * 128, 128, 4 * tix + kd, 128 * kd, True))
                    for kb_ in range(4 * tix):
                        L.append((kb_ * 128, 128, kb_, 0, False))
                return L

            blks = blocks()
            nblk = len(blks)
            steps = [(h, i) for h in range(NH) for i in range(nblk)]
            O1, O2, L1, L2 = 4, 5, 6, 7
            load_head(0)
            finish_head_load(0)
            load_head(1)
            finish_head_load(1)

            def emit_qk(s):
                h, i = steps[s]
                col0, nk, vb, q0, corner = blks[i]
                sp_ = s % 2
                bi = h % 2
                S.emit("pe", lambda e: e.matmul(psum[sp_][0:nk, 0, q0:N], lhsT=kTh[0:64, bi, col0:col0 + nk],
                                                rhs=qT_ap(h)[0:64, q0:N] if not is_s else qT_ap(h)[0:64, :], start=True, stop=True),
                       reads=[BkTh[bi], BqT[h]], writes=[Bps[2 * sp_]], sig=False)
                S.emit("pe", lambda e: e.matmul(psum[sp_][0:nk, 1, q0:N], lhsT=kTh[64:128, bi, col0:col0 + nk],
                                                rhs=qT_ap(h)[64:128, q0:N] if not is_s else qT_ap(h)[64:128, :], start=True, stop=True),
                       reads=[BkTh[bi], BqT[h]], writes=[Bps[2 * sp_ + 1]])

            def emit_exp_pv(s):
                h, i = steps[s]
                col0, nk, vb, q0, corner = blks[i]
                sp_ = s % 2
                pi = s % 3
                bi = h % 2
                first = (i == 0)
                last = (i == nblk - 1)
                S.emit("act", lambda e: e.activation(out=pt[0:nk, pi, :, q0:N], in_=psum[sp_][0:nk, :, q0:N], func=AF.Exp, scale=0.125),
                       reads=[Bps[2 * sp_], Bps[2 * sp_ + 1]], writes=[Bpt[pi]])
                if corner:
                    S.emit("pool", lambda e: e.memset(pt[64:128, pi, :, q0:q0 + 64], 0.0), writes=[Bpt[pi]])
                for c_, ob in ((0, O1), (1, O2)):
                    S.emit("pe", lambda e, c_=c_, ob=ob: e.matmul(psb(ob)[:, q0:N], lhsT=vh[0:nk, bi, vb, :], rhs=pt[0:nk, pi, c_, q0:N],
                                                                   start=first, stop=last),
                           reads=[Bvh[bi], Bpt[pi]], writes=[Bps[ob]], sig=False)
                for c_, lb in ((0, L1), (1, L2)):
                    S.emit("pe", lambda e, c_=c_, lb=lb: e.matmul(psb(lb)[:, q0:N], lhsT=ones_b[0:nk, :], rhs=pt[0:nk, pi, c_, q0:N],
                                                                   start=first, stop=last),
                           reads=[B["cmb"], Bpt[pi]], writes=[Bps[lb]], sig=(c_ == 1))

            fin_state = {}

            def fin_a(h):
                r1 = T32.alloc()
                r2 = T32.alloc()
                o1 = T32.alloc()
                o2 = T32.alloc()
                S.emit("dve", lambda e: e.tensor_copy(out=T32.ap(o1)[:, 0:N], in_=psb(O1)[:, 0:N]), reads=[Bps[O1]], writes=[T32.buf(o1)])
                S.emit("dve", lambda e: e.tensor_copy(out=T32.ap(o2)[:, 0:N], in_=psb(O2)[:, 0:N]), reads=[Bps[O2]], writes=[T32.buf(o2)])
                S.emit("act", lambda e: e.activation(out=T32.ap(r1)[:, 0:N], in_=psb(L1)[:, 0:N], func=AF.Ln), reads=[Bps[L1]], writes=[T32.buf(r1)])
                S.emit("act", lambda e: e.activation(out=T32.ap(r2)[:, 0:N], in_=psb(L2)[:, 0:N], func=AF.Ln), reads=[Bps[L2]], writes=[T32.buf(r2)])
                fin_state[h] = (r1, r2, o1, o2)

            def fin_a2(h):
                r1, r2, o1, o2 = fin_state.pop(h)
                S.emit("act", lambda e: e.activation(out=T32.ap(r1)[:, 0:N], in_=T32.ap(r1)[:, 0:N], func=AF.Exp, scale=-1.0),
                       reads=[T32.buf(r1)], writes=[T32.buf(r1)])
                S.emit("act", lambda e: e.activation(out=T32.ap(r2)[:, 0:N], in_=T32.ap(r2)[:, 0:N], func=AF.Exp, scale=-1.0),
                       reads=[T32.buf(r2)], writes=[T32.buf(r2)])
                S.emit("dve", lambda e: e.tensor_tensor(out=T32.ap(r1)[:, 0:N], in0=T32.ap(o1)[:, 0:N], in1=T32.ap(r1)[:, 0:N], op=ALU.mult),
                       reads=[T32.buf(o1), T32.buf(r1)], writes=[T32.buf(r1)])
                S.emit("dve", lambda e: e.tensor_tensor(out=T32.ap(r2)[:, 0:N], in0=T32.ap(o2)[:, 0:N], in1=T32.ap(r2)[:, 0:N], op=ALU.mult),
                       reads=[T32.buf(o2), T32.buf(r2)], writes=[T32.buf(r2)])
                T32.release(o1)
                T32.release(o2)
                S.emit("dve", lambda e: e.scalar_tensor_tensor(out=T32.ap(r1)[:, 0:N], in0=T32.ap(r2)[:, 0:N], scalar=smc(SM_NLAM),
                                                               in1=T32.ap(r1)[:, 0:N], op0=ALU.mult, op1=ALU.add),
                       reads=[T32.buf(r1), T32.buf(r2), Bsm], writes=[T32.buf(r1)])
                T32.release(r2)
                sq = T16.alloc()
                S.emit("dve", lambda e: e.tensor_tensor(out=T16.ap(sq)[:, 0:N], in0=T32.ap(r1)[:, 0:N], in1=T32.ap(r1)[:, 0:N], op=ALU.mult),
                       reads=[T32.buf(r1)], writes=[T16.buf(sq)])
                fin_state[h] = (r1, sq)

            def fin_b(h, bk):
                r1, sq = fin_state.pop(h)
                S.emit("pe", lambda e: e.matmul(psb(bk)[:, 0:N], lhsT=ones_b, rhs=T16.ap(sq)[:, 0:N], start=True, stop=True),
                       reads=[T16.buf(sq), B["cmb"]], writes=[Bps[bk]])
                T16.release(sq)
                rh = T32.alloc()
                S.emit("act", lambda e: e.activation(out=T32.ap(rh)[:, 0:N], in_=psb(bk)[:, 0:N], func=AF.Ln, scale=1.0 / 128, bias=epsc),
                       reads=[Bps[bk], Bsm], writes=[T32.buf(rh)])
                S.emit("act", lambda e: e.activation(out=T32.ap(rh)[:, 0:N], in_=T32.ap(rh)[:, 0:N], func=AF.Exp, scale=-0.5),
                       reads=[T32.buf(rh)], writes=[T32.buf(rh)])
                S.emit("dve", lambda e: e.scalar_tensor_tensor(out=T32.ap(r1)[:, 0:N], in0=T32.ap(r1)[:, 0:N], scalar=smc(SM_HGS),
                                                               in1=T32.ap(rh)[:, 0:N], op0=ALU.mult, op1=ALU.mult),
                       reads=[T32.buf(r1), T32.buf(rh), Bsm], writes=[T32.buf(r1)])
                S.emit("dve", lambda e: e.tensor_tensor(out=on[:, h, 0:N], in0=T32.ap(r1)[:, 0:N], in1=sg_ap(h), op=ALU.mult),
                       reads=[T32.buf(r1), Bsg[h]], writes=[Bon[h]])
                T32.release(r1)
                T32.release(rh)

            ns = len(steps)
            emit_qk(0)
            if ns > 1:
                emit_qk(1)
            pend_a2 = None
            pend_b = None
            for s in range(ns):
                h, i = steps[s]
                emit_exp_pv(s)
                if pend_a2 is not None and s >= pend_a2[1]:
                    fin_a2(pend_a2[0])
                    pend_b = (pend_a2[0], s + 2)
                    pend_a2 = None
                elif pend_b is not None and s >= pend_b[1]:
                    fin_b(pend_b[0], 2 * (s % 2))
                    pend_b = None
                if i == nblk - 1:
                    fin_a(h)
                    pend_a2 = (h, s + 1)
                    if h + 2 < NH:
                        load_head(h + 2)
                        finish_head_load(h + 2)
                if s + 2 < ns:
                    emit_qk(s + 2)
            if pend_a2 is not None:
                fin_a2(pend_a2[0])
                pend_b = (pend_a2[0], 0)
            if pend_b is not None:
                fin_b(pend_b[0], 0)
            if next_spec is not None and HOIST > 0:
                aphn = make_aphase(next_spec)
                for k in range(HOIST):
                    aphn["run_k"](k)
                aphn["k_done"] = HOIST
                aph_pending[next_spec] = aphn

            for m_ in range(8):
                slot = next_slab("wbo", m_)
                prefetch()
                bk = proj_fm(slot, 8, lambda k: on[:, k, 0:N], Bon, N)
                ys = T32.alloc()
                S.emit("dve", lambda e, m_=m_, bk=bk, ys=ys: e.tensor_tensor(out=T32.ap(ys)[:, 0:N], in0=x1[:, m_, 0:N], in1=psb(bk)[:, 0:N], op=ALU.add),
                       reads=[B["x1"], Bps[bk]], writes=[T32.buf(ys)])
                store_sems.add(f"st32_{ys}")
                S.emit("sp", lambda e, m_=m_, ys=ys: e.dma_start(out=ysrc[m_ * 128:(m_ + 1) * 128, tok0:tok0 + N], in_=T32.ap(ys)[:, 0:N]),
                       reads=[T32.buf(ys)], dsem=f"st32_{ys}")
                T32.release(ys)

        for i_, spec in enumerate(tile_specs):
            run_tile(spec, tile_specs[i_ + 1] if i_ + 1 < len(tile_specs) else None, i_ == 0)
        flush_stores()
        S.final_waits("sp", sorted(store_sems) + ["pe", "act", "dve", "pool"])

        block = es.enter_context(nc.Block())
        S.replay(block)
    return nc


def _tile_w(W, nk, nm):
    return np.ascontiguousarray(W.reshape(nk, 128, nm, 128).transpose(2, 1, 0, 3).reshape(nm, 128, nk * 128))


def _chunkcol(v):
    return np.ascontiguousarray(v.reshape(-1, 128).T)


def _prep_shared(inp):
    f = np.float32
    a_w_in = np.asarray(inp["a_w_in"][0], f)
    shared = {}
    shared["win_u"] = _tile_w(a_w_in[:, :DR], 8, NCH)
    shared["win_g"] = _tile_w(a_w_in[:, DR:], 8, NCH)
    gts = np.zeros((NCH, 128, 3, 2, 128), f)
    for g, key in enumerate(("a_gate_r_w", "a_gate_i_w")):
        Wg = np.asarray(inp[key][0], f)
        Dm = np.zeros((DR, DR), f)
        for n in range(16):
            Dm[88 * n:88 * n + 88, 88 * n:88 * n + 88] = Wg[n]
        for c in range(NCH):
            for dk in range(3):
                kc = c + dk - 1
                if 0 <= kc < NCH:
                    gts[c, :, dk, g, :] = Dm[kc * 128:(kc + 1) * 128, c * 128:(c + 1) * 128]
    shared["gts"] = gts.reshape(NCH, 128, 768)
    shared["wout"] = _tile_w(np.asarray(inp["a_w_out"][0], f), NCH, 8)
    kv_w = np.asarray(inp["kv_w"], f)
    shared["wk"] = _tile_w(kv_w[:, :D], 8, 8)
    Wv = kv_w[:, D:]
    wv = Wv.reshape(4, 2, 128, 2, 512).transpose(3, 0, 2, 1, 4)
    shared["wv"] = np.ascontiguousarray(wv.reshape(8, 128, 1024))
    b_w_in = np.asarray(inp["b_w_in"][0], f)
    shared["wq"] = _tile_w(b_w_in[:, :D], 8, 8)
    shared["wbg"] = _tile_w(b_w_in[:, D:], 8, 8)
    shared["wbo"] = _tile_w(np.asarray(inp["b_w_out"][0], f), 8, 8)
    pp = np.zeros((128, PP_N), f)
    pp[:, PP_AN:PP_AN + 8] = _chunkcol(np.asarray(inp["a_norm"][0], f))
    pp[:, PP_KVN:PP_KVN + 8] = _chunkcol(np.asarray(inp["kv_norm"], f))
    pp[:, PP_BN:PP_BN + 8] = _chunkcol(np.asarray(inp["b_norm"][0], f))
    cw = np.asarray(inp["a_conv_w"][0], f)
    pp[:, PP_CW:PP_CW + 44] = cw.reshape(4, NCH, 128).transpose(2, 1, 0).reshape(128, 44)
    pp[:, PP_CB:PP_CB + 11] = _chunkcol(np.asarray(inp["a_conv_b"][0], f))
    pp[:, PP_BR:PP_BR + 11] = _chunkcol(np.asarray(inp["a_gate_r_b"][0], f))
    pp[:, PP_BI:PP_BI + 11] = _chunkcol(np.asarray(inp["a_gate_i_b"][0], f))
    pp[:, PP_LAM:PP_LAM + 11] = _chunkcol(np.asarray(inp["a_lambda"][0], f))
    p = np.arange(128)
    kn = np.asarray(inp["k_norm"], f)
    qn = np.asarray(inp["b_q_norm"][0], f)
    pp[:, PP_GK] = kn[p % 64]
    pp[:, PP_GKS] = kn[(p ^ 32) % 64]
    pp[:, PP_GQ] = qn[p % 64]
    pp[:, PP_GQS] = qn[(p ^ 32) % 64]
    pp[:, PP_HG] = np.asarray(inp["b_head_norm"][0], f)
    pp[:, PP_LQ1:PP_LQ1 + 64] = np.asarray(inp["b_lambda_q1"][0], f)[None, :]
    pp[:, PP_LK1:PP_LK1 + 64] = np.asarray(inp["b_lambda_k1"][0], f)[None, :]
    pp[:, PP_LQ2:PP_LQ2 + 64] = np.asarray(inp["b_lambda_q2"][0], f)[None, :]
    pp[:, PP_LK2:PP_LK2 + 64] = np.asarray(inp["b_lambda_k2"][0], f)[None, :]
    shared["pp"] = pp
    cm = np.zeros((128, 384), f)
    cm[:, 0:128] = 1.0
    cm[:, 128:256] = (p[:, None] // 64 == p[None, :] // 64).astype(f)
    cm[:, 256:384] = (p[:, None] == (p[None, :] ^ 32)).astype(f)
    shared["cmat"] = cm
    half = 32
    inv = (np.float32(10000.0) ** (-np.arange(half, dtype=f) / np.float32(half))).astype(f)
    pos = np.concatenate([np.arange(SEQ), PAST + np.arange(NS)]).astype(f)
    ang = (pos[:, None] * inv[None, :]).astype(f)
    cos = np.cos(ang).astype(f).T
    sin = np.sin(ang).astype(f).T
    fi = p % 32
    sign = np.where((p % 64) < 32, -1.0, 1.0).astype(f)
    rope = np.empty((128, 2, SEQ + NS), f)
    rope[:, 0, :] = cos[fi]
    rope[:, 1, :] = sin[fi] * sign[:, None]
    shared["rope"] = rope
    return shared


_PROG = None


def kernel(**inp):
    global _PROG
    f = np.float32
    shared = _prep_shared(inp)
    x_prompt = np.asarray(inp["x_prompt"], f)
    x_sample = np.asarray(inp["x_sample"], f)
    state_conv = np.asarray(inp["state_conv"], f)
    state_h = np.asarray(inp["state_h"], f)
    cache_k = np.asarray(inp["cache_k"], f)
    cache_v = np.asarray(inp["cache_v"], f)
    in_maps = []
    for b in range(8):
        m = dict(shared)
        m["xT_p"] = np.ascontiguousarray(x_prompt[b].T)
        m["xT_s"] = np.ascontiguousarray(x_sample[b].T)
        m["conv_s_in"] = np.ascontiguousarray(state_conv[0, b].reshape(3, NCH, 128).transpose(2, 1, 0).reshape(128, NCH * 3))
        m["h_s_in"] = _chunkcol(state_h[0, b])
        m["ckT"] = np.ascontiguousarray(cache_k[b].transpose(1, 2, 0))
        m["cv"] = np.ascontiguousarray(cache_v[b].reshape(PAST, D))
        in_maps.append(m)
    if _PROG is None:
        _PROG = build_program()
    res = run_bass_kernel_spmd(_PROG, in_maps, core_ids=list(range(8)))
    R = res.results

    def unconv(a):
        return a.reshape(128, NCH, 3).transpose(2, 1, 0).reshape(3, DR)

    def unh(a):
        return a.T.reshape(DR)

    y_p = np.stack([R[b]["yT_p"].T for b in range(8)]).astype(f)
    y_s = np.stack([R[b]["yT_s"].T for b in range(8)]).astype(f)
    conv_p = np.stack([unconv(R[b]["conv_p_o"]) for b in range(8)])[None].astype(f)
    h_p = np.stack([unh(R[b]["h_p_o"]) for b in range(8)])[None].astype(f)
    k_p = np.stack([R[b]["kT_p"].transpose(2, 0, 1) for b in range(8)]).astype(f)
    v_p = np.stack([R[b]["v_p"].reshape(SEQ, NH, 128) for b in range(8)]).astype(f)
    conv_s = np.stack([unconv(R[b]["conv_s_o"]) for b in range(8)])[None].astype(f)
    h_s = np.stack([unh(R[b]["h_s_o"]) for b in range(8)])[None].astype(f)
    k_s = np.stack([R[b]["kT_s"].transpose(2, 0, 1) for b in range(8)]).astype(f)
    v_s = np.stack([R[b]["v_s"].reshape(NS, NH, 128) for b in range(8)]).astype(f)
    return (np.ascontiguousarray(y_p), np.ascontiguousarray(y_s), np.ascontiguousarray(conv_p), np.ascontiguousarray(h_p),
            np.ascontiguousarray(k_p), np.ascontiguousarray(v_p), np.ascontiguousarray(conv_s), np.ascontiguousarray(h_s),
            np.ascontiguousarray(k_s), np.ascontiguousarray(v_s))
```

```python
import math
from contextlib import ExitStack

import numpy as np
import concourse.bass as bass
import concourse.mybir as mybir
from concourse.bass_utils import run_bass_kernel_spmd

F32 = mybir.dt.float32
BF16 = mybir.dt.bfloat16
AF = mybir.ActivationFunctionType
ALU = mybir.AluOpType
AX = mybir.AxisListType

D = 1024
SEQ = 4096
NS = 16
PAST = 1024
DR = 1408
NCH = 11
NH = 8
EPS = 1e-6
LAM_INIT = 0.8 - 0.6 * math.exp(-0.3 * 1)
TN = 512
LEAD = 3
NT = SEQ // TN
RING = 8
K32 = 24
K16 = 4

PP_AN, PP_KVN, PP_BN = 0, 8, 16
PP_CW, PP_CB, PP_BR, PP_BI, PP_LAM = 24, 68, 79, 90, 101
PP_GK, PP_GKS, PP_GQ, PP_GQS, PP_HG = 112, 113, 114, 115, 116
PP_LQ1, PP_LK1, PP_LQ2, PP_LK2 = 117, 181, 245, 309
PP_N = 373


class Buf:
    __slots__ = ("name", "w", "r", "excl")

    def __init__(self, name, excl=False):
        self.name = name
        self.w = {}
        self.r = {}
        self.excl = excl


class Sched:
    CE = ("pe", "act", "dve", "pool")

    def __init__(self, nc, es):
        self.nc = nc
        self.es = es
        self.streams = {k: [] for k in ("pe", "act", "dve", "pool", "sp")}
        self.sems = {}
        self.cnt = {}
        self.waited = {k: {} for k in self.streams}
        for e in self.CE:
            self._sem(e)

    def _sem(self, name):
        if name not in self.sems:
            self.sems[name] = self.es.enter_context(self.nc.semaphore("s_" + name))
            self.cnt[name] = 0
        return self.sems[name]

    def emit(self, eng, fn, reads=(), writes=(), sig=True, dsem=None):
        need = {}

        def add(s, v, kind):
            if dsem is None and s == eng:
                if eng == "pe" or kind != "raw":
                    return
            if need.get(s, 0) < v:
                need[s] = v

        for b in reads:
            for s, v in b.w.items():
                add(s, v, "raw")
            if b.excl:
                for s, v in b.r.items():
                    add(s, v, "rar")
        for b in writes:
            for s, v in b.w.items():
                add(s, v, "waw")
            for s, v in b.r.items():
                add(s, v, "war")
        waits = []
        wd = self.waited[eng]
        for s, v in need.items():
            if wd.get(s, 0) < v:
                assert self.cnt[s] >= v, f"wait on future event {s}>={v} (cnt {self.cnt[s]}) from {eng}"
                wd[s] = v
                waits.append((s, v))
        if dsem is not None:
            self._sem(dsem)
            self.cnt[dsem] += 16
            ev = (dsem, self.cnt[dsem])
            sg = (dsem, 16)
        elif sig:
            self.cnt[eng] += 1
            ev = (eng, self.cnt[eng])
            sg = (eng, 1)
        else:
            ev = (eng, self.cnt[eng] + 1)
            sg = None
        self.streams[eng].append((waits, fn, sg))
        for b in writes:
            b.w = {ev[0]: ev[1]}
            b.r = {}
        for b in reads:
            if b.r.get(ev[0], 0) < ev[1]:
                b.r[ev[0]] = ev[1]
        return ev

    def final_waits(self, eng, names):
        waits = [(s, self.cnt[s]) for s in names if self.cnt.get(s, 0) > 0]
        self.streams[eng].append((waits, None, None))

    def replay(self, block):
        def run(stream):
            def body(e):
                for waits, fn, sg in stream:
                    for s, v in waits:
                        e.wait_ge(self.sems[s], v)
                    if fn is None:
                        continue
                    ins = fn(e)
                    if sg is not None:
                        ins.then_inc(self.sems[sg[0]], sg[1])
            return body

        block.tensor(run(self.streams["pe"]))
        block.scalar(run(self.streams["act"]))
        block.vector(run(self.streams["dve"]))
        block.gpsimd(run(self.streams["pool"]))
        block.sync(run(self.streams["sp"]))


class Pool32:
    def __init__(self, aps, name):
        self.items = [(ap, Buf(f"{name}{i}")) for i, ap in enumerate(aps)]
        self.free = list(range(len(aps)))

    def alloc(self):
        assert self.free, "temp pool exhausted"
        i = self.free.pop(0)
        return i

    def ap(self, i):
        return self.items[i][0]

    def buf(self, i):
        return self.items[i][1]

    def release(self, i):
        self.free.append(i)


def build_program():
    nc = bass.Bass("TRN2", target_bir_lowering=False)

    def din(name, shape, dt=F32):
        return nc.dram_tensor(name, list(shape), dt, kind="ExternalInput").ap()

    def dout(name, shape, dt=F32):
        return nc.dram_tensor(name, list(shape), dt, kind="ExternalOutput").ap()

    def dint(name, shape, dt=BF16):
        return nc.dram_tensor(name, list(shape), dt, kind="Internal").ap()

    xT_p = din("xT_p", [D, SEQ])
    xT_s = din("xT_s", [D, NS])
    conv_s_in = din("conv_s_in", [128, NCH * 3])
    h_s_in = din("h_s_in", [128, NCH])
    ckT = din("ckT", [NH, 128, PAST])
    cv = din("cv", [PAST, D])
    wnames = [("win_u", NCH, 1024), ("win_g", NCH, 1024), ("gts", NCH, 768), ("wout", 8, 1408),
              ("wk", 8, 1024), ("wv", 8, 1024), ("wq", 8, 1024), ("wbg", 8, 1024), ("wbo", 8, 1024)]
    wf32 = {n: din(n, [ns, 128, w]) for n, ns, w in wnames}
    wb16 = {n: dint(n + "_b", [ns, 128, w]) for n, ns, w in wnames}
    wwidth = {n: w for n, ns, w in wnames}
    pp_in = din("pp", [128, PP_N])
    cmat_in = din("cmat", [128, 384])
    rope_in = din("rope", [128, 2, SEQ + NS])

    yT_p = dout("yT_p", [D, SEQ])
    yT_s = dout("yT_s", [D, NS])
    conv_p_o = dout("conv_p_o", [128, NCH * 3])
    h_p_o = dout("h_p_o", [128, NCH])
    kT_p = dout("kT_p", [NH, 128, SEQ])
    v_p = dout("v_p", [SEQ, D])
    conv_s_o = dout("conv_s_o", [128, NCH * 3])
    h_s_o = dout("h_s_o", [128, NCH])
    kT_s = dout("kT_s", [NH, 128, NS])
    v_s = dout("v_s", [NS, D])

    kscr = dint("kscr", [NH, 128, SEQ])
    vscr = dint("vscr", [NH, 128, SEQ // 128, 128])

    with ExitStack() as es:
        S = Sched(nc, es)

        def sb(name, shape, dt):
            return es.enter_context(nc.sbuf_tensor(name, list(shape), dt))

        xin = sb("xin", [128, 8, TN], F32)
        x1 = sb("x1", [128, 8, TN], F32)
        xn = sb("xn", [128, 8, TN], BF16)
        ring = sb("ring", [128, RING, 1408], BF16)
        kTh = sb("kTh", [128, 2, SEQ], BF16)
        vh = sb("vh", [128, 2, SEQ // 128, 128], BF16)
        pp = sb("pp_sb", [128, PP_N], F32)
        cmf = sb("cmf", [128, 384], F32)
        cmb = sb("cmb", [128, 384], BF16)
        xn2 = sb("xn2", [128, 8, TN], BF16)
        rope = sb("rope_sb", [128, 2, 2, TN], F32)
        t32 = sb("t32", [128, K32, TN + 4], F32)
        b16 = sb("b16", [128, 22, TN], BF16)
        on = sb("on", [128, 8, TN], BF16)
        pt = sb("pt", [128, 3, 2, TN], BF16)
        t16 = sb("t16", [128, K16, TN], BF16)
        small = sb("small", [128, 64], F32)
        lprod = sb("lprod", [128, 128], F32)
        ucarry_p = sb("ucarry_p", [128, NCH, 3], F32)
        hcarry_p = sb("hcarry_p", [128, NCH], F32)
        ucarry_s = sb("ucarry_s", [128, NCH, 3], F32)
        hcarry_s = sb("hcarry_s", [128, NCH], F32)
        psum = [es.enter_context(nc.psum_tensor(f"ps{i}", [128, 2, 512], F32)) for i in range(4)]

        B = {n: Buf(n) for n in ("xin", "x1", "xn", "xn2", "pp", "cmf", "cmb", "consts", "small", "lprod", "on",
                                 "ucarry_p", "hcarry_p", "ucarry_s", "hcarry_s", "kscr", "vscr",
                                 "conv_p_o", "h_p_o", "conv_s_o", "h_s_o")}
        Bring = [Buf(f"ring{i}") for i in range(RING)]
        BkTh = [Buf("kTh0"), Buf("kTh1")]
        Bvh = [Buf("vh0"), Buf("vh1")]
        Brope = [Buf("rope0"), Buf("rope1")]
        Bb16 = [Buf(f"b16_{i}") for i in range(22)]
        Bpt = [Buf(f"pt{i}") for i in range(3)]
        Bps = [Buf(f"psb{i}", excl=True) for i in range(8)]
        Bon = [Buf(f"on{i}") for i in range(8)]
        Bxn = [Buf(f"xn{i}") for i in range(8)]
        Bxn2 = [Buf(f"xn2_{i}") for i in range(8)]
        Bw = {n: Buf("w_" + n) for n, _, _ in wnames}
        T32 = Pool32([t32[:, i, :] for i in range(K32)], "t32_")
        T16 = Pool32([t16[:, i, :] for i in range(K16)], "t16_")
        store_sems = set()

        def psb(i):
            return psum[i // 2][:, i % 2, :]

        bank_rr = [0]
        bank_held = set()

        def bank():
            while True:
                i = bank_rr[0]
                bank_rr[0] = (i + 1) % 8
                if i not in bank_held:
                    return i

        ones_b = cmb[:, 0:128]
        blk_b = cmb[:, 128:256]
        perm_b = cmb[:, 256:384]

        def ppc(i):
            return pp[:, i:i + 1]

        def smc(i):
            return small[:, i:i + 1]
        SM_HBR, SM_HBI, SM_HC, SM_NLAM, SM_HGS = 0, 11, 22, 33, 34
        epsc = small[:, 62:63]
        onec = small[:, 63:64]

        S.emit("sp", lambda e: e.dma_start(out=pp[:], in_=pp_in), writes=[B["pp"]], dsem="ld_pp")
        S.emit("sp", lambda e: e.dma_start(out=cmf[:], in_=cmat_in), writes=[B["cmf"]], dsem="ld_cm")
        S.emit("sp", lambda e: e.dma_start(out=xin[:, :, 0:TN], in_=xT_p.rearrange("(kc p) n -> p kc n", p=128)[:, :, 0:TN]),
               writes=[B["xin"]], dsem="ld_x")
        for n, ns, w in wnames:
            S.emit("pool", lambda e, n=n: e.dma_start(out=wb16[n], in_=wf32[n]), reads=[B["xin"], B["pp"], B["cmf"]],
                   writes=[Bw[n]], dsem="cv_" + n)
        S.emit("pool", lambda e: e.tensor_copy(out=cmb[:], in_=cmf[:]), reads=[B["cmf"]], writes=[B["cmb"]])
        S.emit("pool", lambda e: e.memset(ucarry_p[:], 0.0), writes=[B["ucarry_p"]])
        S.emit("pool", lambda e: e.memset(hcarry_p[:], 0.0), writes=[B["hcarry_p"]])
        S.emit("sp", lambda e: e.dma_start(out=ucarry_s[:].rearrange("p c j -> p (c j)"), in_=conv_s_in),
               writes=[B["ucarry_s"]], dsem="ld_cs")
        S.emit("sp", lambda e: e.dma_start(out=hcarry_s[:], in_=h_s_in), writes=[B["hcarry_s"]], dsem="ld_hs")
        Bsm = B["small"]
        S.emit("pool", lambda e: e.memset(small[:, 62:63], EPS), writes=[Bsm])
        S.emit("pool", lambda e: e.memset(small[:, 63:64], 1.0), writes=[Bsm])
        S.emit("dve", lambda e: e.tensor_scalar(out=small[:, SM_HBR:SM_HBR + 11], in0=pp[:, PP_BR:PP_BR + 11],
                                                scalar1=-1.0, scalar2=None, op0=ALU.mult),
               reads=[B["pp"]], writes=[Bsm])
        S.emit("dve", lambda e: e.tensor_scalar(out=small[:, SM_HBI:SM_HBI + 11], in0=pp[:, PP_BI:PP_BI + 11],
                                                scalar1=-1.0, scalar2=None, op0=ALU.mult),
               reads=[B["pp"]], writes=[Bsm])
        S.emit("act", lambda e: e.activation(out=small[:, 35:46], in_=pp[:, PP_LAM:PP_LAM + 11], func=AF.Abs),
               reads=[B["pp"]], writes=[Bsm])
        S.emit("act", lambda e: e.activation(out=small[:, 35:46], in_=small[:, 35:46], func=AF.Exp, scale=-1.0),
               reads=[Bsm], writes=[Bsm])
        S.emit("act", lambda e: e.activation(out=small[:, 35:46], in_=small[:, 35:46], func=AF.Ln, bias=onec, scale=1.0),
               reads=[Bsm], writes=[Bsm])
        S.emit("dve", lambda e: e.tensor_scalar(out=small[:, 46:57], in0=pp[:, PP_LAM:PP_LAM + 11],
                                                scalar1=-1.0, scalar2=0.0, op0=ALU.mult, op1=ALU.max),
               reads=[B["pp"]], writes=[Bsm])
        S.emit("dve", lambda e: e.tensor_tensor(out=small[:, 46:57], in0=small[:, 46:57], in1=small[:, 35:46], op=ALU.add),
               reads=[Bsm], writes=[Bsm])
        S.emit("dve", lambda e: e.tensor_scalar(out=small[:, SM_HC:SM_HC + 11], in0=small[:, 46:57],
                                                scalar1=-8.0, scalar2=None, op0=ALU.mult),
               reads=[Bsm], writes=[Bsm])
        S.emit("dve", lambda e: e.tensor_tensor(out=lprod[:, 0:64], in0=pp[:, PP_LQ1:PP_LQ1 + 64],
                                                in1=pp[:, PP_LK1:PP_LK1 + 64], op=ALU.mult),
               reads=[B["pp"]], writes=[B["lprod"]])
        S.emit("dve", lambda e: e.tensor_tensor(out=lprod[:, 64:128], in0=pp[:, PP_LQ2:PP_LQ2 + 64],
                                                in1=pp[:, PP_LK2:PP_LK2 + 64], op=ALU.mult),
               reads=[B["pp"]], writes=[B["lprod"]])
        S.emit("dve", lambda e: e.reduce_sum(out=small[:, 57:58], in_=lprod[:, 0:64], axis=AX.X),
               reads=[B["lprod"]], writes=[Bsm])
        S.emit("dve", lambda e: e.reduce_sum(out=small[:, 58:59], in_=lprod[:, 64:128], axis=AX.X),
               reads=[B["lprod"]], writes=[Bsm])
        S.emit("act", lambda e: e.activation(out=small[:, 59:61], in_=small[:, 57:59], func=AF.Exp),
               reads=[Bsm], writes=[Bsm])
        S.emit("dve", lambda e: e.tensor_tensor(out=small[:, 61:62], in0=small[:, 60:61], in1=small[:, 59:60], op=ALU.subtract),
               reads=[Bsm], writes=[Bsm])
        S.emit("dve", lambda e: e.tensor_scalar(out=small[:, SM_NLAM:SM_NLAM + 1], in0=small[:, 61:62],
                                                scalar1=-LAM_INIT, scalar2=None, op0=ALU.add),
               reads=[Bsm], writes=[Bsm])
        S.emit("dve", lambda e: e.tensor_scalar(out=small[:, SM_HGS:SM_HGS + 1], in0=pp[:, PP_HG:PP_HG + 1],
                                                scalar1=(1.0 - LAM_INIT), scalar2=None, op0=ALU.mult),
               reads=[B["pp"]], writes=[Bsm])

        ring_i = [0]

        def load_slab(name, idx):
            slot = ring_i[0] % RING
            ring_i[0] += 1
            w = wwidth[name]
            S.emit("sp", lambda e: e.dma_start(out=ring[:, slot, 0:w], in_=wb16[name][idx]),
                   reads=[Bw[name]], writes=[Bring[slot]], dsem=f"rg{slot}")
            return slot

        def a_slabs(k0, k1):
            L = []
            for k in range(k0, k1):
                if k < NCH:
                    L.append(("win_u", k))
                if 0 <= k - 2 < NCH:
                    L.append(("win_g", k - 2))
                    L.append(("gts", k - 2))
                if k - 3 == NCH - 3:
                    for m in range(4):
                        L.append(("wout", m))
            return L

        import os as _os
        _lim = int(_os.environ.get("KTILES", str(NT)))
        tile_specs = list(range(_lim)) + (["s"] if _os.environ.get("KNOS", "0") == "0" else [])
        HOIST_ = 8
        slab_seq = []
        for i_, _ in enumerate(tile_specs):
            hoisted_in = (i_ > 0)
            slab_seq += a_slabs(HOIST_ if hoisted_in else 0, NCH + 3)
            slab_seq += [("wout", m) for m in range(4, 8)]
            slab_seq += [("wk", h) for h in range(8)] + [("wv", x) for x in range(8)]
            slab_seq += [("wq", h) for h in range(8)] + [("wbg", m) for m in range(8)]
            if i_ + 1 < len(tile_specs):
                slab_seq += a_slabs(0, HOIST_)
            slab_seq += [("wbo", m) for m in range(8)]
        slab_pos = [0]
        slab_loaded = [0]
        slab_slots = {}

        slab_pin = [None]

        def prefetch():
            base = slab_pos[0] if slab_pin[0] is None else min(slab_pos[0], slab_pin[0])
            while slab_loaded[0] < len(slab_seq) and slab_loaded[0] < base + RING - 1:
                n, i = slab_seq[slab_loaded[0]]
                slab_slots[slab_loaded[0]] = load_slab(n, i)
                slab_loaded[0] += 1

        def next_slab(name, idx):
            p = slab_pos[0]
            assert slab_seq[p] == (name, idx), (slab_seq[p], name, idx)
            if p >= slab_loaded[0]:
                prefetch_force(p)
            slot = slab_slots.pop(p)
            slab_pos[0] += 1
            return slot

        def prefetch_force(p):
            while slab_loaded[0] <= p:
                n, i = slab_seq[slab_loaded[0]]
                slab_slots[slab_loaded[0]] = load_slab(n, i)
                slab_loaded[0] += 1

        deferred = []

        def flush_stores():
            for f in deferred:
                f()
            deferred.clear()

        def store(out_ap, in_ap, rbufs, sem, wbufs=()):
            store_sems.add(sem)

            def f():
                S.emit("sp", lambda e: e.dma_start(out=out_ap, in_=in_ap), reads=rbufs, writes=list(wbufs), dsem=sem)
            deferred.append(f)

        def norm_square(src, bsrc, kc, N):
            S.emit("dve", lambda e: e.tensor_tensor(out=on[:, kc, 0:N], in0=src[:, kc, 0:N], in1=src[:, kc, 0:N], op=ALU.mult),
                   reads=[bsrc], writes=[Bon[kc]])

        def norm_sum_mm(bk, kc, N):
            S.emit("pe", lambda e: e.matmul(psb(bk)[:, 0:N], lhsT=ones_b, rhs=on[:, kc, 0:N], start=(kc == 0), stop=(kc == 7)),
                   reads=[Bon[kc], B["cmb"]], writes=[Bps[bk]], sig=True)

        def norm_rstd(bk, N):
            t = T32.alloc()
            S.emit("act", lambda e: e.activation(out=T32.ap(t)[:, 0:N], in_=psb(bk)[:, 0:N], func=AF.Ln, scale=1.0 / D, bias=epsc),
                   reads=[Bps[bk], Bsm], writes=[T32.buf(t)])
            S.emit("act", lambda e: e.activation(out=T32.ap(t)[:, 0:N], in_=T32.ap(t)[:, 0:N], func=AF.Exp, scale=-0.5),
                   reads=[T32.buf(t)], writes=[T32.buf(t)])
            return t

        def norm_apply(src, bsrc, gcol, t, dst, bdst, N):
            for kc in range(8):
                S.emit("dve", lambda e, kc=kc: e.scalar_tensor_tensor(out=dst[:, kc, 0:N], in0=src[:, kc, 0:N],
                                                                      scalar=ppc(gcol + kc), in1=T32.ap(t)[:, 0:N],
                                                                      op0=ALU.mult, op1=ALU.mult),
                       reads=[bsrc, T32.buf(t), B["pp"]], writes=[bdst[kc]])

        def tile_params(spec):
            is_s = spec == "s"
            return dict(N=NS if is_s else TN, tok0=0 if is_s else spec * TN, xsrc=xT_s if is_s else xT_p)

        def load_x(spec):
            p_ = tile_params(spec)
            S.emit("sp", lambda e: e.dma_start(out=xin[:, :, 0:p_["N"]],
                                               in_=p_["xsrc"].rearrange("(kc p) n -> p kc n", p=128)[:, :, p_["tok0"]:p_["tok0"] + p_["N"]]),
                   writes=[B["xin"]], dsem="ld_x")

        def a_norm(spec):
            N = tile_params(spec)["N"]
            for kc in range(8):
                norm_square(xin, B["xin"], kc, N)
            bk = bank()
            for kc in range(8):
                norm_sum_mm(bk, kc, N)
            t = norm_rstd(bk, N)
            norm_apply(xin, B["xin"], PP_AN, t, xn, Bxn, N)
            T32.release(t)

        def proj_fm(slot, nk, rhs_of, rbufs, N, coff=0):
            bk = bank()
            for k in range(nk):
                S.emit("pe", lambda e, k=k: e.matmul(psb(bk)[:, 0:N], lhsT=ring[:, slot, coff + k * 128: coff + (k + 1) * 128],
                                                      rhs=rhs_of(k), start=(k == 0), stop=(k == nk - 1)),
                       reads=[Bring[slot]] + (rbufs(k) if callable(rbufs) else rbufs), writes=[Bps[bk]], sig=(k == nk - 1))
            return bk

        def qk_stage1(kind, h, N, xsrc_ap, bxsrc):
            slot = next_slab("wk" if kind == "k" else "wq", h)
            prefetch()
            bk = proj_fm(slot, 8, lambda k: xsrc_ap[:, k, 0:N], (lambda k: [bxsrc[k]]), N)
            kb = T16.alloc()
            sq = T16.alloc()
            S.emit("act", lambda e: e.activation(out=T16.ap(kb)[:, 0:N], in_=psb(bk)[:, 0:N], func=AF.Copy),
                   reads=[Bps[bk]], writes=[T16.buf(kb)])
            S.emit("act", lambda e: e.activation(out=T16.ap(sq)[:, 0:N], in_=psb(bk)[:, 0:N], func=AF.Square),
                   reads=[Bps[bk]], writes=[T16.buf(sq)])
            return (bk, kb, sq)

        def qk_stage2(kind, st, N, ropei, dst_fn):
            bk, kb, sq = st
            bsw = bank()
            S.emit("pe", lambda e: e.matmul(psb(bsw)[:, 0:N], lhsT=perm_b, rhs=T16.ap(kb)[:, 0:N], start=True, stop=True),
                   reads=[T16.buf(kb), B["cmb"]], writes=[Bps[bsw]])
            bss = bank()
            S.emit("pe", lambda e: e.matmul(psb(bss)[:, 0:N], lhsT=blk_b, rhs=T16.ap(sq)[:, 0:N], start=True, stop=True),
                   reads=[T16.buf(sq), B["cmb"]], writes=[Bps[bss]])
            T16.release(kb)
            T16.release(sq)
            rk = T32.alloc()
            S.emit("act", lambda e: e.activation(out=T32.ap(rk)[:, 0:N], in_=psb(bss)[:, 0:N], func=AF.Ln, scale=1.0 / 64, bias=epsc),
                   reads=[Bps[bss], Bsm], writes=[T32.buf(rk)])
            S.emit("act", lambda e: e.activation(out=T32.ap(rk)[:, 0:N], in_=T32.ap(rk)[:, 0:N], func=AF.Exp, scale=-0.5),
                   reads=[T32.buf(rk)], writes=[T32.buf(rk)])
            g0, g1 = (PP_GK, PP_GKS) if kind == "k" else (PP_GQ, PP_GQS)
            u1 = T32.alloc()
            u2 = T32.alloc()
            S.emit("dve", lambda e: e.scalar_tensor_tensor(out=T32.ap(u1)[:, 0:N], in0=psb(bk)[:, 0:N], scalar=ppc(g0),
                                                           in1=rope[:, ropei, 0, 0:N], op0=ALU.mult, op1=ALU.mult),
                   reads=[Bps[bk], Brope[ropei], B["pp"]], writes=[T32.buf(u1)])
            S.emit("dve", lambda e: e.scalar_tensor_tensor(out=T32.ap(u2)[:, 0:N], in0=psb(bsw)[:, 0:N], scalar=ppc(g1),
                                                           in1=rope[:, ropei, 1, 0:N], op0=ALU.mult, op1=ALU.mult),
                   reads=[Bps[bsw], Brope[ropei], B["pp"]], writes=[T32.buf(u2)])
            S.emit("dve", lambda e: e.tensor_tensor(out=T32.ap(u1)[:, 0:N], in0=T32.ap(u1)[:, 0:N], in1=T32.ap(u2)[:, 0:N], op=ALU.add),
                   reads=[T32.buf(u1), T32.buf(u2)], writes=[T32.buf(u1)])
            T32.release(u2)
            dst_fn(u1, rk)
            T32.release(u1)
            T32.release(rk)

        def qk_heads(kind, N, ropei, xsrc_ap, bxsrc, dst_of):
            st = qk_stage1(kind, 0, N, xsrc_ap, bxsrc)
            for h in range(NH):
                nxt = qk_stage1(kind, h + 1, N, xsrc_ap, bxsrc) if h + 1 < NH else None
                qk_stage2(kind, st, N, ropei, dst_of(h))
                st = nxt

        def make_aphase(spec):
            is_s = spec == "s"
            N = NS if is_s else TN
            tix = 0 if is_s else spec
            ucarry, hcarry = (ucarry_s, hcarry_s) if is_s else (ucarry_p, hcarry_p)
            Buc, Bhc = (B["ucarry_s"], B["hcarry_s"]) if is_s else (B["ucarry_p"], B["hcarry_p"])
            ucf = {}

            def sigmoid_from_psum(tb_, bk_):
                S.emit("act", lambda e: e.activation(out=T32.ap(tb_)[:, 0:N], in_=psb(bk_)[:, 0:N], func=AF.Exp, scale=-1.0),
                       reads=[Bps[bk_]], writes=[T32.buf(tb_)])
                S.emit("act", lambda e: e.activation(out=T32.ap(tb_)[:, 0:N], in_=T32.ap(tb_)[:, 0:N], func=AF.Ln, scale=1.0, bias=onec),
                       reads=[T32.buf(tb_), Bsm], writes=[T32.buf(tb_)])
                S.emit("act", lambda e: e.activation(out=T32.ap(tb_)[:, 0:N], in_=T32.ap(tb_)[:, 0:N], func=AF.Exp, scale=-1.0),
                       reads=[T32.buf(tb_)], writes=[T32.buf(tb_)])

            def stage_u(c):
                slot = next_slab("win_u", c)
                prefetch()
                bk = proj_fm(slot, 8, lambda k: xn[:, k, 0:N], (lambda k: [Bxn[k]]), N)
                ur = T32.alloc()
                uf = T32.alloc()
                S.emit("dve", lambda e: e.tensor_copy(out=T32.ap(ur)[:, 0:3], in_=ucarry[:, c, :]),
                       reads=[Buc], writes=[T32.buf(ur)])
                S.emit("dve", lambda e: e.tensor_copy(out=T32.ap(ur)[:, 3:3 + N], in_=psb(bk)[:, 0:N]),
                       reads=[Bps[bk]], writes=[T32.buf(ur)])
                S.emit("act", lambda e: e.activation(out=T32.ap(uf)[:, 0:N], in_=T32.ap(ur)[:, 3:3 + N], func=AF.Identity,
                                                     scale=ppc(PP_CW + 4 * c + 3), bias=ppc(PP_CB + c)),
                       reads=[T32.buf(ur), B["pp"]], writes=[T32.buf(uf)])
                for j in (2, 1, 0):
                    S.emit("dve", lambda e, j=j: e.scalar_tensor_tensor(out=T32.ap(uf)[:, 0:N], in0=T32.ap(ur)[:, j:j + N],
                                                                        scalar=ppc(PP_CW + 4 * c + j), in1=T32.ap(uf)[:, 0:N],
                                                                        op0=ALU.mult, op1=ALU.add),
                           reads=[T32.buf(ur), T32.buf(uf), B["pp"]], writes=[T32.buf(uf)])
                S.emit("dve", lambda e: e.tensor_copy(out=ucarry[:, c, :], in_=T32.ap(ur)[:, N:N + 3]),
                       reads=[T32.buf(ur)], writes=[Buc])
                S.emit("pool", lambda e: e.tensor_copy(out=b16[:, c, 0:N], in_=T32.ap(uf)[:, 0:N]),
                       reads=[T32.buf(uf)], writes=[Bb16[c]])
                T32.release(ur)
                ucf[c] = uf

            gst = {}

            def g1(c):
                slot2 = next_slab("win_g", c)
                prefetch()
                bkg = proj_fm(slot2, 8, lambda k: xn[:, k, 0:N], (lambda k: [Bxn[k]]), N)
                tg = T32.alloc()
                sigmoid_from_psum(tg, bkg)
                S.emit("dve", lambda e: e.tensor_tensor(out=T32.ap(tg)[:, 0:N], in0=T32.ap(tg)[:, 0:N], in1=psb(bkg)[:, 0:N], op=ALU.mult),
                       reads=[T32.buf(tg), Bps[bkg]], writes=[T32.buf(tg)])
                slot = next_slab("gts", c)
                prefetch()
                dks = [dk for dk in range(3) if 0 <= c + dk - 1 < NCH]
                bkr = bank()
                bki = bank()
                for g, bk in ((0, bkr), (1, bki)):
                    for n_, dk in enumerate(dks):
                        S.emit("pe", lambda e, g=g, bk=bk, dk=dk, n_=n_: e.matmul(
                            psb(bk)[:, 0:N], lhsT=ring[:, slot, (dk * 2 + g) * 128:(dk * 2 + g + 1) * 128],
                            rhs=b16[:, c + dk - 1, 0:N], start=(n_ == 0), stop=(n_ == len(dks) - 1)),
                            reads=[Bring[slot], Bb16[c + dk - 1]], writes=[Bps[bk]], sig=(n_ == len(dks) - 1))
                tr = T32.alloc()
                ti = T32.alloc()
                a = T32.alloc()
                m = T32.alloc()
                S.emit("act", lambda e: e.activation(out=T32.ap(tr)[:, 0:N], in_=psb(bkr)[:, 0:N], func=AF.Exp,
                                                     scale=-1.0, bias=smc(SM_HBR + c)),
                       reads=[Bps[bkr], Bsm], writes=[T32.buf(tr)])
                S.emit("act", lambda e: e.activation(out=T32.ap(ti)[:, 0:N], in_=psb(bki)[:, 0:N], func=AF.Exp,
                                                     scale=-1.0, bias=smc(SM_HBI + c)),
                       reads=[Bps[bki], Bsm], writes=[T32.buf(ti)])
                S.emit("act", lambda e: e.activation(out=T32.ap(tr)[:, 0:N], in_=T32.ap(tr)[:, 0:N], func=AF.Ln, scale=1.0, bias=onec),
                       reads=[T32.buf(tr), Bsm], writes=[T32.buf(tr)])
                S.emit("act", lambda e: e.activation(out=T32.ap(ti)[:, 0:N], in_=T32.ap(ti)[:, 0:N], func=AF.Ln, scale=1.0, bias=onec),
                       reads=[T32.buf(ti), Bsm], writes=[T32.buf(ti)])
                S.emit("act", lambda e: e.activation(out=T32.ap(tr)[:, 0:N], in_=T32.ap(tr)[:, 0:N], func=AF.Exp, scale=-1.0),
                       reads=[T32.buf(tr)], writes=[T32.buf(tr)])
                S.emit("act", lambda e: e.activation(out=T32.ap(ti)[:, 0:N], in_=T32.ap(ti)[:, 0:N], func=AF.Exp, scale=-1.0),
                       reads=[T32.buf(ti)], writes=[T32.buf(ti)])
                S.emit("act", lambda e: e.activation(out=T32.ap(a)[:, 0:N], in_=T32.ap(tr)[:, 0:N], func=AF.Exp, scale=smc(SM_HC + c)),
                       reads=[T32.buf(tr), Bsm], writes=[T32.buf(a)])
                T32.release(tr)
                S.emit("pool", lambda e: e.tensor_tensor(out=T32.ap(m)[:, 0:N], in0=T32.ap(a)[:, 0:N], in1=T32.ap(a)[:, 0:N], op=ALU.mult),
                       reads=[T32.buf(a)], writes=[T32.buf(m)])
                gst[c] = (tg, ti, a, m)

            def g2_act(c):
                tg, ti, a, m = gst[c]
                S.emit("act", lambda e: e.activation(out=T32.ap(m)[:, 0:N], in_=T32.ap(m)[:, 0:N], func=AF.Ln, scale=-1.0, bias=onec),
                       reads=[T32.buf(m), Bsm], writes=[T32.buf(m)])
                S.emit("act", lambda e: e.activation(out=T32.ap(m)[:, 0:N], in_=T32.ap(m)[:, 0:N], func=AF.Exp, scale=0.5),
                       reads=[T32.buf(m)], writes=[T32.buf(m)])
                if (not is_s) and tix == 0:
                    S.emit("pool", lambda e: e.memset(T32.ap(m)[:, 0:1], 1.0), writes=[T32.buf(m)])

            def g2_dve1(c):
                tg, ti, a, m = gst[c]
                uf = ucf.pop(c)
                S.emit("dve", lambda e: e.tensor_tensor(out=T32.ap(ti)[:, 0:N], in0=T32.ap(ti)[:, 0:N], in1=T32.ap(uf)[:, 0:N], op=ALU.mult),
                       reads=[T32.buf(ti), T32.buf(uf)], writes=[T32.buf(ti)])
                T32.release(uf)
                S.emit("dve", lambda e: e.tensor_tensor(out=T32.ap(ti)[:, 0:N], in0=T32.ap(ti)[:, 0:N], in1=T32.ap(m)[:, 0:N], op=ALU.mult),
                       reads=[T32.buf(ti), T32.buf(m)], writes=[T32.buf(ti)])
                T32.release(m)
                hh = T32.alloc()
                S.emit("dve", lambda e: e.tensor_tensor_scan(out=T32.ap(hh)[:, 0:N], data0=T32.ap(a)[:, 0:N], data1=T32.ap(ti)[:, 0:N],
                                                             initial=hcarry[:, c:c + 1], op0=ALU.mult, op1=ALU.add),
                       reads=[T32.buf(a), T32.buf(ti), Bhc], writes=[T32.buf(hh)])
                T32.release(a)
                T32.release(ti)
                S.emit("dve", lambda e: e.tensor_copy(out=hcarry[:, c:c + 1], in_=T32.ap(hh)[:, N - 1:N]),
                       reads=[T32.buf(hh)], writes=[Bhc])
                gst[c] = (tg, hh)

            def g2_dve2(c):
                tg, hh = gst.pop(c)
                S.emit("dve", lambda e: e.tensor_tensor(out=b16[:, 11 + c, 0:N], in0=T32.ap(hh)[:, 0:N], in1=T32.ap(tg)[:, 0:N], op=ALU.mult),
                       reads=[T32.buf(hh), T32.buf(tg)], writes=[Bb16[11 + c]])
                T32.release(hh)
                T32.release(tg)

            aout = {}

            def aout_part(cs):
                if "slots" not in aout:
                    slab_pin[0] = slab_pos[0]
                    aout["slots"] = [next_slab("wout", j) for j in range(4)]
                    prefetch()
                    aout["banks"] = [bank() for _ in range(4)]
                    bank_held.update(aout["banks"])
                for c in cs:
                    for j in range(4):
                        S.emit("pe", lambda e, c=c, j=j: e.matmul(psb(aout["banks"][j])[:, 0:N],
                                                                  lhsT=ring[:, aout["slots"][j], c * 128:(c + 1) * 128],
                                                                  rhs=b16[:, 11 + c, 0:N], start=(c == 0), stop=(c == NCH - 1)),
                               reads=[Bring[aout["slots"][j]], Bb16[11 + c]], writes=[Bps[aout["banks"][j]]], sig=(j == 3))

            def run_k(k):
                c1, c2 = k - 2, k - 3
                if 0 <= c2 < NCH:
                    g2_act(c2)
                if k < NCH:
                    stage_u(k)
                if 0 <= c2 < NCH:
                    g2_dve1(c2)
                if 0 <= c1 < NCH:
                    g1(c1)
                if 0 <= c2 < NCH:
                    g2_dve2(c2)
                if c2 == NCH - 3:
                    aout_part(range(0, NCH - 2))
                elif c2 == NCH - 2:
                    aout_part([NCH - 2])
                elif c2 == NCH - 1:
                    aout_part([NCH - 1])
                    slab_pin[0] = None

            return {"run_k": run_k, "aout": aout, "sigmoid": sigmoid_from_psum, "k_done": 0}

        aph_pending = {}
        HOIST = 8

        def run_tile(spec, next_spec, first):
            is_s = spec == "s"
            N = NS if is_s else TN
            tix = 0 if is_s else spec
            tok0 = 0 if is_s else spec * TN
            xsrc = xT_s if is_s else xT_p
            ysrc = yT_s if is_s else yT_p
            kTo = kT_s if is_s else kT_p
            vo = v_s if is_s else v_p
            ucarry, hcarry = (ucarry_s, hcarry_s) if is_s else (ucarry_p, hcarry_p)
            Buc, Bhc = (B["ucarry_s"], B["hcarry_s"]) if is_s else (B["ucarry_p"], B["hcarry_p"])
            rope_off = SEQ if is_s else tok0
            ropei = (0 if is_s else (spec + 1)) % 2
            NB = (N + 127) // 128

            S.emit("sp", lambda e: e.dma_start(out=rope[:, ropei, :, 0:N], in_=rope_in[:, :, rope_off:rope_off + N]),
                   writes=[Brope[ropei]], dsem=f"ld_rope{ropei}")
            prefetch()

            if first:
                a_norm(spec)
            aph = aph_pending.pop(spec, None) or make_aphase(spec)
            aout = aph["aout"]
            sigmoid_from_psum = aph["sigmoid"]
            for k in range(aph["k_done"], NCH + 3):
                aph["run_k"](k)

            bkn = bank()
            bank_held.add(bkn)
            for m_ in range(8):
                if m_ < 4:
                    bk = aout["banks"][m_]
                else:
                    slot = next_slab("wout", m_)
                    prefetch()
                    bk = proj_fm(slot, NCH, lambda k: b16[:, 11 + k, 0:N], [Bb16[11 + k] for k in range(NCH)], N)
                if m_ == 4:
                    for b_ in aout["banks"]:
                        bank_held.discard(b_)
                if m_ >= 1:
                    norm_sum_mm(bkn, m_ - 1, N)
                S.emit("dve", lambda e, m_=m_, bk=bk: e.tensor_tensor(out=x1[:, m_, 0:N], in0=xin[:, m_, 0:N], in1=psb(bk)[:, 0:N], op=ALU.add),
                       reads=[B["xin"], Bps[bk]], writes=[B["x1"]])
                norm_square(x1, B["x1"], m_, N)
            norm_sum_mm(bkn, 7, N)
            bank_held.discard(bkn)
            t_x1 = norm_rstd(bkn, N)
            if next_spec is not None:
                load_x(next_spec)
            if is_s:
                store(conv_s_o, ucarry_s[:].rearrange("p c j -> p (c j)"), [B["ucarry_s"]], "st_misc", [B["conv_s_o"]])
                store(h_s_o, hcarry_s[:], [B["hcarry_s"]], "st_misc", [B["h_s_o"]])
            elif tix == NT - 1:
                store(conv_p_o, ucarry_p[:].rearrange("p c j -> p (c j)"), [B["ucarry_p"]], "st_misc", [B["conv_p_o"]])
                store(h_p_o, hcarry_p[:], [B["hcarry_p"]], "st_misc", [B["h_p_o"]])
            flush_stores()

            norm_apply(x1, B["x1"], PP_KVN, t_x1, xn, Bxn, N)
            def kdst_of(h):
                def kdst(u1, rk, h=h):
                    kst = T32.alloc()
                    S.emit("dve", lambda e: e.tensor_tensor(out=T32.ap(kst)[:, 0:N], in0=T32.ap(u1)[:, 0:N], in1=T32.ap(rk)[:, 0:N], op=ALU.mult),
                           reads=[T32.buf(u1), T32.buf(rk)], writes=[T32.buf(kst)])
                    S.emit("pool", lambda e: e.tensor_copy(out=b16[:, h, 0:N], in_=T32.ap(kst)[:, 0:N]),
                           reads=[T32.buf(kst)], writes=[Bb16[h]])
                    store_sems.add(f"st32_{kst}")
                    S.emit("sp", lambda e: e.dma_start(out=kTo[h, :, tok0:tok0 + N], in_=T32.ap(kst)[:, 0:N]),
                           reads=[T32.buf(kst)], dsem=f"st32_{kst}")
                    T32.release(kst)
                return kdst
            qk_heads("k", N, ropei, xn, Bxn, kdst_of)
            if not is_s:
                S.emit("sp", lambda e: e.dma_start(out=kscr.rearrange("h p n -> p h n")[:, :, tok0:tok0 + N], in_=b16[:, 0:8, 0:N]),
                       reads=[Bb16[h] for h in range(8)], writes=[B["kscr"]], dsem="st_kscr")
            norm_apply(x1, B["x1"], PP_BN, t_x1, xn2, Bxn2, N)
            T32.release(t_x1)
            for fh in range(2):
                bks = [bank() for _ in range(NB)]
                for sp_ in range(4):
                    slot = next_slab("wv", fh * 4 + sp_)
                    prefetch()
                    for tb in range(NB):
                        nt = min(128, N - tb * 128)
                        for kk in range(2):
                            kc = 2 * sp_ + kk
                            last = (sp_ == 3 and kk == 1)
                            S.emit("pe", lambda e, tb=tb, nt=nt, kk=kk, kc=kc, last=last, slot=slot, sp_=sp_, bks=bks: e.matmul(
                                psb(bks[tb])[0:nt, :], lhsT=xn[:, kc, tb * 128: tb * 128 + nt],
                                rhs=ring[:, slot, kk * 512:(kk + 1) * 512], start=(sp_ == 0 and kk == 0), stop=last),
                                reads=[Bxn[kc], Bring[slot]], writes=[Bps[bks[tb]]], sig=(kk == 1))
                for tb in range(NB):
                    nt = min(128, N - tb * 128)
                    vs = T32.alloc()
                    S.emit("act", lambda e, tb=tb, nt=nt, vs=vs, bks=bks: e.activation(out=T32.ap(vs)[0:nt, 0:512], in_=psb(bks[tb])[0:nt, :], func=AF.Copy),
                           reads=[Bps[bks[tb]]], writes=[T32.buf(vs)])
                    S.emit("pool", lambda e, tb=tb, nt=nt, vs=vs, fh=fh: e.tensor_copy(
                        out=b16[0:nt, 11 + 4 * fh: 11 + 4 * fh + 4, tb * 128:(tb + 1) * 128],
                        in_=T32.ap(vs)[0:nt, 0:512].rearrange("p (h e) -> p h e", e=128)),
                        reads=[T32.buf(vs)], writes=[Bb16[11 + 4 * fh + i] for i in range(4)])
                    store_sems.add(f"st32_{vs}")
                    S.emit("sp", lambda e, tb=tb, nt=nt, vs=vs, fh=fh: e.dma_start(
                        out=vo[tok0 + tb * 128: tok0 + tb * 128 + nt, fh * 512:(fh + 1) * 512], in_=T32.ap(vs)[0:nt, 0:512]),
                        reads=[T32.buf(vs)], dsem=f"st32_{vs}")
                    T32.release(vs)
            if not is_s:
                S.emit("sp", lambda e: e.dma_start(
                    out=vscr.rearrange("h p kb e -> p h kb e")[:, :, 4 * tix:4 * tix + 4, :],
                    in_=b16[:, 11:19, :].rearrange("p h (tb e) -> p h tb e", e=128)),
                    reads=[Bb16[11 + h] for h in range(8)], writes=[B["vscr"]], dsem="st_vscr")

            nkeys_past = PAST if is_s else (tix + 1) * TN

            def load_head(h):
                bi = h % 2
                if is_s:
                    S.emit("pool", lambda e: e.dma_start(out=kTh[:, bi, 0:PAST], in_=ckT[h]),
                           writes=[BkTh[bi]], dsem=f"ldp_k{bi}")
                    S.emit("pool", lambda e: e.dma_start(
                        out=vh[:, bi, 0:8, :], in_=cv.rearrange("(kb p) (h e) -> h p kb e", p=128, e=128)[h]),
                        writes=[Bvh[bi]], dsem=f"ldp_v{bi}")
                else:
                    S.emit("sp", lambda e: e.dma_start(out=kTh[:, bi, 0:nkeys_past], in_=kscr[h, :, 0:nkeys_past]),
                           reads=[B["kscr"]], writes=[BkTh[bi]], dsem=f"ld_k{bi}")
                    S.emit("sp", lambda e: e.dma_start(out=vh[:, bi, 0:4 * (tix + 1), :], in_=vscr[h, :, 0:4 * (tix + 1), :]),
                           reads=[B["vscr"]], writes=[Bvh[bi]], dsem=f"ld_v{bi}")

            def finish_head_load(h):
                bi = h % 2
                if is_s:
                    S.emit("pool", lambda e: e.tensor_copy(out=kTh[:, bi, PAST:PAST + NS], in_=b16[:, h, 0:NS]),
                           reads=[Bb16[h]], writes=[BkTh[bi]])
                    S.emit("pool", lambda e: e.tensor_copy(out=vh[0:NS, bi, 8, :], in_=b16[0:NS, 11 + h, 0:128]),
                           reads=[Bb16[11 + h]], writes=[Bvh[bi]])

            if is_s:
                pass

            if is_s:
                qbase, gbase = 8, 19
            if is_s:
                def qT_ap(h):
                    return kTh[:, 0, 2048 + h * NS: 2048 + (h + 1) * NS]

                def sg_ap(m_):
                    return kTh[:, 0, 3072 + m_ * NS: 3072 + (m_ + 1) * NS]
                BqT = [Buf(f"qTs{h}") for h in range(8)]
                Bsg = [Buf(f"sgs{h}") for h in range(8)]
            else:
                def qT_ap(h):
                    return b16[:, h, 0:N]

                def sg_ap(m_):
                    return b16[:, 11 + m_, 0:N]
                BqT = [Bb16[h] for h in range(8)]
                Bsg = [Bb16[11 + h] for h in range(8)]

            if is_s:
                pass

            def qdst_of(h):
                def qdst(u1, rk, h=h):
                    S.emit("dve", lambda e: e.tensor_tensor(out=qT_ap(h), in0=T32.ap(u1)[:, 0:N], in1=T32.ap(rk)[:, 0:N], op=ALU.mult),
                           reads=[T32.buf(u1), T32.buf(rk)], writes=[BqT[h]])
                return qdst
            qk_heads("q", N, ropei, xn2, Bxn2, qdst_of)
            for m_ in range(8):
                slot = next_slab("wbg", m_)
                prefetch()
                bkg = proj_fm(slot, 8, lambda k: xn2[:, k, 0:N], (lambda k: [Bxn2[k]]), N)
                tg = T32.alloc()
                sigmoid_from_psum(tg, bkg)
                S.emit("dve", lambda e, tg=tg, bkg=bkg, m_=m_: e.tensor_tensor(out=sg_ap(m_), in0=T32.ap(tg)[:, 0:N], in1=psb(bkg)[:, 0:N], op=ALU.mult),
                       reads=[T32.buf(tg), Bps[bkg]], writes=[Bsg[m_]])
                T32.release(tg)
            flush_stores()

            if next_spec is not None:
                a_norm(next_spec)
            def blocks():
                L = []
                if is_s:
                    for kb_ in range(8):
                        L.append((kb_ * 128, 128, kb_, 0, False))
                    L.append((PAST, NS, 8, 0, False))
                else:
                    for kd in range(4):
                        L.append(((4 * tix + kd) * 128, 128, 4 * tix + kd, 128 * kd, True))
                    for kb_ in range(4 * tix):
                        L.append((kb_ * 128, 128, kb_, 0, False))
                return L

            blks = blocks()
            nblk = len(blks)
            steps = [(h, i) for h in range(NH) for i in range(nblk)]
            O1, O2, L1, L2 = 4, 5, 6, 7
            load_head(0)
            finish_head_load(0)
            load_head(1)
            finish_head_load(1)

            def emit_qk(s):
                h, i = steps[s]
                col0, nk, vb, q0, corner = blks[i]
                sp_ = s % 2
                bi = h % 2
                S.emit("pe", lambda e: e.matmul(psum[sp_][0:nk, 0, q0:N], lhsT=kTh[0:64, bi, col0:col0 + nk],
                                                rhs=qT_ap(h)[0:64, q0:N] if not is_s else qT_ap(h)[0:64, :], start=True, stop=True),
                       reads=[BkTh[bi], BqT[h]], writes=[Bps[2 * sp_]], sig=False)
                S.emit("pe", lambda e: e.matmul(psum[sp_][0:nk, 1, q0:N], lhsT=kTh[64:128, bi, col0:col0 + nk],
                                                rhs=qT_ap(h)[64:128, q0:N] if not is_s else qT_ap(h)[64:128, :], start=True, stop=True),
                       reads=[BkTh[bi], BqT[h]], writes=[Bps[2 * sp_ + 1]])

            def emit_exp_pv(s):
                h, i = steps[s]
                col0, nk, vb, q0, corner = blks[i]
                sp_ = s % 2
                pi = s % 3
                bi = h % 2
                first = (i == 0)
                last = (i == nblk - 1)
                S.emit("act", lambda e: e.activation(out=pt[0:nk, pi, :, q0:N], in_=psum[sp_][0:nk, :, q0:N], func=AF.Exp, scale=0.125),
                       reads=[Bps[2 * sp_], Bps[2 * sp_ + 1]], writes=[Bpt[pi]])
                if corner:
                    S.emit("pool", lambda e: e.memset(pt[64:128, pi, :, q0:q0 + 64], 0.0), writes=[Bpt[pi]])
                for c_, ob in ((0, O1), (1, O2)):
                    S.emit("pe", lambda e, c_=c_, ob=ob: e.matmul(psb(ob)[:, q0:N], lhsT=vh[0:nk, bi, vb, :], rhs=pt[0:nk, pi, c_, q0:N],
                                                                   start=first, stop=last),
                           reads=[Bvh[bi], Bpt[pi]], writes=[Bps[ob]], sig=False)
                for c_, lb in ((0, L1), (1, L2)):
                    S.emit("pe", lambda e, c_=c_, lb=lb: e.matmul(psb(lb)[:, q0:N], lhsT=ones_b[0:nk, :], rhs=pt[0:nk, pi, c_, q0:N],
                                                                   start=first, stop=last),
                           reads=[B["cmb"], Bpt[pi]], writes=[Bps[lb]], sig=(c_ == 1))

            fin_state = {}

            def fin_a(h):
                r1 = T32.alloc()
                r2 = T32.alloc()
                o1 = T32.alloc()
                o2 = T32.alloc()
                S.emit("dve", lambda e: e.tensor_copy(out=T32.ap(o1)[:, 0:N], in_=psb(O1)[:, 0:N]), reads=[Bps[O1]], writes=[T32.buf(o1)])
                S.emit("dve", lambda e: e.tensor_copy(out=T32.ap(o2)[:, 0:N], in_=psb(O2)[:, 0:N]), reads=[Bps[O2]], writes=[T32.buf(o2)])
                S.emit("act", lambda e: e.activation(out=T32.ap(r1)[:, 0:N], in_=psb(L1)[:, 0:N], func=AF.Ln), reads=[Bps[L1]], writes=[T32.buf(r1)])
                S.emit("act", lambda e: e.activation(out=T32.ap(r2)[:, 0:N], in_=psb(L2)[:, 0:N], func=AF.Ln), reads=[Bps[L2]], writes=[T32.buf(r2)])
                fin_state[h] = (r1, r2, o1, o2)

            def fin_a2(h):
                r1, r2, o1, o2 = fin_state.pop(h)
                S.emit("act", lambda e: e.activation(out=T32.ap(r1)[:, 0:N], in_=T32.ap(r1)[:, 0:N], func=AF.Exp, scale=-1.0),
                       reads=[T32.buf(r1)], writes=[T32.buf(r1)])
                S.emit("act", lambda e: e.activation(out=T32.ap(r2)[:, 0:N], in_=T32.ap(r2)[:, 0:N], func=AF.Exp, scale=-1.0),
                       reads=[T32.buf(r2)], writes=[T32.buf(r2)])
                S.emit("dve", lambda e: e.tensor_tensor(out=T32.ap(r1)[:, 0:N], in0=T32.ap(o1)[:, 0:N], in1=T32.ap(r1)[:, 0:N], op=ALU.mult),
                       reads=[T32.buf(o1), T32.buf(r1)], writes=[T32.buf(r1)])
                S.emit("dve", lambda e: e.tensor_tensor(out=T32.ap(r2)[:, 0:N], in0=T32.ap(o2)[:, 0:N], in1=T32.ap(r2)[:, 0:N], op=ALU.mult),
                       reads=[T32.buf(o2), T32.buf(r2)], writes=[T32.buf(r2)])
                T32.release(o1)
                T32.release(o2)
                S.emit("dve", lambda e: e.scalar_tensor_tensor(out=T32.ap(r1)[:, 0:N], in0=T32.ap(r2)[:, 0:N], scalar=smc(SM_NLAM),
                                                               in1=T32.ap(r1)[:, 0:N], op0=ALU.mult, op1=ALU.add),
                       reads=[T32.buf(r1), T32.buf(r2), Bsm], writes=[T32.buf(r1)])
                T32.release(r2)
                sq = T16.alloc()
                S.emit("dve", lambda e: e.tensor_tensor(out=T16.ap(sq)[:, 0:N], in0=T32.ap(r1)[:, 0:N], in1=T32.ap(r1)[:, 0:N], op=ALU.mult),
                       reads=[T32.buf(r1)], writes=[T16.buf(sq)])
                fin_state[h] = (r1, sq)

            def fin_b(h, bk):
                r1, sq = fin_state.pop(h)
                S.emit("pe", lambda e: e.matmul(psb(bk)[:, 0:N], lhsT=ones_b, rhs=T16.ap(sq)[:, 0:N], start=True, stop=True),
                       reads=[T16.buf(sq), B["cmb"]], writes=[Bps[bk]])
                T16.release(sq)
                rh = T32.alloc()
                S.emit("act", lambda e: e.activation(out=T32.ap(rh)[:, 0:N], in_=psb(bk)[:, 0:N], func=AF.Ln, scale=1.0 / 128, bias=epsc),
                       reads=[Bps[bk], Bsm], writes=[T32.buf(rh)])
                S.emit("act", lambda e: e.activation(out=T32.ap(rh)[:, 0:N], in_=T32.ap(rh)[:, 0:N], func=AF.Exp, scale=-0.5),
                       reads=[T32.buf(rh)], writes=[T32.buf(rh)])
                S.emit("dve", lambda e: e.scalar_tensor_tensor(out=T32.ap(r1)[:, 0:N], in0=T32.ap(r1)[:, 0:N], scalar=smc(SM_HGS),
                                                               in1=T32.ap(rh)[:, 0:N], op0=ALU.mult, op1=ALU.mult),
                       reads=[T32.buf(r1), T32.buf(rh), Bsm], writes=[T32.buf(r1)])
                S.emit("dve", lambda e: e.tensor_tensor(out=on[:, h, 0:N], in0=T32.ap(r1)[:, 0:N], in1=sg_ap(h), op=ALU.mult),
                       reads=[T32.buf(r1), Bsg[h]], writes=[Bon[h]])
                T32.release(r1)
                T32.release(rh)

            ns = len(steps)
            emit_qk(0)
            if ns > 1:
                emit_qk(1)
            pend_a2 = None
            pend_b = None
            for s in range(ns):
                h, i = steps[s]
                emit_exp_pv(s)
                if pend_a2 is not None and s >= pend_a2[1]:
                    fin_a2(pend_a2[0])
                    pend_b = (pend_a2[0], s + 2)
                    pend_a2 = None
                elif pend_b is not None and s >= pend_b[1]:
                    fin_b(pend_b[0], 2 * (s % 2))
                    pend_b = None
                if i == nblk - 1:
                    fin_a(h)
                    pend_a2 = (h, s + 1)
                    if h + 2 < NH:
                        load_head(h + 2)
                        finish_head_load(h + 2)
                if s + 2 < ns:
                    emit_qk(s + 2)
            if pend_a2 is not None:
                fin_a2(pend_a2[0])
                pend_b = (pend_a2[0], 0)
            if pend_b is not None:
                fin_b(pend_b[0], 0)
            if next_spec is not None and HOIST > 0:
                aphn = make_aphase(next_spec)
                for k in range(HOIST):
                    aphn["run_k"](k)
                aphn["k_done"] = HOIST
                aph_pending[next_spec] = aphn

            for m_ in range(8):
                slot = next_slab("wbo", m_)
                prefetch()
                bk = proj_fm(slot, 8, lambda k: on[:, k, 0:N], Bon, N)
                ys = T32.alloc()
                S.emit("dve", lambda e, m_=m_, bk=bk, ys=ys: e.tensor_tensor(out=T32.ap(ys)[:, 0:N], in0=x1[:, m_, 0:N], in1=psb(bk)[:, 0:N], op=ALU.add),
                       reads=[B["x1"], Bps[bk]], writes=[T32.buf(ys)])
                store_sems.add(f"st32_{ys}")
                S.emit("sp", lambda e, m_=m_, ys=ys: e.dma_start(out=ysrc[m_ * 128:(m_ + 1) * 128, tok0:tok0 + N], in_=T32.ap(ys)[:, 0:N]),
                       reads=[T32.buf(ys)], dsem=f"st32_{ys}")
                T32.release(ys)

        for i_, spec in enumerate(tile_specs):
            run_tile(spec, tile_specs[i_ + 1] if i_ + 1 < len(tile_specs) else None, i_ == 0)
        flush_stores()
        S.final_waits("sp", sorted(store_sems) + ["pe", "act", "dve", "pool"])

        block = es.enter_context(nc.Block())
        S.replay(block)
    return nc


def _tile_w(W, nk, nm):
    return np.ascontiguousarray(W.reshape(nk, 128, nm, 128).transpose(2, 1, 0, 3).reshape(nm, 128, nk * 128))


def _chunkcol(v):
    return np.ascontiguousarray(v.reshape(-1, 128).T)


def _prep_shared(inp):
    f = np.float32
    a_w_in = np.asarray(inp["a_w_in"][0], f)
    shared = {}
    shared["win_u"] = _tile_w(a_w_in[:, :DR], 8, NCH)
    shared["win_g"] = _tile_w(a_w_in[:, DR:], 8, NCH)
    gts = np.zeros((NCH, 128, 3, 2, 128), f)
    for g, key in enumerate(("a_gate_r_w", "a_gate_i_w")):
        Wg = np.asarray(inp[key][0], f)
        Dm = np.zeros((DR, DR), f)
        for n in range(16):
            Dm[88 * n:88 * n + 88, 88 * n:88 * n + 88] = Wg[n]
        for c in range(NCH):
            for dk in range(3):
                kc = c + dk - 1
                if 0 <= kc < NCH:
                    gts[c, :, dk, g, :] = Dm[kc * 128:(kc + 1) * 128, c * 128:(c + 1) * 128]
    shared["gts"] = gts.reshape(NCH, 128, 768)
    shared["wout"] = _tile_w(np.asarray(inp["a_w_out"][0], f), NCH, 8)
    kv_w = np.asarray(inp["kv_w"], f)
    shared["wk"] = _tile_w(kv_w[:, :D], 8, 8)
    Wv = kv_w[:, D:]
    wv = Wv.reshape(4, 2, 128, 2, 512).transpose(3, 0, 2, 1, 4)
    shared["wv"] = np.ascontiguousarray(wv.reshape(8, 128, 1024))
    b_w_in = np.asarray(inp["b_w_in"][0], f)
    shared["wq"] = _tile_w(b_w_in[:, :D], 8, 8)
    shared["wbg"] = _tile_w(b_w_in[:, D:], 8, 8)
    shared["wbo"] = _tile_w(np.asarray(inp["b_w_out"][0], f), 8, 8)
    pp = np.zeros((128, PP_N), f)
    pp[:, PP_AN:PP_AN + 8] = _chunkcol(np.asarray(inp["a_norm"][0], f))
    pp[:, PP_KVN:PP_KVN + 8] = _chunkcol(np.asarray(inp["kv_norm"], f))
    pp[:, PP_BN:PP_BN + 8] = _chunkcol(np.asarray(inp["b_norm"][0], f))
    cw = np.asarray(inp["a_conv_w"][0], f)
    pp[:, PP_CW:PP_CW + 44] = cw.reshape(4, NCH, 128).transpose(2, 1, 0).reshape(128, 44)
    pp[:, PP_CB:PP_CB + 11] = _chunkcol(np.asarray(inp["a_conv_b"][0], f))
    pp[:, PP_BR:PP_BR + 11] = _chunkcol(np.asarray(inp["a_gate_r_b"][0], f))
    pp[:, PP_BI:PP_BI + 11] = _chunkcol(np.asarray(inp["a_gate_i_b"][0], f))
    pp[:, PP_LAM:PP_LAM + 11] = _chunkcol(np.asarray(inp["a_lambda"][0], f))
    p = np.arange(128)
    kn = np.asarray(inp["k_norm"], f)
    qn = np.asarray(inp["b_q_norm"][0], f)
    pp[:, PP_GK] = kn[p % 64]
    pp[:, PP_GKS] = kn[(p ^ 32) % 64]
    pp[:, PP_GQ] = qn[p % 64]
    pp[:, PP_GQS] = qn[(p ^ 32) % 64]
    pp[:, PP_HG] = np.asarray(inp["b_head_norm"][0], f)
    pp[:, PP_LQ1:PP_LQ1 + 64] = np.asarray(inp["b_lambda_q1"][0], f)[None, :]
    pp[:, PP_LK1:PP_LK1 + 64] = np.asarray(inp["b_lambda_k1"][0], f)[None, :]
    pp[:, PP_LQ2:PP_LQ2 + 64] = np.asarray(inp["b_lambda_q2"][0], f)[None, :]
    pp[:, PP_LK2:PP_LK2 + 64] = np.asarray(inp["b_lambda_k2"][0], f)[None, :]
    shared["pp"] = pp
    cm = np.zeros((128, 384), f)
    cm[:, 0:128] = 1.0
    cm[:, 128:256] = (p[:, None] // 64 == p[None, :] // 64).astype(f)
    cm[:, 256:384] = (p[:, None] == (p[None, :] ^ 32)).astype(f)
    shared["cmat"] = cm
    half = 32
    inv = (np.float32(10000.0) ** (-np.arange(half, dtype=f) / np.float32(half))).astype(f)
    pos = np.concatenate([np.arange(SEQ), PAST + np.arange(NS)]).astype(f)
    ang = (pos[:, None] * inv[None, :]).astype(f)
    cos = np.cos(ang).astype(f).T
    sin = np.sin(ang).astype(f).T
    fi = p % 32
    sign = np.where((p % 64) < 32, -1.0, 1.0).astype(f)
    rope = np.empty((128, 2, SEQ + NS), f)
    rope[:, 0, :] = cos[fi]
    rope[:, 1, :] = sin[fi] * sign[:, None]
    shared["rope"] = rope
    return shared


_PROG = None


def kernel(**inp):
    global _PROG
    f = np.float32
    shared = _prep_shared(inp)
    x_prompt = np.asarray(inp["x_prompt"], f)
    x_sample = np.asarray(inp["x_sample"], f)
    state_conv = np.asarray(inp["state_conv"], f)
    state_h = np.asarray(inp["state_h"], f)
    cache_k = np.asarray(inp["cache_k"], f)
    cache_v = np.asarray(inp["cache_v"], f)
    in_maps = []
    for b in range(8):
        m = dict(shared)
        m["xT_p"] = np.ascontiguousarray(x_prompt[b].T)
        m["xT_s"] = np.ascontiguousarray(x_sample[b].T)
        m["conv_s_in"] = np.ascontiguousarray(state_conv[0, b].reshape(3, NCH, 128).transpose(2, 1, 0).reshape(128, NCH * 3))
        m["h_s_in"] = _chunkcol(state_h[0, b])
        m["ckT"] = np.ascontiguousarray(cache_k[b].transpose(1, 2, 0))
        m["cv"] = np.ascontiguousarray(cache_v[b].reshape(PAST, D))
        in_maps.append(m)
    if _PROG is None:
        _PROG = build_program()
    res = run_bass_kernel_spmd(_PROG, in_maps, core_ids=list(range(8)))
    R = res.results

    def unconv(a):
        return a.reshape(128, NCH, 3).transpose(2, 1, 0).reshape(3, DR)

    def unh(a):
        return a.T.reshape(DR)

    y_p = np.stack([R[b]["yT_p"].T for b in range(8)]).astype(f)
    y_s = np.stack([R[b]["yT_s"].T for b in range(8)]).astype(f)
    conv_p = np.stack([unconv(R[b]["conv_p_o"]) for b in range(8)])[None].astype(f)
    h_p = np.stack([unh(R[b]["h_p_o"]) for b in range(8)])[None].astype(f)
    k_p = np.stack([R[b]["kT_p"].transpose(2, 0, 1) for b in range(8)]).astype(f)
    v_p = np.stack([R[b]["v_p"].reshape(SEQ, NH, 128) for b in range(8)]).astype(f)
    conv_s = np.stack([unconv(R[b]["conv_s_o"]) for b in range(8)])[None].astype(f)
    h_s = np.stack([unh(R[b]["h_s_o"]) for b in range(8)])[None].astype(f)
    k_s = np.stack([R[b]["kT_s"].transpose(2, 0, 1) for b in range(8)]).astype(f)
    v_s = np.stack([R[b]["v_s"].reshape(NS, NH, 128) for b in range(8)]).astype(f)
    return (np.ascontiguousarray(y_p), np.ascontiguousarray(y_s), np.ascontiguousarray(conv_p), np.ascontiguousarray(h_p),
            np.ascontiguousarray(k_p), np.ascontiguousarray(v_p), np.ascontiguousarray(conv_s), np.ascontiguousarray(h_s),
            np.ascontiguousarray(k_s), np.ascontiguousarray(v_s))
```
